# Optimizing a Trainium2 kernel written in Bass

```python
import jax, jax.numpy as jnp
from jax import lax
import numpy as np

D_MODEL = 1024
BATCH = 32
SEQ = 2048
DEPTH = 1
DEC_BATCH = 2
DEC_SEQ = 16384
PAST_LEN = 128

GRID_W = 64
HEAD_DIM = 64
RWKV_WIDTH = D_MODEL // 2
RWKV_HEADS = RWKV_WIDTH // HEAD_DIM
ATT_WIDTH = D_MODEL - RWKV_WIDTH
ATT_Q_HEADS = ATT_WIDTH // HEAD_DIM
ATT_KV_HEADS = 2
ATT_GROUP = ATT_Q_HEADS // ATT_KV_HEADS
KV_WIDTH = ATT_KV_HEADS * HEAD_DIM
W_LORA = 64
A_LORA = 64
G_LORA = 128
D_FF = -(-8 * D_MODEL // (3 * 256)) * 256
Q_BLOCK = 128
ROPE_THETA = 10000.0
ROPE_PAIRS = HEAD_DIM // 4
NORM_EPS = 1e-6
LNX_EPS = 64e-5

RWKV_SPLIT = [RWKV_WIDTH, RWKV_WIDTH, RWKV_WIDTH, W_LORA, W_LORA, A_LORA, A_LORA, G_LORA]
RWKV_COLS = int(sum(RWKV_SPLIT))
ATT_SPLIT = [ATT_WIDTH, KV_WIDTH, KV_WIDTH]
IN_COLS = RWKV_COLS + int(sum(ATT_SPLIT))
RWKV_SPLIT_IDX = [int(i) for i in np.cumsum(RWKV_SPLIT)[:-1]]
ATT_SPLIT_IDX = [int(i) for i in np.cumsum(ATT_SPLIT)[:-1]]

kernel_name = 'hybrid_rwkv7_axial_gqa_encoder'


def rms_norm(x, g):
    xf = x.astype(jnp.float32)
    y = xf * lax.rsqrt(jnp.mean(xf * xf, axis=-1, keepdims=True) + NORM_EPS)
    return (y * g.astype(jnp.float32)).astype(x.dtype)


def centred_shift(z, mu_prev, mu_next):
    z_prev = jnp.pad(z[:, :-1], ((0, 0), (1, 0), (0, 0)))
    z_next = jnp.pad(z[:, 1:], ((0, 0), (0, 1), (0, 0)))
    return z + mu_prev * (z_prev - z) + mu_next * (z_next - z)


def wkv7_scan(r, decay, k, v, a_vec, b_vec, reverse):
    B, T, H, N = r.shape
    xs = tuple(jnp.swapaxes(t, 0, 1) for t in (r, decay, k, v, a_vec, b_vec))

    def step(S, inp):
        r_t, w_t, k_t, v_t, a_t, b_t = inp
        sa = jnp.einsum('bhvk,bhk->bhv', S, a_t)
        S = S * w_t[:, :, None, :] + sa[..., None] * b_t[:, :, None, :] + v_t[..., None] * k_t[:, :, None, :]
        return S, jnp.einsum('bhvk,bhk->bhv', S, r_t)

    S0 = jnp.zeros((B, H, N, N), jnp.float32)
    _, ys = lax.scan(step, S0, xs, reverse=reverse)
    return jnp.swapaxes(ys, 0, 1)


def rwkv7_mix(z, mu_prev, mu_next, k_k, k_a, r_k, w0_f, w_lora_f, w0_b, w_lora_b,
              a0_f, a_lora_f, a0_b, a_lora_b, g_lora, lnx_w, lnx_b):
    B, T, _ = z.shape
    f32 = jnp.float32
    zf = centred_shift(z.astype(f32), mu_prev.astype(f32), mu_next.astype(f32))
    r, k, v, wd_f, wd_b, ad_f, ad_b, gd = jnp.split(zf, RWKV_SPLIT_IDX, axis=-1)

    def heads(t):
        return t.reshape(B, T, RWKV_HEADS, HEAD_DIM)

    kk = heads(k * k_k.astype(f32))
    kk = kk * lax.rsqrt(jnp.maximum(jnp.sum(kk * kk, axis=-1, keepdims=True), 1e-12))
    r_h, v_h = heads(r), heads(v)

    def direction(wd, w0, w_lora, ad, a0, a_lora, reverse):
        w_log = -jax.nn.softplus(-(w0.astype(f32) + jnp.tanh(wd) @ w_lora.astype(f32))) - 0.5
        decay = heads(jnp.exp(-jnp.exp(w_log)))
        a = jax.nn.sigmoid(a0.astype(f32) + ad @ a_lora.astype(f32))
        kd = heads(k * (1.0 + (a - 1.0) * k_a.astype(f32)))
        y = wkv7_scan(r_h, decay, kd, v_h, -kk, kk * heads(a), reverse)
        return y, kd

    y_f, kd_f = direction(wd_f, w0_f, w_lora_f, ad_f, a0_f, a_lora_f, False)
    y_b, kd_b = direction(wd_b, w0_b, w_lora_b, ad_b, a0_b, a_lora_b, True)
    y = y_f + y_b
    mu = jnp.mean(y, axis=-1, keepdims=True)
    var = jnp.mean(jnp.square(y - mu), axis=-1, keepdims=True)
    yn = ((y - mu) * lax.rsqrt(var + LNX_EPS)).reshape(B, T, RWKV_WIDTH)
    yn = yn * lnx_w.astype(f32) + lnx_b.astype(f32)
    kb = 0.5 * (kd_f + kd_b)
    bonus = (jnp.sum(r_h * kb * r_k.astype(f32), axis=-1, keepdims=True) * v_h).reshape(B, T, RWKV_WIDTH)
    g = jax.nn.sigmoid(gd) @ g_lora.astype(f32)
    return ((yn + bonus) * g).astype(z.dtype)


def axial_angles(T):
    n_rows = T // GRID_W
    row = jnp.broadcast_to(jnp.arange(n_rows, dtype=jnp.float32)[:, None], (n_rows, GRID_W)).reshape(-1)
    col = jnp.broadcast_to(jnp.arange(GRID_W, dtype=jnp.float32)[None, :], (n_rows, GRID_W)).reshape(-1)
    inv = ROPE_THETA ** (-jnp.arange(ROPE_PAIRS, dtype=jnp.float32) / ROPE_PAIRS)
    return row[:, None] * inv, col[:, None] * inv


def rope_half(x, ang):
    c = jnp.cos(ang)[None, :, None, :].astype(x.dtype)
    s = jnp.sin(ang)[None, :, None, :].astype(x.dtype)
    x1, x2 = x[..., :ROPE_PAIRS], x[..., ROPE_PAIRS:]
    return jnp.concatenate([x1 * c - x2 * s, x1 * s + x2 * c], axis=-1)


def axial_rope(x, ang_row, ang_col):
    h = HEAD_DIM // 2
    return jnp.concatenate([rope_half(x[..., :h], ang_row), rope_half(x[..., h:], ang_col)], axis=-1)


def axial_gqa(z, q_gain, k_gain):
    B, T, _ = z.shape
    q, k, v = jnp.split(z, ATT_SPLIT_IDX, axis=-1)
    q = q.reshape(B, T, ATT_Q_HEADS, HEAD_DIM)
    k = k.reshape(B, T, ATT_KV_HEADS, HEAD_DIM)
    v = v.reshape(B, T, ATT_KV_HEADS, HEAD_DIM)
    ang_row, ang_col = axial_angles(T)
    q = axial_rope(rms_norm(q, q_gain), ang_row, ang_col)
    k = axial_rope(rms_norm(k, k_gain), ang_row, ang_col)
    nb = T // Q_BLOCK
    qb = q.reshape(B, nb, Q_BLOCK, ATT_KV_HEADS, ATT_GROUP, HEAD_DIM).transpose(1, 0, 2, 3, 4, 5)
    scale = HEAD_DIM ** -0.5

    def block(q_blk):
        s = jnp.einsum('bqhgd,bkhd->bhgqk', q_blk, k).astype(jnp.float32) * scale
        p = jax.nn.softmax(s, axis=-1).astype(v.dtype)
        return jnp.einsum('bhgqk,bkhd->bqhgd', p, v)

    o = lax.map(block, qb)
    return o.transpose(1, 0, 2, 3, 4, 5).reshape(B, T, ATT_WIDTH)


def encoder_layer(x, norm1_g, w_in, mu_prev, mu_next, k_k, k_a, r_k, w0_f, w_lora_f, w0_b, w_lora_b,
                  a0_f, a_lora_f, a0_b, a_lora_b, g_lora, lnx_w, lnx_b, q_gain, k_gain, w_out,
                  norm2_g, ffn_gate, ffn_up, ffn_down):
    h = rms_norm(x, norm1_g)
    z = h @ w_in
    y_rwkv = rwkv7_mix(z[..., :RWKV_COLS], mu_prev, mu_next, k_k, k_a, r_k, w0_f, w_lora_f, w0_b,
                       w_lora_b, a0_f, a_lora_f, a0_b, a_lora_b, g_lora, lnx_w, lnx_b)
    y_att = axial_gqa(z[..., RWKV_COLS:], q_gain, k_gain)
    x = x + jnp.concatenate([y_rwkv, y_att], axis=-1) @ w_out
    h = rms_norm(x, norm2_g)
    x = x + (jax.nn.silu(h @ ffn_gate) * (h @ ffn_up)) @ ffn_down
    return x


def trunk(x, layer_params, norm_f_g):
    for l in range(DEPTH):
        x = encoder_layer(x, *[p[l] for p in layer_params])
    return rms_norm(x, norm_f_g)


def setup_inputs(seed: int = 0) -> dict:
    key = jax.random.key(seed)
    ks = jax.random.split(key, 32)
    L, D, RW, H, N = DEPTH, D_MODEL, RWKV_WIDTH, RWKV_HEADS, HEAD_DIM
    nrm = jax.random.normal
    f32 = jnp.float32
    return {
        'x_prompt': nrm(ks[0], (BATCH, SEQ, D), f32),
        'x_sample': nrm(ks[1], (DEC_BATCH, DEC_SEQ, D), f32),
        'norm1_g': 1.0 + 0.02 * nrm(ks[2], (L, D), f32),
        'w_in': nrm(ks[3], (L, D, IN_COLS), f32) * D ** -0.5,
        'mu_prev': jax.random.uniform(ks[4], (L, RWKV_COLS), f32, 0.0, 0.4),
        'mu_next': jax.random.uniform(ks[5], (L, RWKV_COLS), f32, 0.0, 0.4),
        'k_k': 0.85 + 0.05 * nrm(ks[6], (L, RW), f32),
        'k_a': 1.0 + 0.05 * nrm(ks[7], (L, RW), f32),
        'r_k': 0.1 * nrm(ks[8], (L, H, N), f32),
        'w0_f': jax.random.uniform(ks[9], (L, RW), f32, -6.0, 1.0),
        'w_lora_f': 0.1 * nrm(ks[10], (L, W_LORA, RW), f32) * W_LORA ** -0.5,
        'w0_b': jax.random.uniform(ks[11], (L, RW), f32, -6.0, 1.0),
        'w_lora_b': 0.1 * nrm(ks[12], (L, W_LORA, RW), f32) * W_LORA ** -0.5,
        'a0_f': 0.1 * nrm(ks[13], (L, RW), f32),
        'a_lora_f': 0.1 * nrm(ks[14], (L, A_LORA, RW), f32) * A_LORA ** -0.5,
        'a0_b': 0.1 * nrm(ks[15], (L, RW), f32),
        'a_lora_b': 0.1 * nrm(ks[16], (L, A_LORA, RW), f32) * A_LORA ** -0.5,
        'g_lora': nrm(ks[17], (L, G_LORA, RW), f32) * G_LORA ** -0.5,
        'lnx_w': 1.0 + 0.02 * nrm(ks[18], (L, RW), f32),
        'lnx_b': 0.01 * nrm(ks[19], (L, RW), f32),
        'q_gain': 1.0 + 0.02 * nrm(ks[20], (L, HEAD_DIM), f32),
        'k_gain': 1.0 + 0.02 * nrm(ks[21], (L, HEAD_DIM), f32),
        'w_out': nrm(ks[22], (L, D, D), f32) * D ** -0.5,
        'norm2_g': 1.0 + 0.02 * nrm(ks[23], (L, D), f32),
        'ffn_gate': nrm(ks[24], (L, D, D_FF), f32) * D ** -0.5,
        'ffn_up': nrm(ks[25], (L, D, D_FF), f32) * D ** -0.5,
        'ffn_down': nrm(ks[26], (L, D_FF, D), f32) * D_FF ** -0.5,
        'norm_f_g': 1.0 + 0.02 * nrm(ks[27], (D,), f32),
    }


def reference(x_prompt, x_sample, norm1_g, w_in, mu_prev, mu_next, k_k, k_a, r_k, w0_f, w_lora_f,
              w0_b, w_lora_b, a0_f, a_lora_f, a0_b, a_lora_b, g_lora, lnx_w, lnx_b, q_gain, k_gain,
              w_out, norm2_g, ffn_gate, ffn_up, ffn_down, norm_f_g):
    layer_params = (norm1_g, w_in, mu_prev, mu_next, k_k, k_a, r_k, w0_f, w_lora_f, w0_b, w_lora_b,
                    a0_f, a_lora_f, a0_b, a_lora_b, g_lora, lnx_w, lnx_b, q_gain, k_gain, w_out,
                    norm2_g, ffn_gate, ffn_up, ffn_down)
    y_prompt = trunk(x_prompt, layer_params, norm_f_g)
    y_sample = trunk(x_sample, layer_params, norm_f_g)
    return (y_prompt, y_sample)
```

```python
import contextlib
import numpy as np
import concourse.bass as bass
import concourse.mybir as mybir
from concourse.bass_utils import run_bass_kernel_spmd

F32 = mybir.dt.float32
BF16 = mybir.dt.bfloat16
AF = mybir.ActivationFunctionType
ALU = mybir.AluOpType
AX = mybir.AxisListType
I32 = mybir.dt.int32

D = 1024
DFF = 2816
NFF = DFF // 128
NK = D // 128
NCORES = 8
TOK_PER_CORE = 4 * 2048 + 4096
GT = 256
NSUB = GT // 128
NORM_EPS = 1e-6
LNX_EPS = 64e-5
HEAD_DIM = 64
ROPE_THETA = 10000.0
ROPE_PAIRS = 16
QPERM = (0, 4, 1, 5, 2, 6, 3, 7)


class Sched:
    ENG = ('pe', 'act', 'dve', 'pool', 'sp')
    NDMA = 12

    def __init__(self, nc, same_engine_sync=('act', 'dve', 'pool')):
        self.nc = nc
        self.ops = []
        self.last_w = {}
        self.readers = {}
        self.same = set(same_engine_sync)

    def add(self, eng, fn, reads=(), writes=(), dma=False):
        i = len(self.ops)
        deps = set()
        for r in reads:
            j = self.last_w.get(r)
            if j is not None:
                deps.add(j)
        for w in writes:
            j = self.last_w.get(w)
            if j is not None:
                deps.add(j)
            for j in self.readers.get(w, ()):
                deps.add(j)
        for w in writes:
            self.last_w[w] = i
            self.readers[w] = []
        for r in reads:
            if r not in writes:
                self.readers.setdefault(r, []).append(i)
        self.ops.append(dict(eng=eng, fn=fn, deps=deps, dma=dma, needs_inc=False))
        return i

    def emit(self):
        nc = self.nc
        ops = self.ops
        for op in ops:
            for j in op['deps']:
                oj = ops[j]
                if oj['dma']:
                    oj['needs_inc'] = True
                elif oj['eng'] != op['eng'] or op['dma'] or (op['eng'] in self.same):
                    oj['needs_inc'] = True
        with contextlib.ExitStack() as es:
            pool = SemPool.current
            if pool is None:
                pool = SemPool(nc, es)
            sems, dsems, cnt, dma_cnt = pool.sems, pool.dsems, pool.cnt, pool.dcnt
            dma_rr = {e: 0 for e in self.ENG}
            for op in ops:
                if op['dma']:
                    k = dma_rr[op['eng']] % self.NDMA
                    dma_rr[op['eng']] += 1
                    key = (op['eng'], k)
                    prev = dma_cnt.get(key, 0)
                    op['dsem'] = key
                    op['dprev'] = prev
                    dma_cnt[key] = prev + 16
                    op['ticket'] = prev + 16
                elif op['needs_inc']:
                    cnt[op['eng']] += 1
                    op['ticket'] = cnt[op['eng']]
            block = es.enter_context(nc.Block())

            def run(ename):
                def body(eng):
                    waited = {}

                    def wait(sem_key, sem, val):
                        if waited.get(sem_key, 0) >= val:
                            return
                        waited[sem_key] = val
                        eng.wait_ge(sem, val)
                    last_dma = {}
                    for op in ops:
                        if op['eng'] != ename:
                            continue
                        if op['dma'] and op['dprev'] > 0:
                            wait(op['dsem'], dsems[op['dsem']], op['dprev'])
                        for j in sorted(op['deps']):
                            oj = ops[j]
                            if oj['dma']:
                                wait(oj['dsem'], dsems[oj['dsem']], oj['ticket'])
                            elif oj['eng'] != ename or op['dma'] or (ename in self.same):
                                wait(oj['eng'], sems[oj['eng']], oj['ticket'])
                        ins = op['fn'](eng)
                        if op['dma']:
                            ins.then_inc(dsems[op['dsem']], 16)
                            last_dma[op['dsem']] = op['ticket']
                        elif op['needs_inc']:
                            ins.then_inc(sems[ename], 1)
                    for key, t in last_dma.items():
                        wait(key, dsems[key], t)
                return body
            block.tensor(run('pe'))
            block.scalar(run('act'))
            block.vector(run('dve'))
            block.gpsimd(run('pool'))
            block.sync(run('sp'))


class SemPool:
    current = None

    def __init__(self, nc, es):
        self.sems = {e: es.enter_context(nc.semaphore('s_' + e)) for e in Sched.ENG}
        self.dsems = {}
        for e in ('sp', 'pool'):
            for k in range(Sched.NDMA):
                self.dsems[(e, k)] = es.enter_context(nc.semaphore('d_%s_%d' % (e, k)))
        self.cnt = {e: 0 for e in Sched.ENG}
        self.dcnt = {}


STATS = []


class Phase:
    _n = [0]

    def __init__(self, nc, tag):
        self.nc = nc
        Phase._n[0] += 1
        self.tag = "%s%d_" % (tag, Phase._n[0])
        self.es = contextlib.ExitStack()
        self.S = Sched(nc)

    def sb(self, name, shape, dt=F32):
        return self.es.enter_context(self.nc.sbuf_tensor(self.tag + name, shape, dt))

    def ps(self, name, shape, dt=F32):
        return self.es.enter_context(self.nc.psum_tensor(self.tag + name, shape, dt))

    def add(self, *a, **k):
        return self.S.add(*a, **k)

    def close(self):
        self.S.emit()
        self.es.close()
        import collections
        cnt = collections.Counter(o['eng'] + ('_dma' if o['dma'] else '') for o in self.S.ops)
        STATS.append((self.tag, len(self.S.ops), dict(cnt)))


def emit_identity(P, idb, idf):
    P.add('pool', lambda e: e.memset(idf[:], 1.0), writes=['idf'])
    P.add('pool', lambda e: e.affine_select(out=idf[:], in_=idf[:], pattern=[[-1, 128]], compare_op=ALU.is_equal,
                                            fill=0.0, base=0, channel_multiplier=1), writes=['idf'])
    P.add('dve', lambda e: e.tensor_copy(out=idb[:], in_=idf[:]), reads=['idf'], writes=['idb'])


def load_weight_bf16(P, src, dst, nrows_chunks, ncols, stage, dkey, scale_tile=None, col_piece=1024, eng_alt=True):
    n = [0]
    for k in range(nrows_chunks):
        c0 = 0
        while c0 < ncols:
            cn = min(col_piece, ncols - c0)
            i = n[0] % len(stage)
            n[0] += 1
            st = stage[i]
            skey = 'stage%d' % i
            P.add('sp', lambda e, st=st, k=k, c0=c0, cn=cn: e.dma_start(out=st[:, 0:cn],
                                                                       in_=src[k * 128:(k + 1) * 128, c0:c0 + cn]),
                  writes=[skey], dma=True)
            if scale_tile is not None:
                P.add('dve', lambda e, st=st, k=k, c0=c0, cn=cn: e.tensor_scalar(
                    out=dst[:, k, c0:c0 + cn], in0=st[:, 0:cn], scalar1=scale_tile[:, k:k + 1], scalar2=None,
                    op0=ALU.mult), reads=[skey, 'wscale'], writes=[dkey])
            else:
                P.add('act', lambda e, st=st, k=k, c0=c0, cn=cn: e.activation(
                    out=dst[:, k, c0:c0 + cn], in_=st[:, 0:cn], func=AF.Identity), reads=[skey], writes=[dkey])
            c0 += cn


def emit_rstd(P, ss_in, tmp, out, scale, eps, keys_r, keys_w):
    P.add('act', lambda e: e.activation(out=tmp, in_=ss_in, func=AF.Ln, scale=scale, bias=eps),
          reads=keys_r, writes=keys_w)
    P.add('act', lambda e: e.activation(out=out, in_=tmp, func=AF.Exp, scale=-0.5), reads=[], writes=keys_w)


def emit_norm_hT(P, X, xkey, hb, hT_dst, hT_key, pT, pT_key, ss, idb, extra_scale=None):
    P.add('act', lambda e: e.activation(out=hb[:], in_=X, func=AF.Square, accum_out=ss[:, 0:1]),
          reads=[xkey], writes=['hb', 'ss'])
    emit_rstd(P, ss[:, 0:1], ss[:, 1:2], ss[:, 2:3], 1.0 / D, NORM_EPS, [], ['ss'])
    P.add('dve', lambda e: e.tensor_scalar(out=hb[:], in0=X, scalar1=ss[:, 2:3], scalar2=None, op0=ALU.mult),
          reads=[xkey, 'ss'], writes=['hb'])
    for k in range(NK):
        P.add('pe', lambda e, k=k: e.transpose(pT[:, k * 128:(k + 1) * 128], hb[:, k * 128:(k + 1) * 128], idb[:]),
              reads=['hb', 'idb'], writes=[pT_key])
    P.add('dve', lambda e: e.tensor_copy(out=hT_dst, in_=pT[:].rearrange("p (k t) -> p k t", k=NK)),
          writes=[pT_key, hT_key])


def emit_rope_tables(P, Crow, Srow, ntiles, row0_tile, wk, Ccol=None, Scol=None):
    pi_i, pf, inv, rowi, rowf, ang, t0, t1, ti = (wk['pi_i'], wk['pf'], wk['inv'], wk['rowi'], wk['rowf'],
                                                  wk['ang'], wk['t0'], wk['t1'], wk['ti'])
    P.add('pool', lambda e: e.iota(pi_i[:], pattern=[[0, 1]], base=0, channel_multiplier=1), writes=['pi_i'])
    P.add('dve', lambda e: e.tensor_copy(out=pf[:, 0:1], in_=pi_i[:]), reads=['pi_i'], writes=['pf'])
    P.add('dve', lambda e: e.tensor_scalar(out=pf[:, 1:2], in0=pf[:, 0:1], scalar1=64.0, scalar2=None, op0=ALU.is_ge),
          writes=['pf'])
    P.add('dve', lambda e: e.scalar_tensor_tensor(out=pf[:, 2:3], in0=pf[:, 1:2], scalar=-64.0, in1=pf[:, 0:1],
                                                  op0=ALU.mult, op1=ALU.add), writes=['pf'])
    for i in range(ROPE_PAIRS):
        v = float(ROPE_THETA ** (-i / ROPE_PAIRS)) / (2.0 * np.pi)
        P.add('pool', lambda e, i=i, v=v: e.memset(inv[:, i:i + 1], v), writes=['inv'])

    def sin_turns(dst, n, shift, key):
        P.add('dve', lambda e: e.tensor_scalar(out=t0[:, 0:n], in0=ang[:, 0:n], scalar1=shift, scalar2=None, op0=ALU.add),
              reads=['ang'], writes=['t0'])
        P.add('dve', lambda e: e.tensor_copy(out=ti[:, 0:n], in_=t0[:, 0:n]), reads=['t0'], writes=['ti'])
        P.add('dve', lambda e: e.tensor_copy(out=t1[:, 0:n], in_=ti[:, 0:n]), reads=['ti'], writes=['t1'])
        P.add('dve', lambda e: e.tensor_tensor(out=t0[:, 0:n], in0=t0[:, 0:n], in1=t1[:, 0:n], op=ALU.subtract),
              reads=['t1'], writes=['t0'])
        P.add('dve', lambda e: e.tensor_scalar(out=t1[:, 0:n], in0=t0[:, 0:n], scalar1=0.5, scalar2=None, op0=ALU.is_ge),
              reads=['t0'], writes=['t1'])
        P.add('dve', lambda e: e.tensor_tensor(out=t0[:, 0:n], in0=t0[:, 0:n], in1=t1[:, 0:n], op=ALU.subtract),
              reads=['t1'], writes=['t0'])
        P.add('dve', lambda e: e.tensor_scalar(out=t1[:, 0:n], in0=t0[:, 0:n], scalar1=-0.5, scalar2=None, op0=ALU.is_lt),
              reads=['t0'], writes=['t1'])
        P.add('dve', lambda e: e.tensor_tensor(out=t0[:, 0:n], in0=t0[:, 0:n], in1=t1[:, 0:n], op=ALU.add),
              reads=['t1'], writes=['t0'])
        P.add('act', lambda e: e.activation(out=dst, in_=t0[:, 0:n], func=AF.Sin, scale=6.28318),
              reads=['t0'], writes=[key])

    if Ccol is not None:
        P.add('dve', lambda e: e.tensor_scalar(out=ang[:, 0:16], in0=inv[:, 0:16], scalar1=pf[:, 2:3], scalar2=None,
                                               op0=ALU.mult), reads=['pf', 'inv'], writes=['ang'])
        sin_turns(Scol[:, 0:16], 16, 0.0, 'Stab')
        sin_turns(Ccol[:, 0:16], 16, 0.25, 'Ctab')
    CH = 32
    for n0 in range(0, ntiles, CH):
        NT = min(CH, ntiles - n0)
        P.add('pool', lambda e, n0=n0, NT=NT: e.iota(rowi[:, 0:NT], pattern=[[2, NT]], base=2 * n0, channel_multiplier=0),
              writes=['rowi'])
        P.add('dve', lambda e, NT=NT: e.tensor_copy(out=rowf[:, 0:NT], in_=rowi[:, 0:NT]), reads=['rowi'], writes=['rowf'])
        P.add('dve', lambda e, NT=NT: e.tensor_scalar(out=rowf[:, 0:NT], in0=rowf[:, 0:NT], scalar1=pf[:, 1:2], scalar2=None,
                                                      op0=ALU.add), reads=['pf'], writes=['rowf'])
        if row0_tile is not None:
            P.add('dve', lambda e, NT=NT: e.tensor_scalar(out=rowf[:, 0:NT], in0=rowf[:, 0:NT], scalar1=row0_tile,
                                                          scalar2=None, op0=ALU.add), reads=['row0'], writes=['rowf'])
        P.add('dve', lambda e, NT=NT: e.tensor_tensor(
            out=ang[:, 0:NT * 16].rearrange("p (n c) -> p n c", c=16),
            in0=rowf[:, 0:NT].unsqueeze(2).broadcast_to([128, NT, 16]),
            in1=inv[:, 0:16].unsqueeze(1).broadcast_to([128, NT, 16]), op=ALU.mult),
            reads=['rowf', 'inv'], writes=['ang'])
        sin_turns(Srow[:, n0:n0 + NT, :].rearrange("p n c -> p (n c)"), NT * 16, 0.0, 'Stab')
        sin_turns(Crow[:, n0:n0 + NT, :].rearrange("p n c -> p (n c)"), NT * 16, 0.25, 'Ctab')


def emit_qk_norm_rope(P, src_ps, src_key, nheads, gain_b, Cr, Sr, Cc, Sc, scale, out_bf, out_key, wk, tkeys):
    H = nheads
    W = H * 64
    sq, qn, ta, tb, st = wk['sq'], wk['qn'], wk['ta'], wk['tb'], wk['st']
    P.add('act', lambda e: e.activation(out=qn[:, 0:W], in_=src_ps, func=AF.Identity), writes=[src_key, 'qn'])
    P.add('dve', lambda e: e.tensor_tensor(out=sq[:, 0:W], in0=qn[:, 0:W], in1=qn[:, 0:W], op=ALU.mult),
          reads=['qn'], writes=['sq'])
    P.add('dve', lambda e: e.tensor_reduce(out=st[:, 0:H], in_=sq[:, 0:W].rearrange("p (h c) -> p h c", c=64),
                                           axis=AX.X, op=ALU.add), reads=['sq'], writes=['st'])
    emit_rstd(P, st[:, 0:H], st[:, 8:8 + H], st[:, 16:16 + H], 1.0 / 64, NORM_EPS, ['st'], ['st'])
    q3 = qn[:, 0:W].rearrange("p (h c) -> p h c", c=64)
    P.add('dve', lambda e: e.tensor_tensor(out=q3, in0=q3, in1=st[:, 16:16 + H].unsqueeze(2).broadcast_to([128, H, 64]),
                                           op=ALU.mult), reads=['st'], writes=['qn'])
    P.add('dve', lambda e: e.scalar_tensor_tensor(out=q3, in0=q3, scalar=float(scale),
                                                  in1=gain_b.unsqueeze(1).broadcast_to([128, H, 64]),
                                                  op0=ALU.mult, op1=ALU.mult), reads=['gains'], writes=['qn'])
    q5 = qn[:, 0:W].rearrange("p (h a b i) -> p h a b i", a=2, b=2, i=16)
    o5 = out_bf.rearrange("p (h a b i) -> p h a b i", a=2, b=2, i=16)
    A3 = ta[:, 0:H * 16].rearrange("p (h i) -> p h i", i=16)
    B3 = tb[:, 0:H * 16].rearrange("p (h i) -> p h i", i=16)
    for a, (Ct, St) in enumerate(((Cr, Sr), (Cc, Sc))):
        x1, x2 = q5[:, :, a, 0, :], q5[:, :, a, 1, :]
        C3 = Ct.unsqueeze(1).broadcast_to([128, H, 16])
        S3 = St.unsqueeze(1).broadcast_to([128, H, 16])
        P.add('dve', lambda e, x1=x1, C3=C3: e.tensor_tensor(out=A3, in0=x1, in1=C3, op=ALU.mult),
              reads=['qn'] + tkeys, writes=['ta'])
        P.add('pool', lambda e, x2=x2, S3=S3: e.tensor_tensor(out=B3, in0=x2, in1=S3, op=ALU.mult),
              reads=['qn'] + tkeys, writes=['tb'])
        P.add('dve', lambda e, a=a: e.tensor_tensor(out=o5[:, :, a, 0, :], in0=A3, in1=B3, op=ALU.subtract),
              reads=['ta', 'tb'], writes=[out_key])
        P.add('dve', lambda e, x1=x1, S3=S3: e.tensor_tensor(out=A3, in0=x1, in1=S3, op=ALU.mult),
              reads=['qn'] + tkeys, writes=['ta'])
        P.add('pool', lambda e, x2=x2, C3=C3: e.tensor_tensor(out=B3, in0=x2, in1=C3, op=ALU.mult),
              reads=['qn'] + tkeys, writes=['tb'])
        P.add('dve', lambda e, a=a: e.tensor_tensor(out=o5[:, :, a, 1, :], in0=A3, in1=B3, op=ALU.add),
              reads=['ta', 'tb'], writes=[out_key])


def phase_attention(nc, xk, xq, winA, g1c, qg, kg, qrow0, yatt, Tk, Town):
    P = Phase(nc, "att")
    NB = Tk // 128
    NQT = Town // 512
    WA = P.sb("WA", [128, NK, 768], BF16)
    stage = [P.sb("stage%d" % i, [128, 768]) for i in range(2)]
    g1t = P.sb("g1t", [128, NK])
    gq = P.sb("gq", [128, 64]); gk = P.sb("gk", [128, 64])
    r0t = P.sb("r0t", [128, 1])
    negm = P.sb("negm", [128, 4])
    idf = P.sb("idf", [128, 128]); idb = P.sb("idb", [128, 128], BF16)
    CtK = P.sb("CtK", [128, NB, 16]); StK = P.sb("StK", [128, NB, 16])
    NQ128 = Town // 128
    CtQ = P.sb("CtQ", [128, NQ128, 16]); StQ = P.sb("StQ", [128, NQ128, 16])
    Ccol = P.sb("Ccol", [128, 16]); Scol = P.sb("Scol", [128, 16])
    wk = dict(pi_i=P.sb("pi_i", [128, 1], I32), pf=P.sb("pf", [128, 4]), inv=P.sb("inv", [128, 16]),
              rowi=P.sb("rowi", [128, 32], I32), rowf=P.sb("rowf", [128, 32]), ang=P.sb("ang", [128, 512]),
              t0=P.sb("t0", [128, 512]), t1=P.sb("t1", [128, 512]), ti=P.sb("ti", [128, 512], I32),
              sq=P.sb("sq", [128, 512]), qn=P.sb("qn", [128, 512]), ta=P.sb("ta", [128, 256]), tb=P.sb("tb", [128, 256]),
              st=P.sb("st", [128, 24]))
    KT = P.sb("KT", [128, Tk], BF16)
    V3 = P.sb("V3", [128, NB, 192], BF16)
    xt = [P.sb("xt%d" % i, [128, D]) for i in range(2)]
    hb = P.sb("hb", [128, D], BF16)
    ss = P.sb("ss", [128, 4])
    hT = P.sb("hT", [128, NK, 128], BF16)
    ko = P.sb("ko", [128, 128], BF16)
    qo = P.sb("qo", [128, 512], BF16)
    QT = P.sb("QT", [128, 4, 512], BF16)
    PT = [P.sb("PT%d" % i, [128, 512], BF16) for i in range(4)]
    rl = P.sb("rl", [128, 1024])
    Yt = [P.sb("Yt%d" % i, [128, 512], BF16) for i in range(2)]
    pT = P.ps("pT", [128, 1024], BF16)
    pJ = P.ps("pJ", [128, 512])
    pS = [P.ps("pS%d" % i, [128, 512]) for i in range(4)]
    pO = [P.ps("pO%d" % i, [128, 512]) for i in range(2)]

    P.add('sp', lambda e: e.dma_start(out=g1t[:], in_=g1c[:, :]), writes=['wscale'], dma=True)
    P.add('sp', lambda e: e.dma_start(out=gq[:], in_=qg.partition_broadcast(128)), writes=['gains'], dma=True)
    P.add('sp', lambda e: e.dma_start(out=gk[:], in_=kg.partition_broadcast(128)), writes=['gains'], dma=True)
    P.add('sp', lambda e: e.dma_start(out=r0t[:], in_=qrow0.partition_broadcast(128)), writes=['row0'], dma=True)
    emit_identity(P, idb, idf)
    load_weight_bf16(P, winA, WA, NK, 768, stage, 'WA', scale_tile=g1t, col_piece=768)
    emit_rope_tables(P, CtK, StK, NB, None, wk, Ccol, Scol)
    emit_rope_tables(P, CtQ, StQ, NQ128, r0t[:, 0:1], wk)
    P.add('dve', lambda e: e.tensor_reduce(out=negm[:, 0:1], in_=gq[:], axis=AX.X, op=ALU.max, apply_absolute_value=True),
          reads=['gains'], writes=['negm'])
    P.add('dve', lambda e: e.tensor_reduce(out=negm[:, 1:2], in_=gk[:], axis=AX.X, op=ALU.max, apply_absolute_value=True),
          reads=['gains'], writes=['negm'])
    P.add('dve', lambda e: e.scalar_tensor_tensor(out=negm[:, 2:3], in0=negm[:, 0:1], scalar=-8.0 * 1.0001, in1=negm[:, 1:2],
                                                  op0=ALU.mult, op1=ALU.mult), writes=['negm'])
    P.add('pool', lambda e: e.memset(V3[:, :, 64:128], 1.0), writes=['V3'])

    for n in range(NB):
        X = xt[n % 2]
        xkey = 'xt%d' % (n % 2)
        P.add('sp', lambda e, X=X, n=n: e.dma_start(out=X[:], in_=xk[n * 128:(n + 1) * 128, :]), writes=[xkey], dma=True)
        emit_norm_hT(P, X[:], xkey, hb, hT[:], 'hT', pT, 'pT', ss, idb)
        for k in range(NK):
            P.add('pe', lambda e, k=k: e.matmul(pJ[:, 0:256], hT[:, k, :], WA[:, k, 512:768], start=(k == 0), stop=(k == NK - 1)),
                  reads=['hT', 'WA'], writes=['pJ'])
        P.add('act', lambda e, n=n: e.activation(out=V3[:, n, :].rearrange("p (a b) -> p a b", b=64)[:, 0:3:2, :],
                                                 in_=pJ[:, 128:256].rearrange("p (a b) -> p a b", b=64), func=AF.Identity),
              writes=['pJ', 'V3'])
        emit_qk_norm_rope(P, pJ[:, 0:128], 'pJ', 2, gk[:], CtK[:, n, :], StK[:, n, :], Ccol[:], Scol[:], 1.0, ko[:], 'ko', wk, ['Ctab', 'Stab'])
        P.add('pe', lambda e: e.transpose(pT[:, 0:128], ko[:], idb[:]), reads=['ko', 'idb'], writes=['pT'])
        P.add('act', lambda e, n=n: e.activation(out=KT[:, n * 128:(n + 1) * 128], in_=pT[:, 0:128], func=AF.Identity),
              writes=['pT', 'KT'])

    npt = [0]
    for qt in range(NQT):
        for j in range(4):
            n = qt * 4 + j
            X = xt[n % 2]
            xkey = 'xt%d' % (n % 2)
            P.add('sp', lambda e, X=X, n=n: e.dma_start(out=X[:], in_=xq[n * 128:(n + 1) * 128, :]), writes=[xkey], dma=True)
            emit_norm_hT(P, X[:], xkey, hb, hT[:], 'hT', pT, 'pT', ss, idb)
            for k in range(NK):
                P.add('pe', lambda e, k=k: e.matmul(pJ[:, :], hT[:, k, :], WA[:, k, 0:512], start=(k == 0), stop=(k == NK - 1)),
                      reads=['hT', 'WA'], writes=['pJ'])
            emit_qk_norm_rope(P, pJ[:, :], 'pJ', 8, gq[:], CtQ[:, n, :], StQ[:, n, :], Ccol[:], Scol[:], HEAD_DIM ** -0.5,
                              qo[:], 'qo', wk, ['Ctab', 'Stab'])
            for g in range(4):
                P.add('pe', lambda e, g=g: e.transpose(pT[:, g * 128:(g + 1) * 128], qo[:, g * 128:(g + 1) * 128], idb[:]),
                      reads=['qo', 'idb'], writes=['pT'])
            P.add('dve', lambda e, j=j: e.tensor_copy(out=QT[:, :, j * 128:(j + 1) * 128],
                                                      in_=pT[:, 0:512].rearrange("p (g t) -> p g t", g=4)),
                  writes=['pT', 'QT'])
        for g in range(4):
            for n in range(NB):
                ia, ib = npt[0] % 4, (npt[0] + 1) % 4
                npt[0] += 2
                P.add('pe', lambda e, g=g, n=n, ia=ia: e.matmul(pS[ia][:, :], KT[0:64, n * 128:(n + 1) * 128], QT[0:64, g, :],
                                                                start=True, stop=True),
                      reads=['KT', 'QT'], writes=['pS%d' % ia])
                P.add('pe', lambda e, g=g, n=n, ib=ib: e.matmul(pS[ib][:, :], KT[64:128, n * 128:(n + 1) * 128], QT[64:128, g, :],
                                                                start=True, stop=True),
                      reads=['KT', 'QT'], writes=['pS%d' % ib])
                P.add('act', lambda e, ia=ia: e.activation(out=PT[ia][:], in_=pS[ia][:], func=AF.Exp, bias=negm[:, 2:3], scale=1.0),
                      reads=['negm'], writes=['pS%d' % ia, 'PT%d' % ia])
                P.add('act', lambda e, ib=ib: e.activation(out=PT[ib][:], in_=pS[ib][:], func=AF.Exp, bias=negm[:, 2:3], scale=1.0),
                      reads=['negm'], writes=['pS%d' % ib, 'PT%d' % ib])
                P.add('pe', lambda e, n=n, ia=ia: e.matmul(pO[0][:, :], V3[:, n, 0:128], PT[ia][:], start=(n == 0), stop=(n == NB - 1)),
                      reads=['V3', 'PT%d' % ia], writes=['pO0'])
                P.add('pe', lambda e, n=n, ib=ib: e.matmul(pO[1][:, :], V3[:, n, 64:192], PT[ib][:], start=(n == 0), stop=(n == NB - 1)),
                      reads=['V3', 'PT%d' % ib], writes=['pO1'])
            Y = Yt[g % 2]
            ykey = 'Yt%d' % (g % 2)
            P.add('dve', lambda e: e.reciprocal(out=rl[64:128, 0:512], in_=pO[0][64:128, :]), writes=['pO0', 'rlA'])
            P.add('dve', lambda e: e.reciprocal(out=rl[0:64, 512:1024], in_=pO[1][0:64, :]), writes=['pO1', 'rlB'])
            P.add('dve', lambda e, Y=Y: e.tensor_tensor(out=Y[0:64, :], in0=pO[0][0:64, :], in1=rl[64:128, 0:512], op=ALU.mult),
                  reads=['rlA'], writes=['pO0', ykey])
            P.add('dve', lambda e, Y=Y: e.tensor_tensor(out=Y[64:128, :], in0=pO[1][64:128, :], in1=rl[0:64, 512:1024], op=ALU.mult),
                  reads=['rlB'], writes=['pO1', ykey])
            P.add('pool', lambda e, Y=Y, g=g, qt=qt: e.dma_start(out=yatt[g, :, qt * 512:(qt + 1) * 512], in_=Y[:]),
                  reads=[ykey], dma=True)
    P.close()


def emit_norm_hT_n(P, X, xkey, np_, hb, hT_dst, hT_key, pT, pT_key, ss, idb):
    P.add('act', lambda e: e.activation(out=hb[0:np_, :], in_=X, func=AF.Square, accum_out=ss[0:np_, 0:1]),
          reads=[xkey], writes=['hb', 'ss'])
    emit_rstd(P, ss[0:np_, 0:1], ss[0:np_, 1:2], ss[0:np_, 2:3], 1.0 / D, NORM_EPS, [], ['ss'])
    P.add('dve', lambda e: e.tensor_scalar(out=hb[0:np_, :], in0=X, scalar1=ss[0:np_, 2:3], scalar2=None, op0=ALU.mult),
          reads=[xkey, 'ss'], writes=['hb'])
    for k in range(NK):
        P.add('pe', lambda e, k=k: e.transpose(pT[:, k * np_:(k + 1) * np_], hb[0:np_, k * 128:(k + 1) * 128],
                                               idb[0:np_, 0:np_]),
              reads=['hb', 'idb'], writes=[pT_key])
    P.add('dve', lambda e: e.tensor_copy(out=hT_dst, in_=pT[:, 0:NK * np_].rearrange("p (k t) -> p k t", k=NK)),
          writes=[pT_key, hT_key])


DECAY_C = float(np.exp(-0.5))


def phase_rwkv(nc, fwd, xw, vmask, Tw, own0, own1, WRd, nch, mupc, munc, g1c, Wld, w0c, a0c, kkc, kac, rkc,
               y_out, s_out, Gld=None, g_out=None, v_out=None):
    P = Phase(nc, "rwf" if fwd else "rwb")
    NSC = Tw // 512
    CL, CG = 12, 13
    WR = P.sb("WR", [128, NK, nch * 128], BF16)
    Wl = P.sb("Wl", [128, 512], BF16)
    Gl = P.sb("Gl", [128, 512], BF16) if fwd else None
    g1t = P.sb("g1t", [128, NK])
    mp = P.sb("mp", [128, 16]); mn = P.sb("mn", [128, 16]); c0 = P.sb("c0", [128, 16])
    cv = P.sb("cv", [128, 20])
    idf = P.sb("idf", [128, 128]); idb = P.sb("idb", [128, 128], BF16)
    ones_bd = P.sb("ones_bd", [128, 128])
    E2 = P.sb("E2", [128, 2], BF16)
    Ms = P.sb("Ms", [128, 128], BF16); Mi = P.sb("Mi", [128, 128], BF16); Mt = P.sb("Mt", [128, 128], BF16)
    rmask = P.sb("rmask", [128, 512])
    vmt = P.sb("vmt", [128, 512])
    xt = [P.sb("xt%d" % i, [128, D]) for i in range(4)]
    xh = P.sb("xh", [2, D])
    hb = P.sb("hb", [128, D], BF16)
    ss = P.sb("ss", [128, 4])
    hTw = P.sb("hTw", [128, NK, 514], BF16)
    tl = P.sb("tl", [128, 32])
    T = {n: P.sb(n, [128, 512]) for n in ("zr", "zk", "zv", "zL", "sw", "aa", "kkr", "sq", "rn", "kd", "kka", "cl", "ex",
                                          "rem", "remx", "ea", "eb")}
    LW = P.sb("LW", [128, 512], BF16)
    sg = P.sb("sg", [128, 512], BF16) if fwd else None
    ARt = [P.sb("ARt%d" % c, [128, 4, 256], BF16) for c in range(4)]
    rk = [P.sb("rk%d" % c, [128, 512], BF16) for c in range(4)]
    wc = P.sb("wc", [128, 4, 4])
    Bt = P.sb("Bt", [128, 512], BF16); Kt = P.sb("Kt", [128, 512], BF16)
    Bh = P.sb("Bh", [128, 512], BF16); Kh = P.sb("Kh", [128, 512], BF16); vb = P.sb("vb", [128, 512], BF16)
    tok = P.sb("tok", [128, 4, 4, 384], BF16)
    SC_LM = P.sb("SC_LM", [128, 32, 256], BF16)
    SC_MP = P.sb("SC_MP", [128, 32, 256], BF16)
    QP = [P.sb("QP%d" % i, [128, 4, 256], BF16) for i in range(2)]
    QTt = [P.sb("QT%d" % i, [128, 4, 128], BF16) for i in range(2)]
    S32 = P.sb("S32", [128, 4, 128])
    Sbf = P.sb("Sbf", [128, 4, 128], BF16)
    RHSb = P.sb("RHSb", [128, 4, 128], BF16)
    Ub = P.sb("Ub", [128, 4, 128], BF16)
    yt = P.sb("yt", [128, 512])
    st8 = P.sb("st8", [128, 8])
    gt = P.sb("gt", [128, 512]) if fwd else None
    pm = [P.ps("pm%d" % i, [128, 512]) for i in range(2)]
    pc2 = P.ps("pc2", [128, 512])
    pT = P.ps("pT", [128, 1024], BF16)
    pc = [None] * 4 + [P.ps("pc%d" % i, [128, 512]) for i in range(4, 8)]

    P.add('sp', lambda e: e.dma_start(out=g1t[:], in_=g1c[:, :]), writes=['wscale'], dma=True)
    P.add('sp', lambda e: e.dma_start(out=mp[:, 0:nch], in_=mupc[:, :]), writes=['mu'], dma=True)
    P.add('sp', lambda e: e.dma_start(out=mn[:, 0:nch], in_=munc[:, :]), writes=['mu'], dma=True)
    for i, src in enumerate((w0c, a0c, kkc, kac, rkc)):
        P.add('sp', lambda e, i=i, src=src: e.dma_start(out=cv[:, 4 * i:4 * i + 4], in_=src[:, :]), writes=['cv'], dma=True)
    P.add('dve', lambda e: e.tensor_tensor(out=c0[:, 0:nch], in0=mp[:, 0:nch], in1=mn[:, 0:nch], op=ALU.add),
          reads=['mu'], writes=['c0'])
    P.add('dve', lambda e: e.tensor_scalar(out=c0[:, 0:nch], in0=c0[:, 0:nch], scalar1=-1.0, scalar2=1.0, op0=ALU.mult,
                                           op1=ALU.add), writes=['c0'])
    emit_identity(P, idb, idf)
    P.add('pool', lambda e: e.memset(ones_bd[:], 0.0), writes=['ones_bd'])
    P.add('pool', lambda e: e.memset(ones_bd[0:64, 0:64], 1.0), writes=['ones_bd'])
    P.add('pool', lambda e: e.memset(ones_bd[64:128, 64:128], 1.0), writes=['ones_bd'])
    P.add('pool', lambda e: e.memset(E2[:], 0.0), writes=['E2'])
    P.add('pool', lambda e: e.memset(E2[0:64, 0:1], 1.0), writes=['E2'])
    P.add('pool', lambda e: e.memset(E2[64:128, 1:2], 1.0), writes=['E2'])
    for M, strict, transposed in ((Ms, True, False), (Mi, False, False), (Mt, True, True)):
        key = 'masks'
        P.add('pool', lambda e, M=M: e.memset(idf[:], 1.0), writes=['idf'])
        sgn = 1 if (fwd != transposed) else -1
        P.add('pool', lambda e, M=M, sgn=sgn, strict=strict: e.affine_select(
            out=idf[:], in_=idf[:], pattern=[[sgn, 128]], compare_op=(ALU.is_gt if strict else ALU.is_ge), fill=0.0, base=0,
            channel_multiplier=-sgn), writes=['idf'])
        P.add('dve', lambda e, M=M: e.tensor_copy(out=M[:], in_=idf[:]), reads=['idf'], writes=[key])
    P.add('pool', lambda e: e.memset(rmask[:], 1.0), writes=['rmask'])
    P.add('pool', lambda e: e.memset(rmask[:].rearrange("p (j t) -> p j t", t=128)[:, :, 0:1], 0.0), writes=['rmask'])
    for c in range(4):
        P.add('pool', lambda e, c=c: e.memset(S32[:, c, :], 0.0), writes=['S32_%d' % c])
        P.add('pool', lambda e, c=c: e.memset(Sbf[:, c, :], 0.0), writes=['Sbf%d' % c])
    stage = [T["zr"], T["zk"]]
    load_weight_bf16(P, WRd, WR, NK, nch * 128, stage, 'WR', scale_tile=g1t, col_piece=512)
    P.add('sp', lambda e: e.dma_start(out=T["zv"][:], in_=Wld[:, :]), writes=['zv'], dma=True)
    P.add('act', lambda e: e.activation(out=Wl[:], in_=T["zv"][:], func=AF.Identity), reads=['zv'], writes=['Wl'])
    if fwd:
        P.add('sp', lambda e: e.dma_start(out=T["zL"][:], in_=Gld[:, :]), writes=['zL'], dma=True)
        P.add('act', lambda e: e.activation(out=Gl[:], in_=T["zL"][:], func=AF.Identity), reads=['zL'], writes=['Gl'])

    npm = [0]

    def inproj_shift(ci, dst, dkey):
        b = npm[0] % 2
        npm[0] += 1
        pmb, pk = pm[b], 'pm%d' % b
        for k in range(NK):
            P.add('pe', lambda e, k=k: e.matmul(pmb[:, :], WR[:, k, ci * 128:(ci + 1) * 128], hTw[:, k, 0:512],
                                                start=(k == 0), stop=(k == NK - 1)), reads=['WR', 'hTw'], writes=[pk])
        P.add('act', lambda e: e.activation(out=dst[:], in_=pmb[:], func=AF.Identity, scale=c0[:, ci:ci + 1]),
              reads=['c0'], writes=[pk, dkey])
        P.add('dve', lambda e: e.scalar_tensor_tensor(out=dst[:, 1:512], in0=pmb[:, 0:511], scalar=mp[:, ci:ci + 1],
                                                      in1=dst[:, 1:512], op0=ALU.mult, op1=ALU.add),
              reads=['mu'], writes=[pk, dkey])
        P.add('dve', lambda e: e.scalar_tensor_tensor(out=dst[:, 0:511], in0=pmb[:, 1:512], scalar=mn[:, ci:ci + 1],
                                                      in1=dst[:, 0:511], op0=ALU.mult, op1=ALU.add),
              reads=['mu'], writes=[pk, dkey])
        P.add('dve', lambda e: e.scalar_tensor_tensor(out=dst[:, 0:1], in0=tl[:, 2 * ci:2 * ci + 1], scalar=mp[:, ci:ci + 1],
                                                      in1=dst[:, 0:1], op0=ALU.mult, op1=ALU.add),
              reads=['mu', 'tl'], writes=[dkey])
        P.add('dve', lambda e: e.scalar_tensor_tensor(out=dst[:, 511:512], in0=tl[:, 2 * ci + 1:2 * ci + 2],
                                                      scalar=mn[:, ci:ci + 1], in1=dst[:, 511:512], op0=ALU.mult, op1=ALU.add),
              reads=['mu', 'tl'], writes=[dkey])

    order = list(range(NSC)) if fwd else list(range(NSC - 1, -1, -1))
    chunk_order = [0, 1, 2, 3] if fwd else [3, 2, 1, 0]
    def superchunk(sc):
        t0 = sc * 512
        own = (own0 <= t0 < own1)
        for j in range(4):
            r0 = 128 + t0 + j * 128
            P.add('sp', lambda e, j=j, r0=r0: e.dma_start(out=xt[j][:], in_=xw[r0:r0 + 128, :]), writes=['xt%d' % j], dma=True)
        P.add('sp', lambda e: e.dma_start(out=xh[0:1, :], in_=xw[127 + t0:128 + t0, :]), writes=['xh'], dma=True)
        P.add('sp', lambda e: e.dma_start(out=xh[1:2, :], in_=xw[128 + t0 + 512:129 + t0 + 512, :]), writes=['xh'], dma=True)
        P.add('sp', lambda e: e.dma_start(out=vmt[:], in_=vmask[0:1, t0:t0 + 512].partition_broadcast(128)),
              writes=['vmt'], dma=True)
        for j in range(4):
            emit_norm_hT_n(P, xt[j][:], 'xt%d' % j, 128, hb, hTw[:, :, j * 128:(j + 1) * 128], 'hTw', pT, 'pT', ss, idb)
        emit_norm_hT_n(P, xh[0:2, :], 'xh', 2, hb, hTw[:, :, 512:514], 'hTw', pT, 'pT', ss, idb)
        chunks = list(range(13)) + ([CG] if (fwd and own) else [])
        for ci in chunks:
            for k in range(NK):
                P.add('pe', lambda e, k=k, ci=ci: e.matmul(pc2[:, 2 * ci:2 * ci + 2], WR[:, k, ci * 128:(ci + 1) * 128],
                                                           hTw[:, k, 512:514], start=(k == 0), stop=(k == NK - 1)),
                      reads=['WR', 'hTw'], writes=['pc2'])
        P.add('dve', lambda e: e.tensor_copy(out=tl[:, 0:28], in_=pc2[:, 0:28]), writes=['pc2', 'tl'])
        inproj_shift(CL, T["zL"], 'zL')
        P.add('act', lambda e: e.activation(out=LW[0:64, :], in_=T["zL"][0:64, :], func=AF.Tanh), reads=['zL'], writes=['LW'])
        P.add('act', lambda e: e.activation(out=LW[64:128, :], in_=T["zL"][64:128, :], func=AF.Identity), reads=['zL'],
              writes=['LW'])
        if fwd and own:
            inproj_shift(CG, T["zL"], 'zL')
            P.add('act', lambda e: e.activation(out=sg[:], in_=T["zL"][:], func=AF.Sigmoid), reads=['zL'], writes=['sg'])
        def pair(c):
            zr, zk, zv = T["zr"], T["zk"], T["zv"]
            inproj_shift(c, zr, 'zr')
            inproj_shift(4 + c, zk, 'zk')
            inproj_shift(8 + c, zv, 'zv')
            cs = slice(c * 128, (c + 1) * 128)
            P.add('pe', lambda e, cs=cs: e.matmul(pc[4][:, :], Wl[0:64, cs], LW[0:64, :], start=True, stop=True),
                  reads=['Wl', 'LW'], writes=['pc4'])
            P.add('pe', lambda e, cs=cs: e.matmul(pc[5][:, :], Wl[64:128, cs], LW[64:128, :], start=True, stop=True),
                  reads=['Wl', 'LW'], writes=['pc5'])
            sw, aa, kkr, sq, rn, kd, kka, cl, ex, rem, remx, ea, eb = (T[n] for n in (
                "sw", "aa", "kkr", "sq", "rn", "kd", "kka", "cl", "ex", "rem", "remx", "ea", "eb"))
            P.add('act', lambda e, c=c: e.activation(out=sw[:], in_=pc[4][:], func=AF.Sigmoid, bias=cv[:, c:c + 1], scale=1.0),
                  reads=['cv'], writes=['pc4', 'sw'])
            P.add('act', lambda e, c=c: e.activation(out=aa[:], in_=pc[5][:], func=AF.Sigmoid, bias=cv[:, 4 + c:5 + c], scale=1.0),
                  reads=['cv'], writes=['pc5', 'aa'])
            P.add('pool', lambda e, c=c: e.tensor_scalar(out=kkr[:], in0=zk[:], scalar1=cv[:, 8 + c:9 + c], scalar2=None,
                                                         op0=ALU.mult), reads=['zk', 'cv'], writes=['kkr'])
            P.add('pool', lambda e: e.tensor_tensor(out=sq[:], in0=kkr[:], in1=kkr[:], op=ALU.mult), reads=['kkr'], writes=['sq'])
            P.add('pe', lambda e: e.matmul(pc[6][:, :], ones_bd[:], sq[:], start=True, stop=True),
                  reads=['ones_bd', 'sq'], writes=['pc6'])
            P.add('act', lambda e: e.activation(out=rn[:], in_=pc[6][:], func=AF.Ln, bias=1e-12, scale=1.0),
                  writes=['pc6', 'rn'])
            P.add('act', lambda e: e.activation(out=rn[:], in_=rn[:], func=AF.Exp, scale=-0.5), writes=['rn'])
            P.add('dve', lambda e: e.tensor_tensor(out=kkr[:], in0=kkr[:], in1=rn[:], op=ALU.mult), reads=['rn'], writes=['kkr'])
            P.add('dve', lambda e, c=c: e.tensor_scalar(out=kd[:], in0=aa[:], scalar1=-1.0, scalar2=cv[:, 12 + c:13 + c],
                                                        op0=ALU.add, op1=ALU.mult), reads=['aa', 'cv'], writes=['kd'])
            P.add('dve', lambda e: e.scalar_tensor_tensor(out=kd[:], in0=kd[:], scalar=1.0, in1=zk[:], op0=ALU.add,
                                                          op1=ALU.mult), reads=['zk'], writes=['kd'])
            P.add('pool', lambda e: e.tensor_tensor(out=kka[:], in0=kkr[:], in1=aa[:], op=ALU.mult),
                  reads=['kkr', 'aa'], writes=['kka'])
            P.add('dve', lambda e: e.tensor_tensor_scan(out=cl[:], data0=rmask[:], data1=sw[:], initial=0.0, op0=ALU.mult,
                                                        op1=ALU.add), reads=['rmask', 'sw'], writes=['cl'])
            cl3 = cl[:].rearrange("p (j t) -> p j t", t=128)
            totb = cl3[:, :, 127:128].broadcast_to([128, 4, 128])
            P.add('pool', lambda e: e.tensor_tensor(out=ex[:], in0=cl[:], in1=sw[:], op=ALU.subtract),
                  reads=['cl', 'sw'], writes=['ex'])
            P.add('dve', lambda e: e.tensor_tensor(out=rem[:].rearrange("p (j t) -> p j t", t=128), in0=totb, in1=cl3,
                                                   op=ALU.subtract), reads=['cl'], writes=['rem'])
            if fwd:
                uA, uR, uB, uH = ex, cl, cl, rem
                kA, kR, kB, kH = 'ex', 'cl', 'cl', 'rem'
            else:
                P.add('dve', lambda e: e.tensor_tensor(out=remx[:].rearrange("p (j t) -> p j t", t=128), in0=totb,
                                                       in1=ex[:].rearrange("p (j t) -> p j t", t=128), op=ALU.subtract),
                      reads=['cl', 'ex'], writes=['remx'])
                uA, uR, uB, uH = rem, remx, remx, ex
                kA, kR, kB, kH = 'rem', 'remx', 'remx', 'ex'
            P.add('act', lambda e, c=c: e.activation(out=wc[:, c, :], in_=cl3[:, :, 127], func=AF.Exp, scale=-DECAY_C),
                  reads=['cl'], writes=['wc'])
            AR = ARt[c]
            arkey = 'ARt%d' % c
            P.add('act', lambda e: e.activation(out=ea[:], in_=uA[:], func=AF.Exp, scale=-DECAY_C), reads=[kA], writes=['ea'])
            P.add('dve', lambda e: e.scalar_tensor_tensor(out=AR[:, :, 0:128], in0=kkr[:].rearrange("p (j t) -> p j t", t=128),
                                                          scalar=-1.0, in1=ea[:].rearrange("p (j t) -> p j t", t=128),
                                                          op0=ALU.mult, op1=ALU.mult), reads=['kkr', 'ea'], writes=[arkey])
            P.add('act', lambda e: e.activation(out=eb[:], in_=uR[:], func=AF.Exp, scale=-DECAY_C), reads=[kR], writes=['eb'])
            P.add('pool', lambda e: e.tensor_tensor(out=AR[:, :, 128:256], in0=zr[:].rearrange("p (j t) -> p j t", t=128),
                                                    in1=eb[:].rearrange("p (j t) -> p j t", t=128), op=ALU.mult),
                  reads=['zr', 'eb'], writes=[arkey])
            P.add('act', lambda e: e.activation(out=ea[:], in_=uB[:], func=AF.Exp, scale=DECAY_C), reads=[kB], writes=['ea'])
            P.add('pool', lambda e: e.tensor_tensor(out=Bt[:], in0=kka[:], in1=ea[:], op=ALU.mult), reads=['kka', 'ea'], writes=['Bt'])
            P.add('dve', lambda e: e.tensor_tensor(out=Kt[:], in0=kd[:], in1=ea[:], op=ALU.mult), reads=['kd', 'ea'], writes=['Kt'])
            P.add('act', lambda e: e.activation(out=eb[:], in_=uH[:], func=AF.Exp, scale=-DECAY_C), reads=[kH], writes=['eb'])
            P.add('pool', lambda e: e.tensor_tensor(out=Bh[:], in0=kka[:], in1=eb[:], op=ALU.mult), reads=['kka', 'eb'], writes=['Bh'])
            P.add('dve', lambda e: e.tensor_tensor(out=Kh[:], in0=kd[:], in1=eb[:], op=ALU.mult), reads=['kd', 'eb'], writes=['Kh'])
            P.add('pool', lambda e: e.tensor_tensor(out=vb[:], in0=zv[:], in1=vmt[:], op=ALU.mult), reads=['zv', 'vmt'], writes=['vb'])
            if own:
                P.add('dve', lambda e, c=c: e.scalar_tensor_tensor(out=rk[c][:], in0=zr[:], scalar=cv[:, 16 + c:17 + c],
                                                                   in1=kd[:], op0=ALU.mult, op1=ALU.mult),
                      reads=['zr', 'kd', 'cv'], writes=['rk%d' % c])
            for j in range(4):
                js = slice(j * 128, (j + 1) * 128)
                for q, (src, sk) in enumerate(((vb, 'vb'), (Bh, 'Bh'), (Kh, 'Kh'))):
                    P.add('pe', lambda e, q=q, src=src, js=js: e.transpose(pT[:, q * 128:(q + 1) * 128], src[:, js], idb[:]),
                          reads=[sk, 'idb'], writes=['pT'])
                P.add('dve', lambda e, c=c, j=j: e.tensor_copy(out=tok[:, c, j, :], in_=pT[:, 0:384]),
                      writes=['pT', 'tok%d' % c])
            def head(hh):
                rows = slice(hh * 64, (hh + 1) * 64)
                u0 = (c * 2 + hh) * 4
                for j in range(4):
                    js = slice(j * 128, (j + 1) * 128)
                    P.add('pe', lambda e, j=j, js=js: e.matmul(pc[4][:, js], Bt[rows, js], AR[rows, j, 0:128], start=True, stop=True),
                          reads=['Bt', arkey], writes=['pc4'])
                    P.add('pe', lambda e, j=j, js=js: e.matmul(pc[5][:, js], Bt[rows, js], AR[rows, j, 128:256], start=True, stop=True),
                          reads=['Bt', arkey], writes=['pc5'])
                    P.add('pe', lambda e, j=j, js=js: e.matmul(pc[6 + j // 2][:, (j % 2) * 256:(j % 2) * 256 + 256], Kt[rows, js],
                                                               AR[rows, j, :], start=True, stop=True),
                          reads=['Kt', arkey], writes=['pc%d' % (6 + j // 2)])
                    P.add('pe', lambda e, j=j, js=js: e.matmul(pc2[:, js], AR[rows, j, 0:128], Bt[rows, js], start=True, stop=True),
                          reads=['Bt', arkey], writes=['pc2'])
                Msb = Ms[:].unsqueeze(1).broadcast_to([128, 4, 128])
                Mib = Mi[:].unsqueeze(1).broadcast_to([128, 4, 128])
                Mtb = Mt[:].unsqueeze(1).broadcast_to([128, 4, 128])
                P.add('dve', lambda e: e.tensor_tensor(out=QP[0][:, :, 0:128], in0=pc[4][:].rearrange("p (u t) -> p u t", t=128),
                                                       in1=Msb, op=ALU.mult), reads=['masks'], writes=['pc4', 'QP0'])
                P.add('dve', lambda e, u0=u0: e.tensor_tensor(out=SC_MP[:, u0:u0 + 4, 0:128],
                                                              in0=pc[5][:].rearrange("p (u t) -> p u t", t=128), in1=Mib, op=ALU.mult),
                      reads=['masks'], writes=['pc5', 'SC_MP'])
                for half in range(2):
                    M2s = Ms[:].unsqueeze(1).broadcast_to([128, 2, 128])
                    M2i = Mi[:].unsqueeze(1).broadcast_to([128, 2, 128])
                    pcb = pc[6 + half]
                    P.add('dve', lambda e, u0=u0, half=half, pcb=pcb, M2s=M2s: e.tensor_tensor(
                        out=SC_LM[:, u0 + 2 * half:u0 + 2 * half + 2, 0:128],
                        in0=pcb[:].rearrange("p (u t) -> p u t", t=256)[:, :, 0:128], in1=M2s, op=ALU.mult),
                        reads=['masks'], writes=['pc%d' % (6 + half), 'SC_LM'])
                    P.add('dve', lambda e, u0=u0, half=half, pcb=pcb, M2i=M2i: e.tensor_tensor(
                        out=SC_LM[:, u0 + 2 * half:u0 + 2 * half + 2, 128:256],
                        in0=pcb[:].rearrange("p (u t) -> p u t", t=256)[:, :, 128:256], in1=M2i, op=ALU.mult),
                        reads=['masks'], writes=['pc%d' % (6 + half), 'SC_LM'])
                P.add('dve', lambda e: e.tensor_tensor(out=QTt[0][:], in0=pc2[:].rearrange("p (u t) -> p u t", t=128), in1=Mtb,
                                                       op=ALU.mult), reads=['masks'], writes=['pc2', 'QT0'])
                P.add('pool', lambda e: e.tensor_tensor(out=QP[0][:, :, 128:256], in0=QP[0][:, :, 0:128],
                                                        in1=idb[:].unsqueeze(1).broadcast_to([128, 4, 128]), op=ALU.add),
                      reads=['idb'], writes=['QP0'])
                for u in range(4):
                    P.add('pe', lambda e, u=u: e.matmul(pm[u // 2][:, (u % 2) * 256:(u % 2) * 256 + 128], QTt[0][:, u, :],
                                                        QP[0][:, u, 0:128], start=True, stop=True),
                          reads=['QT0', 'QP0'], writes=['pm%d' % (u // 2)])
                    P.add('pe', lambda e, u=u: e.matmul(pc2[:, u * 128:(u + 1) * 128], QP[0][:, u, 0:128], QTt[0][:, u, :],
                                                        start=True, stop=True), reads=['QT0', 'QP0'], writes=['pc2'])
                for half in range(2):
                    P.add('act', lambda e, half=half: e.activation(
                        out=QP[1][:, 2 * half:2 * half + 2, 0:128],
                        in_=pm[half][:].rearrange("p (u t) -> p u t", t=256)[:, :, 0:128], func=AF.Identity),
                        writes=['pm%d' % half, 'QP1'])
                P.add('pool', lambda e: e.tensor_copy(out=QP[1][:, :, 128:256], in_=QP[0][:, :, 128:256]), reads=['QP0'], writes=['QP1'])
                P.add('act', lambda e: e.activation(out=QTt[1][:], in_=pc2[:].rearrange("p (u t) -> p u t", t=128), func=AF.Identity),
                      writes=['pc2', 'QT1'])
                cur = 1
                for lvl in range(1, 7):
                    nxt = 1 - cur
                    ck, nk = 'QP%d' % cur, 'QP%d' % nxt
                    ctk, ntk = 'QT%d' % cur, 'QT%d' % nxt
                    last = (lvl == 6)
                    for u in range(4):
                        if not last:
                            P.add('pe', lambda e, u=u, cur=cur: e.matmul(pm[u // 2][:, (u % 2) * 256:(u % 2) * 256 + 256],
                                                                        QTt[cur][:, u, :], QP[cur][:, u, :], start=True, stop=True),
                                  reads=[ck, ctk], writes=['pm%d' % (u // 2)])
                            P.add('pe', lambda e, u=u, cur=cur: e.matmul(pc2[:, u * 128:(u + 1) * 128], QP[cur][:, u, 0:128],
                                                                        QTt[cur][:, u, :], start=True, stop=True),
                                  reads=[ck, ctk], writes=['pc2'])
                        else:
                            P.add('pe', lambda e, u=u, cur=cur: e.matmul(pm[u // 2][:, (u % 2) * 256 + 128:(u % 2) * 256 + 256],
                                                                        QTt[cur][:, u, :], QP[cur][:, u, 128:256], start=True, stop=True),
                                  reads=[ck, ctk], writes=['pm%d' % (u // 2)])
                    for half in range(2):
                        pv = pm[half][:].rearrange("p (u t) -> p u t", t=256)
                        us = slice(2 * half, 2 * half + 2)
                        if not last:
                            P.add('act', lambda e, pv=pv, us=us, nxt=nxt: e.activation(out=QP[nxt][:, us, 0:128], in_=pv[:, :, 0:128],
                                                                                    func=AF.Identity),
                                  writes=['pm%d' % half, nk])
                            P.add('dve', lambda e, pv=pv, us=us, nxt=nxt, cur=cur: e.tensor_tensor(
                                out=QP[nxt][:, us, 128:256], in0=pv[:, :, 128:256], in1=QP[cur][:, us, 128:256], op=ALU.add),
                                reads=[ck], writes=['pm%d' % half, nk])
                        else:
                            P.add('dve', lambda e, pv=pv, us=us, cur=cur, u0=u0, half=half: e.tensor_tensor(
                                out=SC_MP[:, u0 + 2 * half:u0 + 2 * half + 2, 128:256], in0=pv[:, :, 128:256],
                                in1=QP[cur][:, us, 128:256], op=ALU.add),
                                reads=[ck], writes=['pm%d' % half, 'SC_MP'])
                    if not last:
                        P.add('act', lambda e, nxt=nxt: e.activation(out=QTt[nxt][:], in_=pc2[:].rearrange("p (u t) -> p u t", t=128),
                                                                     func=AF.Identity), writes=['pc2', ntk])
                    cur = nxt
            head(0)
            head(1)
        for c in range(4):
            pair(c)
        for j in chunk_order:
            js = slice(j * 128, (j + 1) * 128)
            for c in range(4):
                pcc, pk = pc[4 + c], 'pc%d' % (4 + c)
                P.add('pe', lambda e, c=c, j=j, pcc=pcc: e.matmul(pcc[:, 0:128], ARt[c][:, j, 0:128], Sbf[:, c, :], start=True, stop=False),
                      reads=['ARt%d' % c, 'Sbf%d' % c], writes=[pk])
                for hh in range(2):
                    u = (c * 2 + hh) * 4 + j
                    hs = slice(hh * 64, (hh + 1) * 64)
                    P.add('pe', lambda e, c=c, j=j, u=u, hs=hs, hh=hh, pcc=pcc: e.matmul(pcc[:, hs], SC_LM[:, u, 0:128], tok[:, c, j, hs],
                                                                                 start=False, stop=(hh == 1)),
                          reads=['SC_LM', 'tok%d' % c], writes=[pk])
                P.add('act', lambda e, c=c, pcc=pcc: e.activation(out=RHSb[:, c, :], in_=pcc[:, 0:128], func=AF.Identity),
                      writes=[pk, 'RHSb%d' % c])
            for c in range(4):
                pcc, pk = pc[4 + c], 'pc%d' % (4 + c)
                for hh in range(2):
                    u = (c * 2 + hh) * 4 + j
                    hs = slice(hh * 64, (hh + 1) * 64)
                    P.add('pe', lambda e, c=c, u=u, hs=hs, hh=hh, pcc=pcc: e.matmul(pcc[:, 128 + hh * 64:192 + hh * 64], SC_MP[:, u, 128:256],
                                                                            RHSb[:, c, hs], start=True, stop=True),
                          reads=['SC_MP', 'RHSb%d' % c], writes=[pk])
                P.add('dve', lambda e, c=c, pcc=pcc: e.tensor_copy(out=Ub[:, c, :], in_=pcc[:, 128:256]), writes=[pk, 'Ub%d' % c])
            for c in range(4):
                pcc, pk = pc[4 + c], 'pc%d' % (4 + c)
                if own:
                    P.add('pe', lambda e, c=c, j=j, pcc=pcc: e.matmul(pcc[:, 256:384], ARt[c][:, j, 128:256], Sbf[:, c, :], start=True, stop=False),
                          reads=['ARt%d' % c, 'Sbf%d' % c], writes=[pk])
                    for hh in range(2):
                        u = (c * 2 + hh) * 4 + j
                        hs = slice(hh * 64, (hh + 1) * 64)
                        os_ = slice(256 + hh * 64, 320 + hh * 64)
                        P.add('pe', lambda e, c=c, u=u, hs=hs, os_=os_, pcc=pcc: e.matmul(pcc[:, os_], SC_MP[:, u, 0:128], Ub[:, c, hs],
                                                                                  start=False, stop=False),
                              reads=['SC_MP', 'Ub%d' % c], writes=[pk])
                        P.add('pe', lambda e, c=c, j=j, u=u, hs=hs, os_=os_, hh=hh, pcc=pcc: e.matmul(pcc[:, os_], SC_LM[:, u, 128:256], tok[:, c, j, hs],
                                                                                              start=False, stop=(hh == 1)),
                              reads=['SC_LM', 'tok%d' % c], writes=[pk])
                P.add('pe', lambda e, c=c, j=j, pcc=pcc: e.matmul(pcc[:, 384:512], tok[:, c, j, 128:256], Ub[:, c, :], start=True, stop=False),
                      reads=['tok%d' % c, 'Ub%d' % c], writes=[pk])
                P.add('pe', lambda e, c=c, j=j, pcc=pcc: e.matmul(pcc[:, 384:512], tok[:, c, j, 256:384], tok[:, c, j, 0:128], start=False, stop=True),
                      reads=['tok%d' % c], writes=[pk])
                if own:
                    P.add('act', lambda e, c=c, pcc=pcc: e.activation(out=yt[:, c * 128:(c + 1) * 128], in_=pcc[:, 256:384], func=AF.Identity),
                          writes=[pk, 'yt'])
                for hh in range(2):
                    hs = slice(hh * 64, (hh + 1) * 64)
                    P.add('dve', lambda e, c=c, j=j, hs=hs, hh=hh, pcc=pcc: e.scalar_tensor_tensor(
                        out=S32[hs, c, hs], in0=S32[hs, c, hs], scalar=wc[hs, c, j:j + 1], in1=pcc[hs, 384 + hh * 64:448 + hh * 64],
                        op0=ALU.mult, op1=ALU.add), reads=['wc'], writes=[pk, 'S32_%d' % c])
                P.add('act', lambda e, c=c: e.activation(out=Sbf[:, c, :], in_=S32[:, c, :], func=AF.Identity),
                      reads=['S32_%d' % c], writes=['Sbf%d' % c])
            if own:
                tr = t0 + j * 128 - own0
                for c in range(4):
                    P.add('pe', lambda e, c=c, js=js: e.matmul(pc2[:, 2 * c:2 * c + 2], rk[c][:, js], E2[:], start=True, stop=True),
                          reads=['rk%d' % c, 'E2'], writes=['pc2'])
                P.add('dve', lambda e: e.tensor_copy(out=st8[:], in_=pc2[:, 0:8]), writes=['pc2', 'st8'])
                P.add('pool', lambda e, tr=tr: e.dma_start(out=y_out[tr:tr + 128, :], in_=yt[:]), reads=['yt'], dma=True)
                P.add('pool', lambda e, tr=tr: e.dma_start(out=s_out[tr:tr + 128, :], in_=st8[:]), reads=['st8'], dma=True)
                if fwd:
                    P.add('pe', lambda e, js=js: e.matmul(pm[0][:, :], sg[:, js], Gl[:], start=True, stop=True),
                          reads=['sg', 'Gl'], writes=['pm0'])
                    P.add('act', lambda e: e.activation(out=gt[:], in_=pm[0][:], func=AF.Identity), writes=['pm0', 'gt'])
                    P.add('pool', lambda e, tr=tr: e.dma_start(out=g_out[tr:tr + 128, :], in_=gt[:]), reads=['gt'], dma=True)
                    P.add('pool', lambda e, tr=tr, j=j: e.dma_start(
                        out=v_out[tr:tr + 128, :].rearrange("t (c v) -> t c v", c=4), in_=tok[:, :, j, 0:128]),
                        reads=['tok0', 'tok1', 'tok2', 'tok3'], dma=True)
    for sc in order:
        superchunk(sc)
    P.close()


def phase_assembly(nc, Town, yf, yb, sf, sbk, gd_, vd, yatt, xown, woutd, lnxw, lnxb, x1out):
    P = Phase(nc, "asm")
    NT = Town // 128
    Wo = P.sb("Wo", [128, 8, D], BF16)
    stage = [P.sb("stage%d" % i, [128, D]) for i in range(2)]
    lw = P.sb("lw", [128, 512]); lb = P.sb("lb", [128, 512])
    idf = P.sb("idf", [128, 128]); idb = P.sb("idb", [128, 128], BF16)
    yft = P.sb("yft", [128, 512]); ybt = P.sb("ybt", [128, 512]); gt = P.sb("gt", [128, 512])
    sq = P.sb("sq", [128, 512]); bon = P.sb("bon", [128, 512])
    vt = P.sb("vt", [128, 512], BF16)
    s8 = P.sb("s8", [128, 48])
    xo = P.sb("xo", [128, D])
    yr = P.sb("yr", [128, 512], BF16)
    ycT = P.sb("ycT", [128, 8, 128], BF16)
    pT = P.ps("pT", [128, 1024], BF16)
    pO = [P.ps("pO%d" % i, [128, 512]) for i in range(2)]
    P.add('sp', lambda e: e.dma_start(out=lw[:], in_=lnxw.partition_broadcast(128)), writes=['lw'], dma=True)
    P.add('sp', lambda e: e.dma_start(out=lb[:], in_=lnxb.partition_broadcast(128)), writes=['lb'], dma=True)
    emit_identity(P, idb, idf)
    load_weight_bf16(P, woutd, Wo, 8, D, stage, 'Wo', scale_tile=None, col_piece=D)

    def tile(n):
        r = slice(n * 128, (n + 1) * 128)
        P.add('sp', lambda e: e.dma_start(out=yft[:], in_=yf[r, :]), writes=['yft'], dma=True)
        P.add('sp', lambda e: e.dma_start(out=ybt[:], in_=yb[r, :]), writes=['ybt'], dma=True)
        P.add('sp', lambda e: e.dma_start(out=gt[:], in_=gd_[r, :]), writes=['gt'], dma=True)
        P.add('sp', lambda e: e.dma_start(out=vt[:], in_=vd[r, :]), writes=['vt'], dma=True)
        P.add('sp', lambda e: e.dma_start(out=s8[:, 0:8], in_=sf[r, :]), writes=['s8a'], dma=True)
        P.add('sp', lambda e: e.dma_start(out=s8[:, 8:16], in_=sbk[r, :]), writes=['s8b'], dma=True)
        P.add('sp', lambda e: e.dma_start(out=xo[:], in_=xown[r, :]), writes=['xo'], dma=True)
        P.add('sp', lambda e: e.dma_start(out=ycT[:, 4:8, :], in_=yatt[:, :, r].rearrange("g p t -> p g t")),
              writes=['ycTa'], dma=True)
        y3 = yft[:].rearrange("p (h c) -> p h c", c=64)

        def b8(col):
            return s8[:, col:col + 8].unsqueeze(2).broadcast_to([128, 8, 64])
        P.add('dve', lambda e: e.tensor_tensor(out=yft[:], in0=yft[:], in1=ybt[:], op=ALU.add), reads=['ybt'], writes=['yft'])
        P.add('dve', lambda e: e.tensor_reduce(out=s8[:, 16:24], in_=y3, axis=AX.X, op=ALU.add), reads=['yft'], writes=['s8c'])
        P.add('dve', lambda e: e.tensor_scalar(out=s8[:, 16:24], in0=s8[:, 16:24], scalar1=1.0 / 64, scalar2=None, op0=ALU.mult),
              writes=['s8c'])
        P.add('dve', lambda e: e.tensor_tensor(out=y3, in0=y3, in1=b8(16), op=ALU.subtract), reads=['s8c'], writes=['yft'])
        P.add('pool', lambda e: e.tensor_tensor(out=sq[:], in0=yft[:], in1=yft[:], op=ALU.mult), reads=['yft'], writes=['sq'])
        P.add('dve', lambda e: e.tensor_reduce(out=s8[:, 24:32], in_=sq[:].rearrange("p (h c) -> p h c", c=64), axis=AX.X,
                                               op=ALU.add), reads=['sq'], writes=['s8d'])
        emit_rstd(P, s8[:, 24:32], s8[:, 32:40], s8[:, 40:48], 1.0 / 64, LNX_EPS, ['s8d'], ['s8e'])
        P.add('dve', lambda e: e.tensor_tensor(out=y3, in0=y3, in1=b8(40), op=ALU.mult), reads=['s8e'], writes=['yft'])
        P.add('dve', lambda e: e.tensor_tensor(out=yft[:], in0=yft[:], in1=lw[:], op=ALU.mult), reads=['lw'], writes=['yft'])
        P.add('dve', lambda e: e.tensor_tensor(out=yft[:], in0=yft[:], in1=lb[:], op=ALU.add), reads=['lb'], writes=['yft'])
        P.add('dve', lambda e: e.tensor_tensor(out=s8[:, 0:8], in0=s8[:, 0:8], in1=s8[:, 8:16], op=ALU.add), reads=['s8b'],
              writes=['s8a'])
        P.add('dve', lambda e: e.scalar_tensor_tensor(out=bon[:].rearrange("p (h c) -> p h c", c=64),
                                                      in0=vt[:].rearrange("p (h c) -> p h c", c=64), scalar=0.5, in1=b8(0),
                                                      op0=ALU.mult, op1=ALU.mult), reads=['vt', 's8a'], writes=['bon'])
        P.add('pool', lambda e: e.tensor_tensor(out=yft[:], in0=yft[:], in1=bon[:], op=ALU.add), reads=['bon'], writes=['yft'])
        P.add('dve', lambda e: e.tensor_tensor(out=yr[:], in0=yft[:], in1=gt[:], op=ALU.mult), reads=['yft', 'gt'], writes=['yr'])
        for c in range(4):
            P.add('pe', lambda e, c=c: e.transpose(pT[:, c * 128:(c + 1) * 128], yr[:, c * 128:(c + 1) * 128], idb[:]),
                  reads=['yr', 'idb'], writes=['pT'])
        P.add('act', lambda e: e.activation(out=ycT[:, 0:4, :], in_=pT[:, 0:512].rearrange("p (c t) -> p c t", c=4),
                                            func=AF.Identity), writes=['pT', 'ycTr'])
        for half in range(2):
            for ch in range(8):
                P.add('pe', lambda e, half=half, ch=ch: e.matmul(pO[half][:, :], ycT[:, ch, :], Wo[:, ch, half * 512:(half + 1) * 512],
                                                                 start=(ch == 0), stop=(ch == 7)),
                      reads=['ycTr', 'ycTa', 'Wo'], writes=['pO%d' % half])
            P.add('dve', lambda e, half=half: e.tensor_tensor(out=xo[:, half * 512:(half + 1) * 512], in0=pO[half][:],
                                                              in1=xo[:, half * 512:(half + 1) * 512], op=ALU.add),
                  writes=['pO%d' % half, 'xo'])
        P.add('pool', lambda e: e.dma_start(out=x1out[r, :], in_=xo[:]), reads=['xo'], dma=True)
    for n in range(NT):
        tile(n)
    P.close()


def _cols(vec, nchunk):
    return np.ascontiguousarray(np.asarray(vec, np.float32).reshape(nchunk, 128).T)


def host_prep(p):
    f = lambda a: np.asarray(a, np.float32)
    w_in = f(p['w_in'])[0]
    mu_p = f(p['mu_prev'])[0]
    mu_n = f(p['mu_next'])[0]
    r_, k_, v_ = slice(0, 512), slice(512, 1024), slice(1024, 1536)
    wdf, wdb, adf, adb, gdc = slice(1536, 1600), slice(1600, 1664), slice(1664, 1728), slice(1728, 1792), slice(1792, 1920)

    def cat(a, sl):
        return np.concatenate([a[..., s] for s in sl], axis=-1)
    slf = [r_, k_, v_, wdf, adf, gdc]
    slb = [r_, k_, v_, wdb, adb]
    qcols = np.concatenate([np.arange(1920 + h * 64, 1920 + (h + 1) * 64) for h in QPERM])
    out = {}
    out['WRf'] = np.ascontiguousarray(cat(w_in, slf)); out['WRb'] = np.ascontiguousarray(cat(w_in, slb))
    out['mupf'] = _cols(cat(mu_p, slf), 14); out['munf'] = _cols(cat(mu_n, slf), 14)
    out['mupb'] = _cols(cat(mu_p, slb), 13); out['munb'] = _cols(cat(mu_n, slb), 13)
    out['winA'] = np.ascontiguousarray(np.concatenate([w_in[:, qcols], w_in[:, 2432:2688]], axis=1))
    out['g1c'] = _cols(f(p['norm1_g'])[0], 8)
    out['Wlf'] = np.ascontiguousarray(np.concatenate([f(p['w_lora_f'])[0], f(p['a_lora_f'])[0]], axis=0))
    out['Wlb'] = np.ascontiguousarray(np.concatenate([f(p['w_lora_b'])[0], f(p['a_lora_b'])[0]], axis=0))
    out['Gl'] = np.ascontiguousarray(f(p['g_lora'])[0])
    for nm in ('w0_f', 'w0_b', 'a0_f', 'a0_b', 'k_k', 'k_a'):
        out[nm] = _cols(f(p[nm])[0], 4)
    out['r_k'] = _cols(f(p['r_k'])[0].reshape(512), 4)
    out['lnxw'] = np.ascontiguousarray(f(p['lnx_w'])[0].reshape(1, 512)); out['lnxb'] = np.ascontiguousarray(f(p['lnx_b'])[0].reshape(1, 512))
    out['qg'] = np.ascontiguousarray(f(p['q_gain'])[0].reshape(1, 64)); out['kg'] = np.ascontiguousarray(f(p['k_gain'])[0].reshape(1, 64))
    w_out = f(p['w_out'])[0]
    arows = np.concatenate([np.arange(512 + h * 64, 512 + (h + 1) * 64) for h in QPERM])
    out['wout'] = np.ascontiguousarray(np.concatenate([w_out[0:512], w_out[arows]], axis=0))
    out['g2c'] = _cols(f(p['norm2_g'])[0], 8)
    out['gf'] = np.ascontiguousarray(f(p['norm_f_g']).reshape(1, D))
    out['wg'] = np.ascontiguousarray(f(p['ffn_gate'])[0]); out['wu'] = np.ascontiguousarray(f(p['ffn_up'])[0])
    out['wd'] = np.ascontiguousarray(f(p['ffn_down'])[0])
    return out


WEIGHT_SPECS = [('WRf', [D, 1792]), ('WRb', [D, 1664]), ('mupf', [128, 14]), ('munf', [128, 14]), ('mupb', [128, 13]),
                ('munb', [128, 13]), ('winA', [D, 768]), ('g1c', [128, 8]), ('Wlf', [128, 512]), ('Wlb', [128, 512]),
                ('Gl', [128, 512]), ('w0_f', [128, 4]), ('w0_b', [128, 4]), ('a0_f', [128, 4]), ('a0_b', [128, 4]),
                ('k_k', [128, 4]), ('k_a', [128, 4]), ('r_k', [128, 4]), ('lnxw', [1, 512]), ('lnxb', [1, 512]),
                ('qg', [1, 64]), ('kg', [1, 64]), ('wout', [D, D]), ('g2c', [128, 8]), ('gf', [1, D]),
                ('wg', [D, DFF]), ('wu', [D, DFF]), ('wd', [DFF, D])]


def declare_weights(nc):
    return {n: nc.dram_tensor(n, s, F32, kind="ExternalInput").ap() for n, s in WEIGHT_SPECS}


def mixer_job(nc, W, tag, xw, vmask, Tw_f, own_f, Tw_b, own_b, xw_f_off, xw_b_off, xk, Tk, xown, Town, qrow0, x1rows, scr):
    phase_attention(nc, xk, xown, W['winA'], W['g1c'], W['qg'], W['kg'], qrow0, scr['yatt'], Tk, Town)
    phase_rwkv(nc, False, xw[xw_b_off:xw_b_off + Tw_b + 256, :], vmask[:, xw_b_off:xw_b_off + Tw_b], Tw_b, own_b[0], own_b[1],
               W['WRb'], 13, W['mupb'], W['munb'], W['g1c'], W['Wlb'], W['w0_b'], W['a0_b'], W['k_k'], W['k_a'], W['r_k'],
               scr['yb'], scr['sb'])
    phase_rwkv(nc, True, xw[xw_f_off:xw_f_off + Tw_f + 256, :], vmask[:, xw_f_off:xw_f_off + Tw_f], Tw_f, own_f[0], own_f[1],
               W['WRf'], 14, W['mupf'], W['munf'], W['g1c'], W['Wlf'], W['w0_f'], W['a0_f'], W['k_k'], W['k_a'], W['r_k'],
               scr['yf'], scr['sf'], Gld=W['Gl'], g_out=scr['g'], v_out=scr['v'])
    phase_assembly(nc, Town, scr['yf'], scr['yb'], scr['sf'], scr['sb'], scr['g'], scr['v'], scr['yatt'], xown, W['wout'],
                   W['lnxw'], W['lnxb'], x1rows)


def phase_ffn(nc, x, y, g2c, gf, wg, wu, wd, ntok):
    Phase._n[0] += 1
    tagp = "ffn%d_" % Phase._n[0]
    ngroups = ntok // GT
    xg = x.rearrange("(n s p) d -> n p s d", s=NSUB, p=128)
    yg = y.rearrange("(n s p) d -> n p s d", s=NSUB, p=128)

    es = contextlib.ExitStack()
    with es:
        def sb(name, shape, dt=F32):
            return es.enter_context(nc.sbuf_tensor(tagp + name, shape, dt))

        def pt(name, shape, dt=F32):
            return es.enter_context(nc.psum_tensor(tagp + name, shape, dt))

        Wg = sb("Wg", [128, NK, DFF], BF16)
        Wu = sb("Wu", [128, NK, DFF], BF16)
        Wd = sb("Wd", [128, NFF, D], BF16)
        stage = [sb("stage%d" % i, [128, 1024]) for i in range(2)]
        g2t = sb("g2t", [128, NK])
        gft = sb("gft", [128, D])
        idf = sb("idf", [128, 128])
        idb = sb("idb", [128, 128], BF16)
        xt = [sb("xt%d" % i, [128, NSUB, D]) for i in range(2)]
        hb = sb("hb", [128, NSUB, D], BF16)
        hT = sb("hT", [128, NK, GT], BF16)
        actT = sb("actT", [128, NFF, GT], BF16)
        tmp = [sb("tmp%d" % i, [128, GT]) for i in range(2)]
        ot = sb("ot", [128, NSUB, D])
        ss = sb("ss", [128, 4 * NSUB])
        pst = [pt("pst%d" % i, [128, 1024], BF16)[:, 0:GT] for i in range(2)]
        psg = [pt("psg%d" % i, [128, 512])[:, 0:GT] for i in range(2)]
        psu = [pt("psu%d" % i, [128, 512])[:, 0:GT] for i in range(2)]
        psd = [pt("psd%d" % i, [128, 512]) for i in range(2)]

        S = Sched(nc)

        S.add('sp', lambda e: e.dma_start(out=g2t[:], in_=g2c[:, :]), writes=['g2t'], dma=True)
        S.add('sp', lambda e: e.dma_start(out=gft[:], in_=gf.partition_broadcast(128)), writes=['gft'], dma=True)
        S.add('pool', lambda e: e.memset(idf[:], 1.0), writes=['idf'])
        S.add('pool', lambda e: e.affine_select(out=idf[:], in_=idf[:], pattern=[[-1, 128]], compare_op=ALU.is_equal,
                                                 fill=0.0, base=0, channel_multiplier=1), writes=['idf'])
        S.add('dve', lambda e: e.tensor_copy(out=idb[:], in_=idf[:]), reads=['idf'], writes=['idb'])

        nst = [0]

        def load_cast(src_ap, dst_ap, ncols, dkey, scale_ap=None):
            i = nst[0] % 2
            nst[0] += 1
            skey = 'stage%d' % i
            S.add('sp', lambda e: e.dma_start(out=stage[i][:, 0:ncols], in_=src_ap), writes=[skey], dma=True)
            if scale_ap is not None:
                S.add('dve', lambda e: e.tensor_scalar(out=dst_ap, in0=stage[i][:, 0:ncols], scalar1=scale_ap, scalar2=None,
                                                        op0=ALU.mult), reads=[skey, 'g2t'], writes=[dkey])
            else:
                S.add('act', lambda e: e.activation(out=dst_ap, in_=stage[i][:, 0:ncols], func=AF.Identity),
                      reads=[skey], writes=[dkey])

        pieces = [(0, 1024), (1024, 1024), (2048, DFF - 2048)]
        for k in range(NK):
            for (c0, cn) in pieces:
                load_cast(wg[k * 128:(k + 1) * 128, c0:c0 + cn], Wg[:, k, c0:c0 + cn], cn, 'Wg', g2t[:, k:k + 1])
                load_cast(wu[k * 128:(k + 1) * 128, c0:c0 + cn], Wu[:, k, c0:c0 + cn], cn, 'Wu', g2t[:, k:k + 1])
        for f in range(NFF):
            load_cast(wd[f * 128:(f + 1) * 128, :], Wd[:, f, :], D, 'Wd')

        nps = [0, 0, 0]
        for g in range(ngroups):
            X = xt[g % 2]
            xk = 'xt%d' % (g % 2)
            S.add('sp', lambda e, X=X, g=g: e.dma_start(out=X[:], in_=xg[g]), writes=[xk], dma=True)
            for s in range(NSUB):
                S.add('act', lambda e, X=X, s=s: e.activation(out=hb[:, s, :], in_=X[:, s, :], func=AF.Square,
                                                              accum_out=ss[:, s:s + 1]),
                      reads=[xk], writes=['hb', 'ss'])
            S.add('act', lambda e: e.activation(out=ss[:, NSUB:2 * NSUB], in_=ss[:, 0:NSUB], func=AF.Ln,
                                                scale=1.0 / D, bias=NORM_EPS), reads=[], writes=['ss'])
            S.add('act', lambda e: e.activation(out=ss[:, 0:NSUB], in_=ss[:, NSUB:2 * NSUB], func=AF.Exp, scale=-0.5),
                  reads=[], writes=['ss'])
            for s in range(NSUB):
                S.add('dve', lambda e, X=X, s=s: e.tensor_scalar(out=hb[:, s, :], in0=X[:, s, :], scalar1=ss[:, s:s + 1],
                                                                 scalar2=None, op0=ALU.mult),
                      reads=[xk, 'ss'], writes=['hb'])
            for k in range(NK):
                P = pst[nps[0] % 2]
                pk = 'pst%d' % (nps[0] % 2)
                nps[0] += 1
                for s in range(NSUB):
                    S.add('pe', lambda e, P=P, s=s, k=k: e.transpose(P[:, s * 128:(s + 1) * 128],
                                                                      hb[:, s, k * 128:(k + 1) * 128], idb[:]),
                          reads=['hb', 'idb'], writes=[pk])
                if k % 2 == 0:
                    S.add('dve', lambda e, P=P, k=k: e.tensor_copy(out=hT[:, k, :], in_=P[:]), writes=[pk, 'hT'])
                else:
                    S.add('act', lambda e, P=P, k=k: e.activation(out=hT[:, k, :], in_=P[:], func=AF.Identity),
                          writes=[pk, 'hT'])
            for f in range(NFF):
                i = nps[1] % 2
                nps[1] += 1
                G, U, T = psg[i], psu[i], tmp[i]
                for k in range(NK):
                    S.add('pe', lambda e, G=G, k=k, f=f: e.matmul(G[:, :], Wg[:, k, f * 128:(f + 1) * 128], hT[:, k, :],
                                                                  start=(k == 0), stop=(k == NK - 1)),
                          reads=['Wg', 'hT'], writes=['psg%d' % i])
                for k in range(NK):
                    S.add('pe', lambda e, U=U, k=k, f=f: e.matmul(U[:, :], Wu[:, k, f * 128:(f + 1) * 128], hT[:, k, :],
                                                                  start=(k == 0), stop=(k == NK - 1)),
                          reads=['Wu', 'hT'], writes=['psu%d' % i])
                S.add('act', lambda e, G=G, T=T: e.activation(out=T[:], in_=G[:], func=AF.Silu),
                      writes=['psg%d' % i, 'tmp%d' % i])
                S.add('dve', lambda e, U=U, T=T, f=f: e.tensor_tensor(out=actT[:, f, :], in0=U[:], in1=T[:], op=ALU.mult),
                      reads=['tmp%d' % i], writes=['psu%d' % i, 'actT'])
            for s in range(NSUB):
                for c in range(2):
                    i = nps[2] % 2
                    nps[2] += 1
                    Pd = psd[i]
                    for f in range(NFF):
                        S.add('pe', lambda e, Pd=Pd, f=f, s=s, c=c: e.matmul(Pd[:, :], actT[:, f, s * 128:(s + 1) * 128],
                                                                            Wd[:, f, c * 512:(c + 1) * 512],
                                                                            start=(f == 0), stop=(f == NFF - 1)),
                              reads=['actT', 'Wd'], writes=['psd%d' % i])
                    S.add('dve', lambda e, Pd=Pd, X=X, s=s, c=c: e.tensor_tensor(out=X[:, s, c * 512:(c + 1) * 512],
                                                                               in0=Pd[:], in1=X[:, s, c * 512:(c + 1) * 512],
                                                                               op=ALU.add),
                          writes=['psd%d' % i, xk])
            for s in range(NSUB):
                S.add('act', lambda e, X=X, s=s: e.activation(out=hb[:, s, :], in_=X[:, s, :], func=AF.Square,
                                                              accum_out=ss[:, 2 * NSUB + s:2 * NSUB + s + 1]),
                      reads=[xk], writes=['hb', 'ss'])
            S.add('act', lambda e: e.activation(out=ss[:, 3 * NSUB:4 * NSUB], in_=ss[:, 2 * NSUB:3 * NSUB], func=AF.Ln,
                                                scale=1.0 / D, bias=NORM_EPS), writes=['ss'])
            S.add('act', lambda e: e.activation(out=ss[:, 2 * NSUB:3 * NSUB], in_=ss[:, 3 * NSUB:4 * NSUB], func=AF.Exp,
                                                scale=-0.5), writes=['ss'])
            for s in range(NSUB):
                S.add('dve', lambda e, X=X, s=s: e.scalar_tensor_tensor(out=ot[:, s, :], in0=X[:, s, :],
                                                                        scalar=ss[:, 2 * NSUB + s:2 * NSUB + s + 1],
                                                                        in1=gft[:], op0=ALU.mult, op1=ALU.mult),
                      reads=[xk, 'ss', 'gft'], writes=['ot'])
            S.add('pool', lambda e, g=g: e.dma_start(out=yg[g], in_=ot[:]), reads=['ot'], dma=True)
        S.emit()


def build_program(T, NPQ, ST):
    Q = ST // 4
    ntok = NPQ * T + Q
    nc = bass.Bass("TRN2", target_bir_lowering=False)
    W = declare_weights(nc)
    xp = nc.dram_tensor("xp", [NPQ, T + 256, D], F32, kind="ExternalInput").ap()
    vones = nc.dram_tensor("vones", [1, T], F32, kind="ExternalInput").ap()
    xsw = nc.dram_tensor("xsw", [7 * Q + 256, D], F32, kind="ExternalInput").ap()
    xsk = nc.dram_tensor("xsk", [ST, D], F32, kind="ExternalInput").ap()
    vms = nc.dram_tensor("vms", [1, 7 * Q], F32, kind="ExternalInput").ap()
    qr0s = nc.dram_tensor("qr0s", [1, 1], F32, kind="ExternalInput").ap()
    qr0p = nc.dram_tensor("qr0p", [1, 1], F32, kind="ExternalInput").ap()
    y = nc.dram_tensor("y", [ntok, D], F32, kind="ExternalOutput").ap()
    x1 = nc.dram_tensor("x1", [ntok, D], F32, kind="Internal").ap()

    def scratch(tag, n):
        return dict(yatt=nc.dram_tensor(tag + "yatt", [4, 128, n], BF16, kind="Internal").ap(),
                    yf=nc.dram_tensor(tag + "yf", [n, 512], F32, kind="Internal").ap(),
                    yb=nc.dram_tensor(tag + "yb", [n, 512], F32, kind="Internal").ap(),
                    sf=nc.dram_tensor(tag + "sf", [n, 8], F32, kind="Internal").ap(),
                    sb=nc.dram_tensor(tag + "sb", [n, 8], F32, kind="Internal").ap(),
                    g=nc.dram_tensor(tag + "g", [n, 512], F32, kind="Internal").ap(),
                    v=nc.dram_tensor(tag + "v", [n, 512], BF16, kind="Internal").ap())
    _skip = ''
    _es = contextlib.ExitStack()
    SemPool.current = SemPool(nc, _es)
    scr = scratch("s_", Q)
    if 's' not in _skip:
      mixer_job(nc, W, "s", xsw, vms, 4 * Q, (3 * Q, 4 * Q), 4 * Q, (0, Q), 0, 3 * Q, xsk, ST,
                xsw[128 + 3 * Q:128 + 4 * Q, :], Q, qr0s, x1[NPQ * T:NPQ * T + Q, :], scr)
    for i in range(0 if 'p' not in _skip else NPQ, NPQ):
        scr = scratch("p%d_" % i, T)
        xown = xp[i, 128:128 + T, :]
        mixer_job(nc, W, "p%d" % i, xp[i], vones, T, (0, T), T, (0, T), 0, 0, xown, T, xown, T, qr0p,
                  x1[i * T:(i + 1) * T, :], scr)
    if 'f' not in _skip:
      phase_ffn(nc, x1, y, W['g2c'], W['gf'], W['wg'], W['wu'], W['wd'], ntok)
    SemPool.current = None
    _es.close()
    return nc


_NC_CACHE = {}


def kernel(x_prompt, x_sample, norm1_g, w_in, mu_prev, mu_next, k_k, k_a, r_k, w0_f, w_lora_f, w0_b, w_lora_b,
           a0_f, a_lora_f, a0_b, a_lora_b, g_lora, lnx_w, lnx_b, q_gain, k_gain, w_out, norm2_g, ffn_gate,
           ffn_up, ffn_down, norm_f_g):
    params = dict(norm1_g=norm1_g, w_in=w_in, mu_prev=mu_prev, mu_next=mu_next, k_k=k_k, k_a=k_a, r_k=r_k, w0_f=w0_f,
                  w_lora_f=w_lora_f, w0_b=w0_b, w_lora_b=w_lora_b, a0_f=a0_f, a_lora_f=a_lora_f, a0_b=a0_b,
                  a_lora_b=a_lora_b, g_lora=g_lora, lnx_w=lnx_w, lnx_b=lnx_b, q_gain=q_gain, k_gain=k_gain, w_out=w_out,
                  norm2_g=norm2_g, ffn_gate=ffn_gate, ffn_up=ffn_up, ffn_down=ffn_down, norm_f_g=norm_f_g)
    x_prompt = np.asarray(x_prompt, np.float32)
    x_sample = np.asarray(x_sample, np.float32)
    B, T, _ = x_prompt.shape
    SB, ST, _ = x_sample.shape
    NPQ = B // NCORES
    Q = ST // 4
    key = (T, NPQ, ST)
    if key not in _NC_CACHE:
        _NC_CACHE[key] = build_program(T, NPQ, ST)
    nc = _NC_CACHE[key]
    Wh = host_prep(params)
    in_maps = []
    for c in range(NCORES):
        s, j = c // 4, c % 4
        m = {n: Wh[n] for n, _ in WEIGHT_SPECS}
        xp = np.zeros((NPQ, T + 256, D), np.float32)
        xp[:, 128:128 + T] = x_prompt[c * NPQ:(c + 1) * NPQ]
        m["xp"] = xp
        m["vones"] = np.ones((1, T), np.float32)
        xsw = np.zeros((7 * Q + 256, D), np.float32)
        tlo = j * Q - 3 * Q - 128
        lo, hi = max(0, tlo), min(ST, tlo + 7 * Q + 256)
        xsw[lo - tlo:hi - tlo] = x_sample[s, lo:hi]
        m["xsw"] = xsw
        vm = np.zeros((1, 7 * Q), np.float32)
        t_first = j * Q - 3 * Q
        lo, hi = max(0, t_first), min(ST, t_first + 7 * Q)
        vm[0, lo - t_first:hi - t_first] = 1.0
        m["vms"] = vm
        m["xsk"] = np.ascontiguousarray(x_sample[s])
        m["qr0s"] = np.full((1, 1), float(j * Q // 64), np.float32)
        m["qr0p"] = np.zeros((1, 1), np.float32)
        in_maps.append(m)
    res = run_bass_kernel_spmd(nc, in_maps, core_ids=list(range(NCORES)))
    y_prompt = np.empty((B, T, D), np.float32)
    y_sample = np.empty((SB, ST, D), np.float32)
    for c in range(NCORES):
        yc = np.asarray(res.results[c]["y"])
        y_prompt[c * NPQ:(c + 1) * NPQ] = yc[:NPQ * T].reshape(NPQ, T, D)
        y_sample[c // 4, (c % 4) * Q:(c % 4 + 1) * Q] = yc[NPQ * T:]
    return (y_prompt, y_sample)
```

```python
import contextlib
import numpy as np
import concourse.bass as bass
import concourse.mybir as mybir
from concourse.bass_utils import run_bass_kernel_spmd

F32 = mybir.dt.float32
BF16 = mybir.dt.bfloat16
AF = mybir.ActivationFunctionType
ALU = mybir.AluOpType
AX = mybir.AxisListType
I32 = mybir.dt.int32

D = 1024
DFF = 2816
NFF = DFF // 128
NK = D // 128
NCORES = 8
TOK_PER_CORE = 4 * 2048 + 4096
GT = 256
NSUB = GT // 128
NORM_EPS = 1e-6
LNX_EPS = 64e-5
HEAD_DIM = 64
ROPE_THETA = 10000.0
ROPE_PAIRS = 16
QPERM = (0, 4, 1, 5, 2, 6, 3, 7)


SAME_ENGINE_SYNC = ('act', 'dve', 'pool')


class Sched:
    ENG = ('pe', 'act', 'dve', 'pool', 'sp')
    NDMA = 12

    def __init__(self, nc, same_engine_sync=SAME_ENGINE_SYNC):
        self.nc = nc
        self.ops = []
        self.last_w = {}
        self.readers = {}
        self.same = set(same_engine_sync)

    def add(self, eng, fn, reads=(), writes=(), dma=False):
        i = len(self.ops)
        deps = set()
        for r in reads:
            j = self.last_w.get(r)
            if j is not None:
                deps.add(j)
        for w in writes:
            j = self.last_w.get(w)
            if j is not None:
                deps.add(j)
            for j in self.readers.get(w, ()):
                deps.add(j)
        for w in writes:
            self.last_w[w] = i
            self.readers[w] = []
        for r in reads:
            if r not in writes:
                self.readers.setdefault(r, []).append(i)
        self.ops.append(dict(eng=eng, fn=fn, deps=deps, dma=dma, needs_inc=False))
        return i

    def emit(self):
        nc = self.nc
        ops = self.ops
        for op in ops:
            for j in op['deps']:
                oj = ops[j]
                if oj['dma']:
                    oj['needs_inc'] = True
                elif oj['eng'] != op['eng'] or op['dma'] or (op['eng'] in self.same):
                    oj['needs_inc'] = True
        with contextlib.ExitStack() as es:
            pool = SemPool.current
            if pool is None:
                pool = SemPool(nc, es)
            sems, dsems, cnt, dma_cnt = pool.sems, pool.dsems, pool.cnt, pool.dcnt
            dma_rr = {e: 0 for e in self.ENG}
            for op in ops:
                if op['dma']:
                    k = dma_rr[op['eng']] % self.NDMA
                    dma_rr[op['eng']] += 1
                    key = (op['eng'], k)
                    prev = dma_cnt.get(key, 0)
                    op['dsem'] = key
                    op['dprev'] = prev
                    dma_cnt[key] = prev + 16
                    op['ticket'] = prev + 16
                elif op['needs_inc']:
                    cnt[op['eng']] += 1
                    op['ticket'] = cnt[op['eng']]
            block = es.enter_context(nc.Block())

            def run(ename):
                def body(eng):
                    waited = {}

                    def wait(sem_key, sem, val):
                        if waited.get(sem_key, 0) >= val:
                            return
                        waited[sem_key] = val
                        eng.wait_ge(sem, val)
                    last_dma = {}
                    for op in ops:
                        if op['eng'] != ename:
                            continue
                        if op['dma'] and op['dprev'] > 0:
                            wait(op['dsem'], dsems[op['dsem']], op['dprev'])
                        for j in sorted(op['deps']):
                            oj = ops[j]
                            if oj['dma']:
                                wait(oj['dsem'], dsems[oj['dsem']], oj['ticket'])
                            elif oj['eng'] != ename or op['dma'] or (ename in self.same):
                                wait(oj['eng'], sems[oj['eng']], oj['ticket'])
                        ins = op['fn'](eng)
                        if op['dma']:
                            ins.then_inc(dsems[op['dsem']], 16)
                            last_dma[op['dsem']] = op['ticket']
                        elif op['needs_inc']:
                            ins.then_inc(sems[ename], 1)
                    for key, t in last_dma.items():
                        wait(key, dsems[key], t)
                return body
            block.tensor(run('pe'))
            block.scalar(run('act'))
            block.vector(run('dve'))
            block.gpsimd(run('pool'))
            block.sync(run('sp'))


class SemPool:
    current = None

    def __init__(self, nc, es):
        self.sems = {e: es.enter_context(nc.semaphore('s_' + e)) for e in Sched.ENG}
        self.dsems = {}
        for e in ('sp', 'pool'):
            for k in range(Sched.NDMA):
                self.dsems[(e, k)] = es.enter_context(nc.semaphore('d_%s_%d' % (e, k)))
        self.cnt = {e: 0 for e in Sched.ENG}
        self.dcnt = {}


STATS = []


class Phase:
    _n = [0]

    def __init__(self, nc, tag):
        self.nc = nc
        Phase._n[0] += 1
        self.tag = "%s%d_" % (tag, Phase._n[0])
        self.es = contextlib.ExitStack()
        self.S = Sched(nc)

    def sb(self, name, shape, dt=F32):
        return self.es.enter_context(self.nc.sbuf_tensor(self.tag + name, shape, dt))

    def ps(self, name, shape, dt=F32):
        return self.es.enter_context(self.nc.psum_tensor(self.tag + name, shape, dt))

    def add(self, *a, **k):
        return self.S.add(*a, **k)

    def close(self):
        self.S.emit()
        self.es.close()
        import collections
        cnt = collections.Counter(o['eng'] + ('_dma' if o['dma'] else '') for o in self.S.ops)
        STATS.append((self.tag, len(self.S.ops), dict(cnt)))


def emit_identity(P, idb, idf):
    P.add('pool', lambda e: e.memset(idf[:], 1.0), writes=['idf'])
    P.add('pool', lambda e: e.affine_select(out=idf[:], in_=idf[:], pattern=[[-1, 128]], compare_op=ALU.is_equal,
                                            fill=0.0, base=0, channel_multiplier=1), writes=['idf'])
    P.add('dve', lambda e: e.tensor_copy(out=idb[:], in_=idf[:]), reads=['idf'], writes=['idb'])


def load_weight_bf16(P, src, dst, nrows_chunks, ncols, stage, dkey, scale_tile=None, col_piece=1024, eng_alt=True):
    n = [0]
    for k in range(nrows_chunks):
        c0 = 0
        while c0 < ncols:
            cn = min(col_piece, ncols - c0)
            i = n[0] % len(stage)
            n[0] += 1
            st = stage[i]
            skey = 'stage%d' % i
            P.add('sp', lambda e, st=st, k=k, c0=c0, cn=cn: e.dma_start(out=st[:, 0:cn],
                                                                       in_=src[k * 128:(k + 1) * 128, c0:c0 + cn]),
                  writes=[skey], dma=True)
            if scale_tile is not None:
                P.add('dve', lambda e, st=st, k=k, c0=c0, cn=cn: e.tensor_scalar(
                    out=dst[:, k, c0:c0 + cn], in0=st[:, 0:cn], scalar1=scale_tile[:, k:k + 1], scalar2=None,
                    op0=ALU.mult), reads=[skey, 'wscale'], writes=[dkey])
            else:
                P.add('act', lambda e, st=st, k=k, c0=c0, cn=cn: e.activation(
                    out=dst[:, k, c0:c0 + cn], in_=st[:, 0:cn], func=AF.Identity), reads=[skey], writes=[dkey])
            c0 += cn


def emit_rstd(P, ss_in, tmp, out, scale, eps, keys_r, keys_w):
    P.add('act', lambda e: e.activation(out=tmp, in_=ss_in, func=AF.Ln, scale=scale, bias=eps),
          reads=keys_r, writes=keys_w)
    P.add('act', lambda e: e.activation(out=out, in_=tmp, func=AF.Exp, scale=-0.5), reads=[], writes=keys_w)


def emit_norm_hT(P, X, xkey, hb, hT_dst, hT_key, pT, pT_key, ss, idb, extra_scale=None):
    P.add('act', lambda e: e.activation(out=hb[:], in_=X, func=AF.Square, accum_out=ss[:, 0:1]),
          reads=[xkey], writes=['hb', 'ss'])
    emit_rstd(P, ss[:, 0:1], ss[:, 1:2], ss[:, 2:3], 1.0 / D, NORM_EPS, [], ['ss'])
    P.add('dve', lambda e: e.tensor_scalar(out=hb[:], in0=X, scalar1=ss[:, 2:3], scalar2=None, op0=ALU.mult),
          reads=[xkey, 'ss'], writes=['hb'])
    for k in range(NK):
        P.add('pe', lambda e, k=k: e.transpose(pT[:, k * 128:(k + 1) * 128], hb[:, k * 128:(k + 1) * 128], idb[:]),
              reads=['hb', 'idb'], writes=[pT_key])
    P.add('dve', lambda e: e.tensor_copy(out=hT_dst, in_=pT[:].rearrange("p (k t) -> p k t", k=NK)),
          writes=[pT_key, hT_key])


def emit_rope_tables(P, Crow, Srow, ntiles, row0_tile, wk, Ccol=None, Scol=None):
    pi_i, pf, inv, rowi, rowf, ang, t0, t1, ti = (wk['pi_i'], wk['pf'], wk['inv'], wk['rowi'], wk['rowf'],
                                                  wk['ang'], wk['t0'], wk['t1'], wk['ti'])
    P.add('pool', lambda e: e.iota(pi_i[:], pattern=[[0, 1]], base=0, channel_multiplier=1), writes=['pi_i'])
    P.add('dve', lambda e: e.tensor_copy(out=pf[:, 0:1], in_=pi_i[:]), reads=['pi_i'], writes=['pf'])
    P.add('dve', lambda e: e.tensor_scalar(out=pf[:, 1:2], in0=pf[:, 0:1], scalar1=64.0, scalar2=None, op0=ALU.is_ge),
          writes=['pf'])
    P.add('dve', lambda e: e.scalar_tensor_tensor(out=pf[:, 2:3], in0=pf[:, 1:2], scalar=-64.0, in1=pf[:, 0:1],
                                                  op0=ALU.mult, op1=ALU.add), writes=['pf'])
    for i in range(ROPE_PAIRS):
        v = float(ROPE_THETA ** (-i / ROPE_PAIRS)) / (2.0 * np.pi)
        P.add('pool', lambda e, i=i, v=v: e.memset(inv[:, i:i + 1], v), writes=['inv'])

    def sin_turns(dst, n, shift, key):
        P.add('dve', lambda e: e.tensor_scalar(out=t0[:, 0:n], in0=ang[:, 0:n], scalar1=shift, scalar2=None, op0=ALU.add),
              reads=['ang'], writes=['t0'])
        P.add('dve', lambda e: e.tensor_copy(out=ti[:, 0:n], in_=t0[:, 0:n]), reads=['t0'], writes=['ti'])
        P.add('dve', lambda e: e.tensor_copy(out=t1[:, 0:n], in_=ti[:, 0:n]), reads=['ti'], writes=['t1'])
        P.add('dve', lambda e: e.tensor_tensor(out=t0[:, 0:n], in0=t0[:, 0:n], in1=t1[:, 0:n], op=ALU.subtract),
              reads=['t1'], writes=['t0'])
        P.add('dve', lambda e: e.tensor_scalar(out=t1[:, 0:n], in0=t0[:, 0:n], scalar1=0.5, scalar2=None, op0=ALU.is_ge),
              reads=['t0'], writes=['t1'])
        P.add('dve', lambda e: e.tensor_tensor(out=t0[:, 0:n], in0=t0[:, 0:n], in1=t1[:, 0:n], op=ALU.subtract),
              reads=['t1'], writes=['t0'])
        P.add('dve', lambda e: e.tensor_scalar(out=t1[:, 0:n], in0=t0[:, 0:n], scalar1=-0.5, scalar2=None, op0=ALU.is_lt),
              reads=['t0'], writes=['t1'])
        P.add('dve', lambda e: e.tensor_tensor(out=t0[:, 0:n], in0=t0[:, 0:n], in1=t1[:, 0:n], op=ALU.add),
              reads=['t1'], writes=['t0'])
        P.add('act', lambda e: e.activation(out=dst, in_=t0[:, 0:n], func=AF.Sin, scale=6.28318),
              reads=['t0'], writes=[key])

    if Ccol is not None:
        P.add('dve', lambda e: e.tensor_scalar(out=ang[:, 0:16], in0=inv[:, 0:16], scalar1=pf[:, 2:3], scalar2=None,
                                               op0=ALU.mult), reads=['pf', 'inv'], writes=['ang'])
        sin_turns(Scol[:, 0:16], 16, 0.0, 'Stab')
        sin_turns(Ccol[:, 0:16], 16, 0.25, 'Ctab')
    CH = 32
    for n0 in range(0, ntiles, CH):
        NT = min(CH, ntiles - n0)
        P.add('pool', lambda e, n0=n0, NT=NT: e.iota(rowi[:, 0:NT], pattern=[[2, NT]], base=2 * n0, channel_multiplier=0),
              writes=['rowi'])
        P.add('dve', lambda e, NT=NT: e.tensor_copy(out=rowf[:, 0:NT], in_=rowi[:, 0:NT]), reads=['rowi'], writes=['rowf'])
        P.add('dve', lambda e, NT=NT: e.tensor_scalar(out=rowf[:, 0:NT], in0=rowf[:, 0:NT], scalar1=pf[:, 1:2], scalar2=None,
                                                      op0=ALU.add), reads=['pf'], writes=['rowf'])
        if row0_tile is not None:
            P.add('dve', lambda e, NT=NT: e.tensor_scalar(out=rowf[:, 0:NT], in0=rowf[:, 0:NT], scalar1=row0_tile,
                                                          scalar2=None, op0=ALU.add), reads=['row0'], writes=['rowf'])
        P.add('dve', lambda e, NT=NT: e.tensor_tensor(
            out=ang[:, 0:NT * 16].rearrange("p (n c) -> p n c", c=16),
            in0=rowf[:, 0:NT].unsqueeze(2).broadcast_to([128, NT, 16]),
            in1=inv[:, 0:16].unsqueeze(1).broadcast_to([128, NT, 16]), op=ALU.mult),
            reads=['rowf', 'inv'], writes=['ang'])
        sin_turns(Srow[:, n0:n0 + NT, :].rearrange("p n c -> p (n c)"), NT * 16, 0.0, 'Stab')
        sin_turns(Crow[:, n0:n0 + NT, :].rearrange("p n c -> p (n c)"), NT * 16, 0.25, 'Ctab')


def emit_qk_norm_rope(P, src_ps, src_key, nheads, gain_b, Cr, Sr, Cc, Sc, scale, out_bf, out_key, wk, tkeys):
    H = nheads
    W = H * 64
    sq, qn, ta, tb, st = wk['sq'], wk['qn'], wk['ta'], wk['tb'], wk['st']
    P.add('act', lambda e: e.activation(out=qn[:, 0:W], in_=src_ps, func=AF.Identity), writes=[src_key, 'qn'])
    P.add('dve', lambda e: e.tensor_tensor(out=sq[:, 0:W], in0=qn[:, 0:W], in1=qn[:, 0:W], op=ALU.mult),
          reads=['qn'], writes=['sq'])
    P.add('dve', lambda e: e.tensor_reduce(out=st[:, 0:H], in_=sq[:, 0:W].rearrange("p (h c) -> p h c", c=64),
                                           axis=AX.X, op=ALU.add), reads=['sq'], writes=['st'])
    emit_rstd(P, st[:, 0:H], st[:, 8:8 + H], st[:, 16:16 + H], 1.0 / 64, NORM_EPS, ['st'], ['st'])
    q3 = qn[:, 0:W].rearrange("p (h c) -> p h c", c=64)
    P.add('dve', lambda e: e.tensor_tensor(out=q3, in0=q3, in1=st[:, 16:16 + H].unsqueeze(2).broadcast_to([128, H, 64]),
                                           op=ALU.mult), reads=['st'], writes=['qn'])
    P.add('dve', lambda e: e.scalar_tensor_tensor(out=q3, in0=q3, scalar=float(scale),
                                                  in1=gain_b.unsqueeze(1).broadcast_to([128, H, 64]),
                                                  op0=ALU.mult, op1=ALU.mult), reads=['gains'], writes=['qn'])
    q5 = qn[:, 0:W].rearrange("p (h a b i) -> p h a b i", a=2, b=2, i=16)
    o5 = out_bf.rearrange("p (h a b i) -> p h a b i", a=2, b=2, i=16)
    A3 = ta[:, 0:H * 16].rearrange("p (h i) -> p h i", i=16)
    B3 = tb[:, 0:H * 16].rearrange("p (h i) -> p h i", i=16)
    for a, (Ct, St) in enumerate(((Cr, Sr), (Cc, Sc))):
        x1, x2 = q5[:, :, a, 0, :], q5[:, :, a, 1, :]
        C3 = Ct.unsqueeze(1).broadcast_to([128, H, 16])
        S3 = St.unsqueeze(1).broadcast_to([128, H, 16])
        P.add('dve', lambda e, x1=x1, C3=C3: e.tensor_tensor(out=A3, in0=x1, in1=C3, op=ALU.mult),
              reads=['qn'] + tkeys, writes=['ta'])
        P.add('pool', lambda e, x2=x2, S3=S3: e.tensor_tensor(out=B3, in0=x2, in1=S3, op=ALU.mult),
              reads=['qn'] + tkeys, writes=['tb'])
        P.add('dve', lambda e, a=a: e.tensor_tensor(out=o5[:, :, a, 0, :], in0=A3, in1=B3, op=ALU.subtract),
              reads=['ta', 'tb'], writes=[out_key])
        P.add('dve', lambda e, x1=x1, S3=S3: e.tensor_tensor(out=A3, in0=x1, in1=S3, op=ALU.mult),
              reads=['qn'] + tkeys, writes=['ta'])
        P.add('pool', lambda e, x2=x2, C3=C3: e.tensor_tensor(out=B3, in0=x2, in1=C3, op=ALU.mult),
              reads=['qn'] + tkeys, writes=['tb'])
        P.add('dve', lambda e, a=a: e.tensor_tensor(out=o5[:, :, a, 1, :], in0=A3, in1=B3, op=ALU.add),
              reads=['ta', 'tb'], writes=[out_key])


def phase_attention(nc, xk, xq, winA, g1c, qg, kg, qrow0, yatt, Tk, Town):
    P = Phase(nc, "att")
    NB = Tk // 128
    NQT = Town // 512
    WA = P.sb("WA", [128, NK, 768], BF16)
    stage = [P.sb("stage%d" % i, [128, 768]) for i in range(2)]
    g1t = P.sb("g1t", [128, NK])
    gq = P.sb("gq", [128, 64]); gk = P.sb("gk", [128, 64])
    r0t = P.sb("r0t", [128, 1])
    negm = P.sb("negm", [128, 4])
    idf = P.sb("idf", [128, 128]); idb = P.sb("idb", [128, 128], BF16)
    CtK = P.sb("CtK", [128, NB, 16]); StK = P.sb("StK", [128, NB, 16])
    NQ128 = Town // 128
    CtQ = P.sb("CtQ", [128, NQ128, 16]); StQ = P.sb("StQ", [128, NQ128, 16])
    Ccol = P.sb("Ccol", [128, 16]); Scol = P.sb("Scol", [128, 16])
    wk = dict(pi_i=P.sb("pi_i", [128, 1], I32), pf=P.sb("pf", [128, 4]), inv=P.sb("inv", [128, 16]),
              rowi=P.sb("rowi", [128, 32], I32), rowf=P.sb("rowf", [128, 32]), ang=P.sb("ang", [128, 512]),
              t0=P.sb("t0", [128, 512]), t1=P.sb("t1", [128, 512]), ti=P.sb("ti", [128, 512], I32),
              sq=P.sb("sq", [128, 512]), qn=P.sb("qn", [128, 512]), ta=P.sb("ta", [128, 256]), tb=P.sb("tb", [128, 256]),
              st=P.sb("st", [128, 24]))
    KT = P.sb("KT", [128, Tk], BF16)
    V3 = P.sb("V3", [128, NB, 192], BF16)
    xt = [P.sb("xt%d" % i, [128, D]) for i in range(2)]
    hb = P.sb("hb", [128, D], BF16)
    ss = P.sb("ss", [128, 4])
    hT = P.sb("hT", [128, NK, 128], BF16)
    ko = P.sb("ko", [128, 128], BF16)
    qo = P.sb("qo", [128, 512], BF16)
    QT = P.sb("QT", [128, 4, 512], BF16)
    PT = [P.sb("PT%d" % i, [128, 512], BF16) for i in range(4)]
    rl = P.sb("rl", [128, 1024])
    Yt = [P.sb("Yt%d" % i, [128, 512], BF16) for i in range(2)]
    pT = P.ps("pT", [128, 1024], BF16)
    pJ = P.ps("pJ", [128, 512])
    pS = [P.ps("pS%d" % i, [128, 512]) for i in range(4)]
    pO = [P.ps("pO%d" % i, [128, 512]) for i in range(2)]

    P.add('sp', lambda e: e.dma_start(out=g1t[:], in_=g1c[:, :]), writes=['wscale'], dma=True)
    P.add('sp', lambda e: e.dma_start(out=gq[:], in_=qg.partition_broadcast(128)), writes=['gains'], dma=True)
    P.add('sp', lambda e: e.dma_start(out=gk[:], in_=kg.partition_broadcast(128)), writes=['gains'], dma=True)
    P.add('sp', lambda e: e.dma_start(out=r0t[:], in_=qrow0.partition_broadcast(128)), writes=['row0'], dma=True)
    emit_identity(P, idb, idf)
    load_weight_bf16(P, winA, WA, NK, 768, stage, 'WA', scale_tile=g1t, col_piece=768)
    emit_rope_tables(P, CtK, StK, NB, None, wk, Ccol, Scol)
    emit_rope_tables(P, CtQ, StQ, NQ128, r0t[:, 0:1], wk)
    P.add('dve', lambda e: e.tensor_reduce(out=negm[:, 0:1], in_=gq[:], axis=AX.X, op=ALU.max, apply_absolute_value=True),
          reads=['gains'], writes=['negm'])
    P.add('dve', lambda e: e.tensor_reduce(out=negm[:, 1:2], in_=gk[:], axis=AX.X, op=ALU.max, apply_absolute_value=True),
          reads=['gains'], writes=['negm'])
    P.add('dve', lambda e: e.scalar_tensor_tensor(out=negm[:, 2:3], in0=negm[:, 0:1], scalar=-8.0 * 1.0001, in1=negm[:, 1:2],
                                                  op0=ALU.mult, op1=ALU.mult), writes=['negm'])
    P.add('pool', lambda e: e.memset(V3[:, :, 64:128], 1.0), writes=['V3'])

    for n in range(NB):
        X = xt[n % 2]
        xkey = 'xt%d' % (n % 2)
        P.add('sp', lambda e, X=X, n=n: e.dma_start(out=X[:], in_=xk[n * 128:(n + 1) * 128, :]), writes=[xkey], dma=True)
        emit_norm_hT(P, X[:], xkey, hb, hT[:], 'hT', pT, 'pT', ss, idb)
        for k in range(NK):
            P.add('pe', lambda e, k=k: e.matmul(pJ[:, 0:256], hT[:, k, :], WA[:, k, 512:768], start=(k == 0), stop=(k == NK - 1)),
                  reads=['hT', 'WA'], writes=['pJ'])
        P.add('act', lambda e, n=n: e.activation(out=V3[:, n, :].rearrange("p (a b) -> p a b", b=64)[:, 0:3:2, :],
                                                 in_=pJ[:, 128:256].rearrange("p (a b) -> p a b", b=64), func=AF.Identity),
              writes=['pJ', 'V3'])
        emit_qk_norm_rope(P, pJ[:, 0:128], 'pJ', 2, gk[:], CtK[:, n, :], StK[:, n, :], Ccol[:], Scol[:], 1.0, ko[:], 'ko', wk, ['Ctab', 'Stab'])
        P.add('pe', lambda e: e.transpose(pT[:, 0:128], ko[:], idb[:]), reads=['ko', 'idb'], writes=['pT'])
        P.add('act', lambda e, n=n: e.activation(out=KT[:, n * 128:(n + 1) * 128], in_=pT[:, 0:128], func=AF.Identity),
              writes=['pT', 'KT'])

    npt = [0]
    for qt in range(NQT):
        for j in range(4):
            n = qt * 4 + j
            X = xt[n % 2]
            xkey = 'xt%d' % (n % 2)
            P.add('sp', lambda e, X=X, n=n: e.dma_start(out=X[:], in_=xq[n * 128:(n + 1) * 128, :]), writes=[xkey], dma=True)
            emit_norm_hT(P, X[:], xkey, hb, hT[:], 'hT', pT, 'pT', ss, idb)
            for k in range(NK):
                P.add('pe', lambda e, k=k: e.matmul(pJ[:, :], hT[:, k, :], WA[:, k, 0:512], start=(k == 0), stop=(k == NK - 1)),
                      reads=['hT', 'WA'], writes=['pJ'])
            emit_qk_norm_rope(P, pJ[:, :], 'pJ', 8, gq[:], CtQ[:, n, :], StQ[:, n, :], Ccol[:], Scol[:], HEAD_DIM ** -0.5,
                              qo[:], 'qo', wk, ['Ctab', 'Stab'])
            for g in range(4):
                P.add('pe', lambda e, g=g: e.transpose(pT[:, g * 128:(g + 1) * 128], qo[:, g * 128:(g + 1) * 128], idb[:]),
                      reads=['qo', 'idb'], writes=['pT'])
            P.add('dve', lambda e, j=j: e.tensor_copy(out=QT[:, :, j * 128:(j + 1) * 128],
                                                      in_=pT[:, 0:512].rearrange("p (g t) -> p g t", g=4)),
                  writes=['pT', 'QT'])
        for g in range(4):
            for n in range(NB):
                ia, ib = npt[0] % 4, (npt[0] + 1) % 4
                npt[0] += 2
                P.add('pe', lambda e, g=g, n=n, ia=ia: e.matmul(pS[ia][:, :], KT[0:64, n * 128:(n + 1) * 128], QT[0:64, g, :],
                                                                start=True, stop=True),
                      reads=['KT', 'QT'], writes=['pS%d' % ia])
                P.add('pe', lambda e, g=g, n=n, ib=ib: e.matmul(pS[ib][:, :], KT[64:128, n * 128:(n + 1) * 128], QT[64:128, g, :],
                                                                start=True, stop=True),
                      reads=['KT', 'QT'], writes=['pS%d' % ib])
                P.add('act', lambda e, ia=ia: e.activation(out=PT[ia][:], in_=pS[ia][:], func=AF.Exp, bias=negm[:, 2:3], scale=1.0),
                      reads=['negm'], writes=['pS%d' % ia, 'PT%d' % ia])
                P.add('act', lambda e, ib=ib: e.activation(out=PT[ib][:], in_=pS[ib][:], func=AF.Exp, bias=negm[:, 2:3], scale=1.0),
                      reads=['negm'], writes=['pS%d' % ib, 'PT%d' % ib])
                P.add('pe', lambda e, n=n, ia=ia: e.matmul(pO[0][:, :], V3[:, n, 0:128], PT[ia][:], start=(n == 0), stop=(n == NB - 1)),
                      reads=['V3', 'PT%d' % ia], writes=['pO0'])
                P.add('pe', lambda e, n=n, ib=ib: e.matmul(pO[1][:, :], V3[:, n, 64:192], PT[ib][:], start=(n == 0), stop=(n == NB - 1)),
                      reads=['V3', 'PT%d' % ib], writes=['pO1'])
            Y = Yt[g % 2]
            ykey = 'Yt%d' % (g % 2)
            P.add('dve', lambda e: e.reciprocal(out=rl[64:128, 0:512], in_=pO[0][64:128, :]), writes=['pO0', 'rlA'])
            P.add('dve', lambda e: e.reciprocal(out=rl[0:64, 512:1024], in_=pO[1][0:64, :]), writes=['pO1', 'rlB'])
            P.add('dve', lambda e, Y=Y: e.tensor_tensor(out=Y[0:64, :], in0=pO[0][0:64, :], in1=rl[64:128, 0:512], op=ALU.mult),
                  reads=['rlA'], writes=['pO0', ykey])
            P.add('dve', lambda e, Y=Y: e.tensor_tensor(out=Y[64:128, :], in0=pO[1][64:128, :], in1=rl[0:64, 512:1024], op=ALU.mult),
                  reads=['rlB'], writes=['pO1', ykey])
            P.add('pool', lambda e, Y=Y, g=g, qt=qt: e.dma_start(out=yatt[g, :, qt * 512:(qt + 1) * 512], in_=Y[:]),
                  reads=[ykey], dma=True)
    P.close()


def emit_norm_hT_n(P, X, xkey, np_, hb, hT_dst, hT_key, pT, pT_key, ss, idb):
    P.add('act', lambda e: e.activation(out=hb[0:np_, :], in_=X, func=AF.Square, accum_out=ss[0:np_, 0:1]),
          reads=[xkey], writes=['hb', 'ss'])
    emit_rstd(P, ss[0:np_, 0:1], ss[0:np_, 1:2], ss[0:np_, 2:3], 1.0 / D, NORM_EPS, [], ['ss'])
    P.add('dve', lambda e: e.tensor_scalar(out=hb[0:np_, :], in0=X, scalar1=ss[0:np_, 2:3], scalar2=None, op0=ALU.mult),
          reads=[xkey, 'ss'], writes=['hb'])
    for k in range(NK):
        P.add('pe', lambda e, k=k: e.transpose(pT[:, k * np_:(k + 1) * np_], hb[0:np_, k * 128:(k + 1) * 128],
                                               idb[0:np_, 0:np_]),
              reads=['hb', 'idb'], writes=[pT_key])
    P.add('dve', lambda e: e.tensor_copy(out=hT_dst, in_=pT[:, 0:NK * np_].rearrange("p (k t) -> p k t", k=NK)),
          writes=[pT_key, hT_key])


DECAY_C = float(np.exp(-0.5))


def phase_rwkv(nc, fwd, xw, vmask, Tw, own0, own1, WRd, nch, mupc, munc, g1c, Wld, w0c, a0c, kkc, kac, rkc,
               y_out, s_out, Gld=None, g_out=None, v_out=None):
    P = Phase(nc, "rwf" if fwd else "rwb")
    NSC = Tw // 512
    CL, CG = 12, 13
    WR = P.sb("WR", [128, NK, nch * 128], BF16)
    Wl = P.sb("Wl", [128, 512], BF16)
    Gl = P.sb("Gl", [128, 512], BF16) if fwd else None
    g1t = P.sb("g1t", [128, NK])
    mp = P.sb("mp", [128, 16]); mn = P.sb("mn", [128, 16]); c0 = P.sb("c0", [128, 16])
    cv = P.sb("cv", [128, 20])
    idf = P.sb("idf", [128, 128]); idb = P.sb("idb", [128, 128], BF16)
    ones_bd = P.sb("ones_bd", [128, 128])
    E2 = P.sb("E2", [128, 2], BF16)
    Ms = P.sb("Ms", [128, 128], BF16); Mi = P.sb("Mi", [128, 128], BF16); Mt = P.sb("Mt", [128, 128], BF16)
    rmask = P.sb("rmask", [128, 512])
    vmt = P.sb("vmt", [128, 512])
    xt = [P.sb("xt%d" % i, [128, D]) for i in range(4)]
    xh = P.sb("xh", [2, D])
    hb = P.sb("hb", [128, D], BF16)
    ss = P.sb("ss", [128, 4])
    hTw = P.sb("hTw", [128, NK, 514], BF16)
    tl = P.sb("tl", [128, 32])
    T = {n: P.sb(n, [128, 512]) for n in ("zr", "zk", "zv", "zL", "sw", "aa", "kkr", "sq", "rn", "kd", "kka", "cl", "ex",
                                          "rem", "remx", "ea", "eb")}
    LW = P.sb("LW", [128, 512], BF16)
    sg = P.sb("sg", [128, 512], BF16) if fwd else None
    ARt = [P.sb("ARt%d" % c, [128, 4, 256], BF16) for c in range(4)]
    rk = [P.sb("rk%d" % c, [128, 512], BF16) for c in range(4)]
    wc = P.sb("wc", [128, 4, 4])
    Bt2 = [P.sb("Bt%d" % i, [128, 512], BF16) for i in range(2)]
    Kt2 = [P.sb("Kt%d" % i, [128, 512], BF16) for i in range(2)]
    Bh = P.sb("Bh", [128, 512], BF16); Kh = P.sb("Kh", [128, 512], BF16); vb = P.sb("vb", [128, 512], BF16)
    tok = P.sb("tok", [128, 4, 4, 384], BF16)
    SC_LM = P.sb("SC_LM", [128, 32, 256], BF16)
    SC_MP = P.sb("SC_MP", [128, 32, 256], BF16)
    QP = [P.sb("QP%d" % i, [128, 4, 256], BF16) for i in range(2)]
    QTt = [P.sb("QT%d" % i, [128, 4, 128], BF16) for i in range(2)]
    S32 = P.sb("S32", [128, 4, 128])
    Sbf = P.sb("Sbf", [128, 4, 128], BF16)
    RHSb = P.sb("RHSb", [128, 4, 128], BF16)
    Ub = P.sb("Ub", [128, 4, 128], BF16)
    yt = P.sb("yt", [128, 512])
    st8 = P.sb("st8", [128, 8])
    gt = P.sb("gt", [128, 512]) if fwd else None
    pm = [P.ps("pm%d" % i, [128, 512]) for i in range(2)]
    pc2 = P.ps("pc2", [128, 512])
    pT = P.ps("pT", [128, 1024], BF16)
    pc = [None] * 4 + [P.ps("pc%d" % i, [128, 512]) for i in range(4, 8)]

    P.add('sp', lambda e: e.dma_start(out=g1t[:], in_=g1c[:, :]), writes=['wscale'], dma=True)
    P.add('sp', lambda e: e.dma_start(out=mp[:, 0:nch], in_=mupc[:, :]), writes=['mu'], dma=True)
    P.add('sp', lambda e: e.dma_start(out=mn[:, 0:nch], in_=munc[:, :]), writes=['mu'], dma=True)
    for i, src in enumerate((w0c, a0c, kkc, kac, rkc)):
        P.add('sp', lambda e, i=i, src=src: e.dma_start(out=cv[:, 4 * i:4 * i + 4], in_=src[:, :]), writes=['cv'], dma=True)
    P.add('dve', lambda e: e.tensor_tensor(out=c0[:, 0:nch], in0=mp[:, 0:nch], in1=mn[:, 0:nch], op=ALU.add),
          reads=['mu'], writes=['c0'])
    P.add('dve', lambda e: e.tensor_scalar(out=c0[:, 0:nch], in0=c0[:, 0:nch], scalar1=-1.0, scalar2=1.0, op0=ALU.mult,
                                           op1=ALU.add), writes=['c0'])
    emit_identity(P, idb, idf)
    P.add('pool', lambda e: e.memset(ones_bd[:], 0.0), writes=['ones_bd'])
    P.add('pool', lambda e: e.memset(ones_bd[0:64, 0:64], 1.0), writes=['ones_bd'])
    P.add('pool', lambda e: e.memset(ones_bd[64:128, 64:128], 1.0), writes=['ones_bd'])
    P.add('pool', lambda e: e.memset(E2[:], 0.0), writes=['E2'])
    P.add('pool', lambda e: e.memset(E2[0:64, 0:1], 1.0), writes=['E2'])
    P.add('pool', lambda e: e.memset(E2[64:128, 1:2], 1.0), writes=['E2'])
    for M, strict, transposed in ((Ms, True, False), (Mi, False, False), (Mt, True, True)):
        key = 'masks'
        P.add('pool', lambda e, M=M: e.memset(idf[:], 1.0), writes=['idf'])
        sgn = 1 if (fwd != transposed) else -1
        P.add('pool', lambda e, M=M, sgn=sgn, strict=strict: e.affine_select(
            out=idf[:], in_=idf[:], pattern=[[sgn, 128]], compare_op=(ALU.is_gt if strict else ALU.is_ge), fill=0.0, base=0,
            channel_multiplier=-sgn), writes=['idf'])
        P.add('dve', lambda e, M=M: e.tensor_copy(out=M[:], in_=idf[:]), reads=['idf'], writes=[key])
    P.add('pool', lambda e: e.memset(rmask[:], 1.0), writes=['rmask'])
    P.add('pool', lambda e: e.memset(rmask[:].rearrange("p (j t) -> p j t", t=128)[:, :, 0:1], 0.0), writes=['rmask'])
    for c in range(4):
        P.add('pool', lambda e, c=c: e.memset(S32[:, c, :], 0.0), writes=['S32_%d' % c])
        P.add('pool', lambda e, c=c: e.memset(Sbf[:, c, :], 0.0), writes=['Sbf%d' % c])
    stage = [T["zr"], T["zk"]]
    load_weight_bf16(P, WRd, WR, NK, nch * 128, stage, 'WR', scale_tile=g1t, col_piece=512)
    P.add('sp', lambda e: e.dma_start(out=T["zv"][:], in_=Wld[:, :]), writes=['zv'], dma=True)
    P.add('act', lambda e: e.activation(out=Wl[:], in_=T["zv"][:], func=AF.Identity), reads=['zv'], writes=['Wl'])
    if fwd:
        P.add('sp', lambda e: e.dma_start(out=T["zL"][:], in_=Gld[:, :]), writes=['zL'], dma=True)
        P.add('act', lambda e: e.activation(out=Gl[:], in_=T["zL"][:], func=AF.Identity), reads=['zL'], writes=['Gl'])

    npm = [0]

    def inproj_shift(ci, dst, dkey):
        b = npm[0] % 2
        npm[0] += 1
        pmb, pk = pm[b], 'pm%d' % b
        for k in range(NK):
            P.add('pe', lambda e, k=k: e.matmul(pmb[:, :], WR[:, k, ci * 128:(ci + 1) * 128], hTw[:, k, 0:512],
                                                start=(k == 0), stop=(k == NK - 1)), reads=['WR', 'hTw'], writes=[pk])
        P.add('act', lambda e: e.activation(out=dst[:], in_=pmb[:], func=AF.Identity, scale=c0[:, ci:ci + 1]),
              reads=['c0'], writes=[pk, dkey])
        P.add('dve', lambda e: e.scalar_tensor_tensor(out=dst[:, 1:512], in0=pmb[:, 0:511], scalar=mp[:, ci:ci + 1],
                                                      in1=dst[:, 1:512], op0=ALU.mult, op1=ALU.add),
              reads=['mu'], writes=[pk, dkey])
        P.add('dve', lambda e: e.scalar_tensor_tensor(out=dst[:, 0:511], in0=pmb[:, 1:512], scalar=mn[:, ci:ci + 1],
                                                      in1=dst[:, 0:511], op0=ALU.mult, op1=ALU.add),
              reads=['mu'], writes=[pk, dkey])
        P.add('dve', lambda e: e.scalar_tensor_tensor(out=dst[:, 0:1], in0=tl[:, 2 * ci:2 * ci + 1], scalar=mp[:, ci:ci + 1],
                                                      in1=dst[:, 0:1], op0=ALU.mult, op1=ALU.add),
              reads=['mu', 'tl'], writes=[dkey])
        P.add('dve', lambda e: e.scalar_tensor_tensor(out=dst[:, 511:512], in0=tl[:, 2 * ci + 1:2 * ci + 2],
                                                      scalar=mn[:, ci:ci + 1], in1=dst[:, 511:512], op0=ALU.mult, op1=ALU.add),
              reads=['mu', 'tl'], writes=[dkey])

    order = list(range(NSC)) if fwd else list(range(NSC - 1, -1, -1))
    chunk_order = [0, 1, 2, 3] if fwd else [3, 2, 1, 0]
    def superchunk(sc):
        t0 = sc * 512
        own = (own0 <= t0 < own1)
        for j in range(4):
            r0 = 128 + t0 + j * 128
            P.add('sp', lambda e, j=j, r0=r0: e.dma_start(out=xt[j][:], in_=xw[r0:r0 + 128, :]), writes=['xt%d' % j], dma=True)
        P.add('sp', lambda e: e.dma_start(out=xh[0:1, :], in_=xw[127 + t0:128 + t0, :]), writes=['xh'], dma=True)
        P.add('sp', lambda e: e.dma_start(out=xh[1:2, :], in_=xw[128 + t0 + 512:129 + t0 + 512, :]), writes=['xh'], dma=True)
        P.add('sp', lambda e: e.dma_start(out=vmt[:], in_=vmask[0:1, t0:t0 + 512].partition_broadcast(128)),
              writes=['vmt'], dma=True)
        for j in range(4):
            emit_norm_hT_n(P, xt[j][:], 'xt%d' % j, 128, hb, hTw[:, :, j * 128:(j + 1) * 128], 'hTw', pT, 'pT', ss, idb)
        emit_norm_hT_n(P, xh[0:2, :], 'xh', 2, hb, hTw[:, :, 512:514], 'hTw', pT, 'pT', ss, idb)
        chunks = list(range(13)) + ([CG] if (fwd and own) else [])
        for ci in chunks:
            for k in range(NK):
                P.add('pe', lambda e, k=k, ci=ci: e.matmul(pc2[:, 2 * ci:2 * ci + 2], WR[:, k, ci * 128:(ci + 1) * 128],
                                                           hTw[:, k, 512:514], start=(k == 0), stop=(k == NK - 1)),
                      reads=['WR', 'hTw'], writes=['pc2'])
        P.add('dve', lambda e: e.tensor_copy(out=tl[:, 0:28], in_=pc2[:, 0:28]), writes=['pc2', 'tl'])
        inproj_shift(CL, T["zL"], 'zL')
        P.add('act', lambda e: e.activation(out=LW[0:64, :], in_=T["zL"][0:64, :], func=AF.Tanh), reads=['zL'], writes=['LW'])
        P.add('act', lambda e: e.activation(out=LW[64:128, :], in_=T["zL"][64:128, :], func=AF.Identity), reads=['zL'],
              writes=['LW'])
        if fwd and own:
            inproj_shift(CG, T["zL"], 'zL')
            P.add('act', lambda e: e.activation(out=sg[:], in_=T["zL"][:], func=AF.Sigmoid), reads=['zL'], writes=['sg'])
        def pair(c):
            zr, zk, zv = T["zr"], T["zk"], T["zv"]
            Bt, Kt, btk, ktk = Bt2[c % 2], Kt2[c % 2], 'Bt%d' % (c % 2), 'Kt%d' % (c % 2)
            inproj_shift(c, zr, 'zr')
            yield
            inproj_shift(4 + c, zk, 'zk')
            yield
            inproj_shift(8 + c, zv, 'zv')
            yield
            cs = slice(c * 128, (c + 1) * 128)
            P.add('pe', lambda e, cs=cs: e.matmul(pm[0][:, :], Wl[0:64, cs], LW[0:64, :], start=True, stop=True),
                  reads=['Wl', 'LW'], writes=['pm0'])
            P.add('pe', lambda e, cs=cs: e.matmul(pm[1][:, :], Wl[64:128, cs], LW[64:128, :], start=True, stop=True),
                  reads=['Wl', 'LW'], writes=['pm1'])
            yield
            sw, aa, kkr, sq, rn, kd, kka, cl, ex, rem, remx, ea, eb = (T[n] for n in (
                "sw", "aa", "kkr", "sq", "rn", "kd", "kka", "cl", "ex", "rem", "remx", "ea", "eb"))
            P.add('act', lambda e, c=c: e.activation(out=sw[:], in_=pm[0][:], func=AF.Sigmoid, bias=cv[:, c:c + 1], scale=1.0),
                  reads=['cv'], writes=['pm0', 'sw'])
            P.add('act', lambda e, c=c: e.activation(out=aa[:], in_=pm[1][:], func=AF.Sigmoid, bias=cv[:, 4 + c:5 + c], scale=1.0),
                  reads=['cv'], writes=['pm1', 'aa'])
            yield
            P.add('pool', lambda e, c=c: e.tensor_scalar(out=kkr[:], in0=zk[:], scalar1=cv[:, 8 + c:9 + c], scalar2=None,
                                                         op0=ALU.mult), reads=['zk', 'cv'], writes=['kkr'])
            P.add('pool', lambda e: e.tensor_tensor(out=sq[:], in0=kkr[:], in1=kkr[:], op=ALU.mult), reads=['kkr'], writes=['sq'])
            P.add('pe', lambda e: e.matmul(pm[0][:, :], ones_bd[:], sq[:], start=True, stop=True),
                  reads=['ones_bd', 'sq'], writes=['pm0'])
            yield
            P.add('act', lambda e: e.activation(out=rn[:], in_=pm[0][:], func=AF.Ln, bias=1e-12, scale=1.0),
                  writes=['pm0', 'rn'])
            P.add('act', lambda e: e.activation(out=rn[:], in_=rn[:], func=AF.Exp, scale=-0.5), writes=['rn'])
            P.add('dve', lambda e: e.tensor_tensor(out=kkr[:], in0=kkr[:], in1=rn[:], op=ALU.mult), reads=['rn'], writes=['kkr'])
            yield
            P.add('dve', lambda e, c=c: e.tensor_scalar(out=kd[:], in0=aa[:], scalar1=-1.0, scalar2=cv[:, 12 + c:13 + c],
                                                        op0=ALU.add, op1=ALU.mult), reads=['aa', 'cv'], writes=['kd'])
            P.add('dve', lambda e: e.scalar_tensor_tensor(out=kd[:], in0=kd[:], scalar=1.0, in1=zk[:], op0=ALU.add,
                                                          op1=ALU.mult), reads=['zk'], writes=['kd'])
            P.add('pool', lambda e: e.tensor_tensor(out=kka[:], in0=kkr[:], in1=aa[:], op=ALU.mult),
                  reads=['kkr', 'aa'], writes=['kka'])
            yield
            P.add('dve', lambda e: e.tensor_tensor_scan(out=cl[:], data0=rmask[:], data1=sw[:], initial=0.0, op0=ALU.mult,
                                                        op1=ALU.add), reads=['rmask', 'sw'], writes=['cl'])
            cl3 = cl[:].rearrange("p (j t) -> p j t", t=128)
            totb = cl3[:, :, 127:128].broadcast_to([128, 4, 128])
            P.add('pool', lambda e: e.tensor_tensor(out=ex[:], in0=cl[:], in1=sw[:], op=ALU.subtract),
                  reads=['cl', 'sw'], writes=['ex'])
            P.add('dve', lambda e: e.tensor_tensor(out=rem[:].rearrange("p (j t) -> p j t", t=128), in0=totb, in1=cl3,
                                                   op=ALU.subtract), reads=['cl'], writes=['rem'])
            if fwd:
                uA, uR, uB, uH = ex, cl, cl, rem
                kA, kR, kB, kH = 'ex', 'cl', 'cl', 'rem'
            else:
                P.add('dve', lambda e: e.tensor_tensor(out=remx[:].rearrange("p (j t) -> p j t", t=128), in0=totb,
                                                       in1=ex[:].rearrange("p (j t) -> p j t", t=128), op=ALU.subtract),
                      reads=['cl', 'ex'], writes=['remx'])
                uA, uR, uB, uH = rem, remx, remx, ex
                kA, kR, kB, kH = 'rem', 'remx', 'remx', 'ex'
            P.add('act', lambda e, c=c: e.activation(out=wc[:, c, :], in_=cl3[:, :, 127], func=AF.Exp, scale=-DECAY_C),
                  reads=['cl'], writes=['wc'])
            yield
            AR = ARt[c]
            arkey = 'ARt%d' % c
            P.add('act', lambda e: e.activation(out=ea[:], in_=uA[:], func=AF.Exp, scale=-DECAY_C), reads=[kA], writes=['ea'])
            P.add('dve', lambda e: e.scalar_tensor_tensor(out=AR[:, :, 0:128], in0=kkr[:].rearrange("p (j t) -> p j t", t=128),
                                                          scalar=-1.0, in1=ea[:].rearrange("p (j t) -> p j t", t=128),
                                                          op0=ALU.mult, op1=ALU.mult), reads=['kkr', 'ea'], writes=[arkey])
            P.add('act', lambda e: e.activation(out=eb[:], in_=uR[:], func=AF.Exp, scale=-DECAY_C), reads=[kR], writes=['eb'])
            P.add('pool', lambda e: e.tensor_tensor(out=AR[:, :, 128:256], in0=zr[:].rearrange("p (j t) -> p j t", t=128),
                                                    in1=eb[:].rearrange("p (j t) -> p j t", t=128), op=ALU.mult),
                  reads=['zr', 'eb'], writes=[arkey])
            yield
            P.add('act', lambda e: e.activation(out=ea[:], in_=uB[:], func=AF.Exp, scale=DECAY_C), reads=[kB], writes=['ea'])
            P.add('pool', lambda e: e.tensor_tensor(out=Bt[:], in0=kka[:], in1=ea[:], op=ALU.mult), reads=['kka', 'ea'], writes=[btk])
            P.add('dve', lambda e: e.tensor_tensor(out=Kt[:], in0=kd[:], in1=ea[:], op=ALU.mult), reads=['kd', 'ea'], writes=[ktk])
            P.add('act', lambda e: e.activation(out=eb[:], in_=uH[:], func=AF.Exp, scale=-DECAY_C), reads=[kH], writes=['eb'])
            P.add('pool', lambda e: e.tensor_tensor(out=Bh[:], in0=kka[:], in1=eb[:], op=ALU.mult), reads=['kka', 'eb'], writes=['Bh'])
            P.add('dve', lambda e: e.tensor_tensor(out=Kh[:], in0=kd[:], in1=eb[:], op=ALU.mult), reads=['kd', 'eb'], writes=['Kh'])
            yield
            P.add('pool', lambda e: e.tensor_tensor(out=vb[:], in0=zv[:], in1=vmt[:], op=ALU.mult), reads=['zv', 'vmt'], writes=['vb'])
            if own:
                P.add('dve', lambda e, c=c: e.scalar_tensor_tensor(out=rk[c][:], in0=zr[:], scalar=cv[:, 16 + c:17 + c],
                                                                   in1=kd[:], op0=ALU.mult, op1=ALU.mult),
                      reads=['zr', 'kd', 'cv'], writes=['rk%d' % c])
            for j in range(4):
                js = slice(j * 128, (j + 1) * 128)
                for q, (src, sk) in enumerate(((vb, 'vb'), (Bh, 'Bh'), (Kh, 'Kh'))):
                    P.add('pe', lambda e, q=q, src=src, js=js: e.transpose(pT[:, q * 128:(q + 1) * 128], src[:, js], idb[:]),
                          reads=[sk, 'idb'], writes=['pT'])
                P.add('dve', lambda e, c=c, j=j: e.tensor_copy(out=tok[:, c, j, :], in_=pT[:, 0:384]),
                      writes=['pT', 'tok%d' % c])
                yield
            yield
        def head(c, hh):
            Bt, Kt, btk, ktk = Bt2[c % 2], Kt2[c % 2], 'Bt%d' % (c % 2), 'Kt%d' % (c % 2)
            AR = ARt[c]
            arkey = 'ARt%d' % c
            rows = slice(hh * 64, (hh + 1) * 64)
            u0 = (c * 2 + hh) * 4
            for j in range(4):
                js = slice(j * 128, (j + 1) * 128)
                P.add('pe', lambda e, j=j, js=js: e.matmul(pc[4][:, js], Bt[rows, js], AR[rows, j, 0:128], start=True, stop=True),
                      reads=[btk, arkey], writes=['pc4'])
                P.add('pe', lambda e, j=j, js=js: e.matmul(pc[5][:, js], Bt[rows, js], AR[rows, j, 128:256], start=True, stop=True),
                      reads=[btk, arkey], writes=['pc5'])
                P.add('pe', lambda e, j=j, js=js: e.matmul(pc[6 + j // 2][:, (j % 2) * 256:(j % 2) * 256 + 256], Kt[rows, js],
                                                           AR[rows, j, :], start=True, stop=True),
                      reads=[ktk, arkey], writes=['pc%d' % (6 + j // 2)])
                P.add('pe', lambda e, j=j, js=js: e.matmul(pc2[:, js], AR[rows, j, 0:128], Bt[rows, js], start=True, stop=True),
                      reads=[btk, arkey], writes=['pc2'])
            yield
            Msb = Ms[:].unsqueeze(1).broadcast_to([128, 4, 128])
            Mib = Mi[:].unsqueeze(1).broadcast_to([128, 4, 128])
            Mtb = Mt[:].unsqueeze(1).broadcast_to([128, 4, 128])
            P.add('dve', lambda e: e.tensor_tensor(out=QP[0][:, :, 0:128], in0=pc[4][:].rearrange("p (u t) -> p u t", t=128),
                                                   in1=Msb, op=ALU.mult), reads=['masks'], writes=['pc4', 'QP0'])
            P.add('dve', lambda e, u0=u0: e.tensor_tensor(out=SC_MP[:, u0:u0 + 4, 0:128],
                                                          in0=pc[5][:].rearrange("p (u t) -> p u t", t=128), in1=Mib, op=ALU.mult),
                  reads=['masks'], writes=['pc5', 'SC_MP'])
            for half in range(2):
                M2s = Ms[:].unsqueeze(1).broadcast_to([128, 2, 128])
                M2i = Mi[:].unsqueeze(1).broadcast_to([128, 2, 128])
                pcb = pc[6 + half]
                P.add('dve', lambda e, u0=u0, half=half, pcb=pcb, M2s=M2s: e.tensor_tensor(
                    out=SC_LM[:, u0 + 2 * half:u0 + 2 * half + 2, 0:128],
                    in0=pcb[:].rearrange("p (u t) -> p u t", t=256)[:, :, 0:128], in1=M2s, op=ALU.mult),
                    reads=['masks'], writes=['pc%d' % (6 + half), 'SC_LM'])
                P.add('dve', lambda e, u0=u0, half=half, pcb=pcb, M2i=M2i: e.tensor_tensor(
                    out=SC_LM[:, u0 + 2 * half:u0 + 2 * half + 2, 128:256],
                    in0=pcb[:].rearrange("p (u t) -> p u t", t=256)[:, :, 128:256], in1=M2i, op=ALU.mult),
                    reads=['masks'], writes=['pc%d' % (6 + half), 'SC_LM'])
            P.add('dve', lambda e: e.tensor_tensor(out=QTt[0][:], in0=pc2[:].rearrange("p (u t) -> p u t", t=128), in1=Mtb,
                                                   op=ALU.mult), reads=['masks'], writes=['pc2', 'QT0'])
            yield
            P.add('pool', lambda e: e.tensor_tensor(out=QP[0][:, :, 128:256], in0=QP[0][:, :, 0:128],
                                                    in1=idb[:].unsqueeze(1).broadcast_to([128, 4, 128]), op=ALU.add),
                  reads=['idb'], writes=['QP0'])
            for u in range(4):
                P.add('pe', lambda e, u=u: e.matmul(pc[6 + u // 2][:, (u % 2) * 256:(u % 2) * 256 + 128], QTt[0][:, u, :],
                                                    QP[0][:, u, 0:128], start=True, stop=True),
                      reads=['QT0', 'QP0'], writes=['pc%d' % (6 + u // 2)])
                P.add('pe', lambda e, u=u: e.matmul(pc2[:, u * 128:(u + 1) * 128], QP[0][:, u, 0:128], QTt[0][:, u, :],
                                                    start=True, stop=True), reads=['QT0', 'QP0'], writes=['pc2'])
            yield
            for half in range(2):
                P.add('act', lambda e, half=half: e.activation(
                    out=QP[1][:, 2 * half:2 * half + 2, 0:128],
                    in_=pc[6 + half][:].rearrange("p (u t) -> p u t", t=256)[:, :, 0:128], func=AF.Identity),
                    writes=['pc%d' % (6 + half), 'QP1'])
            P.add('pool', lambda e: e.tensor_copy(out=QP[1][:, :, 128:256], in_=QP[0][:, :, 128:256]), reads=['QP0'], writes=['QP1'])
            P.add('act', lambda e: e.activation(out=QTt[1][:], in_=pc2[:].rearrange("p (u t) -> p u t", t=128), func=AF.Identity),
                  writes=['pc2', 'QT1'])
            yield
            cur = 1
            for lvl in range(1, 7):
                nxt = 1 - cur
                ck, nk = 'QP%d' % cur, 'QP%d' % nxt
                ctk, ntk = 'QT%d' % cur, 'QT%d' % nxt
                last = (lvl == 6)
                for u in range(4):
                    if not last:
                        P.add('pe', lambda e, u=u, cur=cur: e.matmul(pc[6 + u // 2][:, (u % 2) * 256:(u % 2) * 256 + 256],
                                                                    QTt[cur][:, u, :], QP[cur][:, u, :], start=True, stop=True),
                              reads=[ck, ctk], writes=['pc%d' % (6 + u // 2)])
                        P.add('pe', lambda e, u=u, cur=cur: e.matmul(pc2[:, u * 128:(u + 1) * 128], QP[cur][:, u, 0:128],
                                                                    QTt[cur][:, u, :], start=True, stop=True),
                              reads=[ck, ctk], writes=['pc2'])
                    else:
                        P.add('pe', lambda e, u=u, cur=cur: e.matmul(pc[6 + u // 2][:, (u % 2) * 256 + 128:(u % 2) * 256 + 256],
                                                                    QTt[cur][:, u, :], QP[cur][:, u, 128:256], start=True, stop=True),
                              reads=[ck, ctk], writes=['pc%d' % (6 + u // 2)])
                yield
                for half in range(2):
                    pv = pc[6 + half][:].rearrange("p (u t) -> p u t", t=256)
                    us = slice(2 * half, 2 * half + 2)
                    if not last:
                        P.add('act', lambda e, pv=pv, us=us, nxt=nxt: e.activation(out=QP[nxt][:, us, 0:128], in_=pv[:, :, 0:128],
                                                                                func=AF.Identity),
                              writes=['pc%d' % (6 + half), nk])
                        P.add('dve', lambda e, pv=pv, us=us, nxt=nxt, cur=cur: e.tensor_tensor(
                            out=QP[nxt][:, us, 128:256], in0=pv[:, :, 128:256], in1=QP[cur][:, us, 128:256], op=ALU.add),
                            reads=[ck], writes=['pc%d' % (6 + half), nk])
                    else:
                        P.add('dve', lambda e, pv=pv, us=us, cur=cur, u0=u0, half=half: e.tensor_tensor(
                            out=SC_MP[:, u0 + 2 * half:u0 + 2 * half + 2, 128:256], in0=pv[:, :, 128:256],
                            in1=QP[cur][:, us, 128:256], op=ALU.add),
                            reads=[ck], writes=['pc%d' % (6 + half), 'SC_MP'])
                if not last:
                    P.add('act', lambda e, nxt=nxt: e.activation(out=QTt[nxt][:], in_=pc2[:].rearrange("p (u t) -> p u t", t=128),
                                                                 func=AF.Identity), writes=['pc2', ntk])
                cur = nxt
                yield

        def run_interleaved(gens):
            gens = list(gens)
            while gens:
                for g_ in list(gens):
                    try:
                        next(g_)
                    except StopIteration:
                        gens.remove(g_)

        def chain_b(c):
            yield from head(c, 0)
            yield from head(c, 1)
        run_interleaved([pair(0)])
        for c in range(1, 4):
            run_interleaved([pair(c), chain_b(c - 1)])
        run_interleaved([chain_b(3)])
        for j in chunk_order:
            js = slice(j * 128, (j + 1) * 128)
            for c in range(4):
                pcc, pk = pc[4 + c], 'pc%d' % (4 + c)
                P.add('pe', lambda e, c=c, j=j, pcc=pcc: e.matmul(pcc[:, 0:128], ARt[c][:, j, 0:128], Sbf[:, c, :], start=True, stop=False),
                      reads=['ARt%d' % c, 'Sbf%d' % c], writes=[pk])
                for hh in range(2):
                    u = (c * 2 + hh) * 4 + j
                    hs = slice(hh * 64, (hh + 1) * 64)
                    P.add('pe', lambda e, c=c, j=j, u=u, hs=hs, hh=hh, pcc=pcc: e.matmul(pcc[:, hs], SC_LM[:, u, 0:128], tok[:, c, j, hs],
                                                                                 start=False, stop=(hh == 1)),
                          reads=['SC_LM', 'tok%d' % c], writes=[pk])
                P.add('act', lambda e, c=c, pcc=pcc: e.activation(out=RHSb[:, c, :], in_=pcc[:, 0:128], func=AF.Identity),
                      writes=[pk, 'RHSb%d' % c])
            for c in range(4):
                pcc, pk = pc[4 + c], 'pc%d' % (4 + c)
                for hh in range(2):
                    u = (c * 2 + hh) * 4 + j
                    hs = slice(hh * 64, (hh + 1) * 64)
                    P.add('pe', lambda e, c=c, u=u, hs=hs, hh=hh, pcc=pcc: e.matmul(pcc[:, 128 + hh * 64:192 + hh * 64], SC_MP[:, u, 128:256],
                                                                            RHSb[:, c, hs], start=True, stop=True),
                          reads=['SC_MP', 'RHSb%d' % c], writes=[pk])
                P.add('dve', lambda e, c=c, pcc=pcc: e.tensor_copy(out=Ub[:, c, :], in_=pcc[:, 128:256]), writes=[pk, 'Ub%d' % c])
            for c in range(4):
                pcc, pk = pc[4 + c], 'pc%d' % (4 + c)
                if own:
                    P.add('pe', lambda e, c=c, j=j, pcc=pcc: e.matmul(pcc[:, 256:384], ARt[c][:, j, 128:256], Sbf[:, c, :], start=True, stop=False),
                          reads=['ARt%d' % c, 'Sbf%d' % c], writes=[pk])
                    for hh in range(2):
                        u = (c * 2 + hh) * 4 + j
                        hs = slice(hh * 64, (hh + 1) * 64)
                        os_ = slice(256 + hh * 64, 320 + hh * 64)
                        P.add('pe', lambda e, c=c, u=u, hs=hs, os_=os_, pcc=pcc: e.matmul(pcc[:, os_], SC_MP[:, u, 0:128], Ub[:, c, hs],
                                                                                  start=False, stop=False),
                              reads=['SC_MP', 'Ub%d' % c], writes=[pk])
                        P.add('pe', lambda e, c=c, j=j, u=u, hs=hs, os_=os_, hh=hh, pcc=pcc: e.matmul(pcc[:, os_], SC_LM[:, u, 128:256], tok[:, c, j, hs],
                                                                                              start=False, stop=(hh == 1)),
                              reads=['SC_LM', 'tok%d' % c], writes=[pk])
                P.add('pe', lambda e, c=c, j=j, pcc=pcc: e.matmul(pcc[:, 384:512], tok[:, c, j, 128:256], Ub[:, c, :], start=True, stop=False),
                      reads=['tok%d' % c, 'Ub%d' % c], writes=[pk])
                P.add('pe', lambda e, c=c, j=j, pcc=pcc: e.matmul(pcc[:, 384:512], tok[:, c, j, 256:384], tok[:, c, j, 0:128], start=False, stop=True),
                      reads=['tok%d' % c], writes=[pk])
                if own:
                    P.add('act', lambda e, c=c, pcc=pcc: e.activation(out=yt[:, c * 128:(c + 1) * 128], in_=pcc[:, 256:384], func=AF.Identity),
                          writes=[pk, 'yt'])
                for hh in range(2):
                    hs = slice(hh * 64, (hh + 1) * 64)
                    P.add('dve', lambda e, c=c, j=j, hs=hs, hh=hh, pcc=pcc: e.scalar_tensor_tensor(
                        out=S32[hs, c, hs], in0=S32[hs, c, hs], scalar=wc[hs, c, j:j + 1], in1=pcc[hs, 384 + hh * 64:448 + hh * 64],
                        op0=ALU.mult, op1=ALU.add), reads=['wc'], writes=[pk, 'S32_%d' % c])
                P.add('act', lambda e, c=c: e.activation(out=Sbf[:, c, :], in_=S32[:, c, :], func=AF.Identity),
                      reads=['S32_%d' % c], writes=['Sbf%d' % c])
            if own:
                tr = t0 + j * 128 - own0
                for c in range(4):
                    P.add('pe', lambda e, c=c, js=js: e.matmul(pc2[:, 2 * c:2 * c + 2], rk[c][:, js], E2[:], start=True, stop=True),
                          reads=['rk%d' % c, 'E2'], writes=['pc2'])
                P.add('dve', lambda e: e.tensor_copy(out=st8[:], in_=pc2[:, 0:8]), writes=['pc2', 'st8'])
                P.add('pool', lambda e, tr=tr: e.dma_start(out=y_out[tr:tr + 128, :], in_=yt[:]), reads=['yt'], dma=True)
                P.add('pool', lambda e, tr=tr: e.dma_start(out=s_out[tr:tr + 128, :], in_=st8[:]), reads=['st8'], dma=True)
                if fwd:
                    P.add('pe', lambda e, js=js: e.matmul(pm[0][:, :], sg[:, js], Gl[:], start=True, stop=True),
                          reads=['sg', 'Gl'], writes=['pm0'])
                    P.add('act', lambda e: e.activation(out=gt[:], in_=pm[0][:], func=AF.Identity), writes=['pm0', 'gt'])
                    P.add('pool', lambda e, tr=tr: e.dma_start(out=g_out[tr:tr + 128, :], in_=gt[:]), reads=['gt'], dma=True)
                    P.add('pool', lambda e, tr=tr, j=j: e.dma_start(
                        out=v_out[tr:tr + 128, :].rearrange("t (c v) -> t c v", c=4), in_=tok[:, :, j, 0:128]),
                        reads=['tok0', 'tok1', 'tok2', 'tok3'], dma=True)
    for sc in order:
        superchunk(sc)
    P.close()


def phase_assembly(nc, Town, yf, yb, sf, sbk, gd_, vd, yatt, xown, woutd, lnxw, lnxb, x1out):
    P = Phase(nc, "asm")
    NT = Town // 128
    Wo = P.sb("Wo", [128, 8, D], BF16)
    stage = [P.sb("stage%d" % i, [128, D]) for i in range(2)]
    lw = P.sb("lw", [128, 512]); lb = P.sb("lb", [128, 512])
    idf = P.sb("idf", [128, 128]); idb = P.sb("idb", [128, 128], BF16)
    yft = P.sb("yft", [128, 512]); ybt = P.sb("ybt", [128, 512]); gt = P.sb("gt", [128, 512])
    sq = P.sb("sq", [128, 512]); bon = P.sb("bon", [128, 512])
    vt = P.sb("vt", [128, 512], BF16)
    s8 = P.sb("s8", [128, 48])
    xo = P.sb("xo", [128, D])
    yr = P.sb("yr", [128, 512], BF16)
    ycT = P.sb("ycT", [128, 8, 128], BF16)
    pT = P.ps("pT", [128, 1024], BF16)
    pO = [P.ps("pO%d" % i, [128, 512]) for i in range(2)]
    P.add('sp', lambda e: e.dma_start(out=lw[:], in_=lnxw.partition_broadcast(128)), writes=['lw'], dma=True)
    P.add('sp', lambda e: e.dma_start(out=lb[:], in_=lnxb.partition_broadcast(128)), writes=['lb'], dma=True)
    emit_identity(P, idb, idf)
    load_weight_bf16(P, woutd, Wo, 8, D, stage, 'Wo', scale_tile=None, col_piece=D)

    def tile(n):
        r = slice(n * 128, (n + 1) * 128)
        P.add('sp', lambda e: e.dma_start(out=yft[:], in_=yf[r, :]), writes=['yft'], dma=True)
        P.add('sp', lambda e: e.dma_start(out=ybt[:], in_=yb[r, :]), writes=['ybt'], dma=True)
        P.add('sp', lambda e: e.dma_start(out=gt[:], in_=gd_[r, :]), writes=['gt'], dma=True)
        P.add('sp', lambda e: e.dma_start(out=vt[:], in_=vd[r, :]), writes=['vt'], dma=True)
        P.add('sp', lambda e: e.dma_start(out=s8[:, 0:8], in_=sf[r, :]), writes=['s8a'], dma=True)
        P.add('sp', lambda e: e.dma_start(out=s8[:, 8:16], in_=sbk[r, :]), writes=['s8b'], dma=True)
        P.add('sp', lambda e: e.dma_start(out=xo[:], in_=xown[r, :]), writes=['xo'], dma=True)
        P.add('sp', lambda e: e.dma_start(out=ycT[:, 4:8, :], in_=yatt[:, :, r].rearrange("g p t -> p g t")),
              writes=['ycTa'], dma=True)
        y3 = yft[:].rearrange("p (h c) -> p h c", c=64)

        def b8(col):
            return s8[:, col:col + 8].unsqueeze(2).broadcast_to([128, 8, 64])
        P.add('dve', lambda e: e.tensor_tensor(out=yft[:], in0=yft[:], in1=ybt[:], op=ALU.add), reads=['ybt'], writes=['yft'])
        P.add('dve', lambda e: e.tensor_reduce(out=s8[:, 16:24], in_=y3, axis=AX.X, op=ALU.add), reads=['yft'], writes=['s8c'])
        P.add('dve', lambda e: e.tensor_scalar(out=s8[:, 16:24], in0=s8[:, 16:24], scalar1=1.0 / 64, scalar2=None, op0=ALU.mult),
              writes=['s8c'])
        P.add('dve', lambda e: e.tensor_tensor(out=y3, in0=y3, in1=b8(16), op=ALU.subtract), reads=['s8c'], writes=['yft'])
        P.add('pool', lambda e: e.tensor_tensor(out=sq[:], in0=yft[:], in1=yft[:], op=ALU.mult), reads=['yft'], writes=['sq'])
        P.add('dve', lambda e: e.tensor_reduce(out=s8[:, 24:32], in_=sq[:].rearrange("p (h c) -> p h c", c=64), axis=AX.X,
                                               op=ALU.add), reads=['sq'], writes=['s8d'])
        emit_rstd(P, s8[:, 24:32], s8[:, 32:40], s8[:, 40:48], 1.0 / 64, LNX_EPS, ['s8d'], ['s8e'])
        P.add('dve', lambda e: e.tensor_tensor(out=y3, in0=y3, in1=b8(40), op=ALU.mult), reads=['s8e'], writes=['yft'])
        P.add('dve', lambda e: e.tensor_tensor(out=yft[:], in0=yft[:], in1=lw[:], op=ALU.mult), reads=['lw'], writes=['yft'])
        P.add('dve', lambda e: e.tensor_tensor(out=yft[:], in0=yft[:], in1=lb[:], op=ALU.add), reads=['lb'], writes=['yft'])
        P.add('dve', lambda e: e.tensor_tensor(out=s8[:, 0:8], in0=s8[:, 0:8], in1=s8[:, 8:16], op=ALU.add), reads=['s8b'],
              writes=['s8a'])
        P.add('dve', lambda e: e.scalar_tensor_tensor(out=bon[:].rearrange("p (h c) -> p h c", c=64),
                                                      in0=vt[:].rearrange("p (h c) -> p h c", c=64), scalar=0.5, in1=b8(0),
                                                      op0=ALU.mult, op1=ALU.mult), reads=['vt', 's8a'], writes=['bon'])
        P.add('pool', lambda e: e.tensor_tensor(out=yft[:], in0=yft[:], in1=bon[:], op=ALU.add), reads=['bon'], writes=['yft'])
        P.add('dve', lambda e: e.tensor_tensor(out=yr[:], in0=yft[:], in1=gt[:], op=ALU.mult), reads=['yft', 'gt'], writes=['yr'])
        for c in range(4):
            P.add('pe', lambda e, c=c: e.transpose(pT[:, c * 128:(c + 1) * 128], yr[:, c * 128:(c + 1) * 128], idb[:]),
                  reads=['yr', 'idb'], writes=['pT'])
        P.add('act', lambda e: e.activation(out=ycT[:, 0:4, :], in_=pT[:, 0:512].rearrange("p (c t) -> p c t", c=4),
                                            func=AF.Identity), writes=['pT', 'ycTr'])
        for half in range(2):
            for ch in range(8):
                P.add('pe', lambda e, half=half, ch=ch: e.matmul(pO[half][:, :], ycT[:, ch, :], Wo[:, ch, half * 512:(half + 1) * 512],
                                                                 start=(ch == 0), stop=(ch == 7)),
                      reads=['ycTr', 'ycTa', 'Wo'], writes=['pO%d' % half])
            P.add('dve', lambda e, half=half: e.tensor_tensor(out=xo[:, half * 512:(half + 1) * 512], in0=pO[half][:],
                                                              in1=xo[:, half * 512:(half + 1) * 512], op=ALU.add),
                  writes=['pO%d' % half, 'xo'])
        P.add('pool', lambda e: e.dma_start(out=x1out[r, :], in_=xo[:]), reads=['xo'], dma=True)
    for n in range(NT):
        tile(n)
    P.close()


def _cols(vec, nchunk):
    return np.ascontiguousarray(np.asarray(vec, np.float32).reshape(nchunk, 128).T)


def host_prep(p):
    f = lambda a: np.asarray(a, np.float32)
    w_in = f(p['w_in'])[0]
    mu_p = f(p['mu_prev'])[0]
    mu_n = f(p['mu_next'])[0]
    r_, k_, v_ = slice(0, 512), slice(512, 1024), slice(1024, 1536)
    wdf, wdb, adf, adb, gdc = slice(1536, 1600), slice(1600, 1664), slice(1664, 1728), slice(1728, 1792), slice(1792, 1920)

    def cat(a, sl):
        return np.concatenate([a[..., s] for s in sl], axis=-1)
    slf = [r_, k_, v_, wdf, adf, gdc]
    slb = [r_, k_, v_, wdb, adb]
    qcols = np.concatenate([np.arange(1920 + h * 64, 1920 + (h + 1) * 64) for h in QPERM])
    out = {}
    out['WRf'] = np.ascontiguousarray(cat(w_in, slf)); out['WRb'] = np.ascontiguousarray(cat(w_in, slb))
    out['mupf'] = _cols(cat(mu_p, slf), 14); out['munf'] = _cols(cat(mu_n, slf), 14)
    out['mupb'] = _cols(cat(mu_p, slb), 13); out['munb'] = _cols(cat(mu_n, slb), 13)
    out['winA'] = np.ascontiguousarray(np.concatenate([w_in[:, qcols], w_in[:, 2432:2688]], axis=1))
    out['g1c'] = _cols(f(p['norm1_g'])[0], 8)
    out['Wlf'] = np.ascontiguousarray(np.concatenate([f(p['w_lora_f'])[0], f(p['a_lora_f'])[0]], axis=0))
    out['Wlb'] = np.ascontiguousarray(np.concatenate([f(p['w_lora_b'])[0], f(p['a_lora_b'])[0]], axis=0))
    out['Gl'] = np.ascontiguousarray(f(p['g_lora'])[0])
    for nm in ('w0_f', 'w0_b', 'a0_f', 'a0_b', 'k_k', 'k_a'):
        out[nm] = _cols(f(p[nm])[0], 4)
    out['r_k'] = _cols(f(p['r_k'])[0].reshape(512), 4)
    out['lnxw'] = np.ascontiguousarray(f(p['lnx_w'])[0].reshape(1, 512)); out['lnxb'] = np.ascontiguousarray(f(p['lnx_b'])[0].reshape(1, 512))
    out['qg'] = np.ascontiguousarray(f(p['q_gain'])[0].reshape(1, 64)); out['kg'] = np.ascontiguousarray(f(p['k_gain'])[0].reshape(1, 64))
    w_out = f(p['w_out'])[0]
    arows = np.concatenate([np.arange(512 + h * 64, 512 + (h + 1) * 64) for h in QPERM])
    out['wout'] = np.ascontiguousarray(np.concatenate([w_out[0:512], w_out[arows]], axis=0))
    out['g2c'] = _cols(f(p['norm2_g'])[0], 8)
    out['gf'] = np.ascontiguousarray(f(p['norm_f_g']).reshape(1, D))
    out['wg'] = np.ascontiguousarray(f(p['ffn_gate'])[0]); out['wu'] = np.ascontiguousarray(f(p['ffn_up'])[0])
    out['wd'] = np.ascontiguousarray(f(p['ffn_down'])[0])
    return out


WEIGHT_SPECS = [('WRf', [D, 1792]), ('WRb', [D, 1664]), ('mupf', [128, 14]), ('munf', [128, 14]), ('mupb', [128, 13]),
                ('munb', [128, 13]), ('winA', [D, 768]), ('g1c', [128, 8]), ('Wlf', [128, 512]), ('Wlb', [128, 512]),
                ('Gl', [128, 512]), ('w0_f', [128, 4]), ('w0_b', [128, 4]), ('a0_f', [128, 4]), ('a0_b', [128, 4]),
                ('k_k', [128, 4]), ('k_a', [128, 4]), ('r_k', [128, 4]), ('lnxw', [1, 512]), ('lnxb', [1, 512]),
                ('qg', [1, 64]), ('kg', [1, 64]), ('wout', [D, D]), ('g2c', [128, 8]), ('gf', [1, D]),
                ('wg', [D, DFF]), ('wu', [D, DFF]), ('wd', [DFF, D])]


def declare_weights(nc):
    return {n: nc.dram_tensor(n, s, F32, kind="ExternalInput").ap() for n, s in WEIGHT_SPECS}


def mixer_job(nc, W, tag, xw, vmask, Tw_f, own_f, Tw_b, own_b, xw_f_off, xw_b_off, xk, Tk, xown, Town, qrow0, x1rows, scr):
    phase_attention(nc, xk, xown, W['winA'], W['g1c'], W['qg'], W['kg'], qrow0, scr['yatt'], Tk, Town)
    phase_rwkv(nc, False, xw[xw_b_off:xw_b_off + Tw_b + 256, :], vmask[:, xw_b_off:xw_b_off + Tw_b], Tw_b, own_b[0], own_b[1],
               W['WRb'], 13, W['mupb'], W['munb'], W['g1c'], W['Wlb'], W['w0_b'], W['a0_b'], W['k_k'], W['k_a'], W['r_k'],
               scr['yb'], scr['sb'])
    phase_rwkv(nc, True, xw[xw_f_off:xw_f_off + Tw_f + 256, :], vmask[:, xw_f_off:xw_f_off + Tw_f], Tw_f, own_f[0], own_f[1],
               W['WRf'], 14, W['mupf'], W['munf'], W['g1c'], W['Wlf'], W['w0_f'], W['a0_f'], W['k_k'], W['k_a'], W['r_k'],
               scr['yf'], scr['sf'], Gld=W['Gl'], g_out=scr['g'], v_out=scr['v'])
    phase_assembly(nc, Town, scr['yf'], scr['yb'], scr['sf'], scr['sb'], scr['g'], scr['v'], scr['yatt'], xown, W['wout'],
                   W['lnxw'], W['lnxb'], x1rows)


def phase_ffn(nc, x, y, g2c, gf, wg, wu, wd, ntok):
    Phase._n[0] += 1
    tagp = "ffn%d_" % Phase._n[0]
    ngroups = ntok // GT
    xg = x.rearrange("(n s p) d -> n p s d", s=NSUB, p=128)
    yg = y.rearrange("(n s p) d -> n p s d", s=NSUB, p=128)

    es = contextlib.ExitStack()
    with es:
        def sb(name, shape, dt=F32):
            return es.enter_context(nc.sbuf_tensor(tagp + name, shape, dt))

        def pt(name, shape, dt=F32):
            return es.enter_context(nc.psum_tensor(tagp + name, shape, dt))

        Wg = sb("Wg", [128, NK, DFF], BF16)
        Wu = sb("Wu", [128, NK, DFF], BF16)
        Wd = sb("Wd", [128, NFF, D], BF16)
        stage = [sb("stage%d" % i, [128, 1024]) for i in range(2)]
        g2t = sb("g2t", [128, NK])
        gft = sb("gft", [128, D])
        idf = sb("idf", [128, 128])
        idb = sb("idb", [128, 128], BF16)
        xt = [sb("xt%d" % i, [128, NSUB, D]) for i in range(2)]
        hb = sb("hb", [128, NSUB, D], BF16)
        hT = sb("hT", [128, NK, GT], BF16)
        actT = sb("actT", [128, NFF, GT], BF16)
        tmp = [sb("tmp%d" % i, [128, GT]) for i in range(2)]
        ot = sb("ot", [128, NSUB, D])
        ss = sb("ss", [128, 4 * NSUB])
        pst = [pt("pst%d" % i, [128, 1024], BF16)[:, 0:GT] for i in range(2)]
        psg = [pt("psg%d" % i, [128, 512])[:, 0:GT] for i in range(2)]
        psu = [pt("psu%d" % i, [128, 512])[:, 0:GT] for i in range(2)]
        psd = [pt("psd%d" % i, [128, 512]) for i in range(2)]

        S = Sched(nc)

        S.add('sp', lambda e: e.dma_start(out=g2t[:], in_=g2c[:, :]), writes=['g2t'], dma=True)
        S.add('sp', lambda e: e.dma_start(out=gft[:], in_=gf.partition_broadcast(128)), writes=['gft'], dma=True)
        S.add('pool', lambda e: e.memset(idf[:], 1.0), writes=['idf'])
        S.add('pool', lambda e: e.affine_select(out=idf[:], in_=idf[:], pattern=[[-1, 128]], compare_op=ALU.is_equal,
                                                 fill=0.0, base=0, channel_multiplier=1), writes=['idf'])
        S.add('dve', lambda e: e.tensor_copy(out=idb[:], in_=idf[:]), reads=['idf'], writes=['idb'])

        nst = [0]

        def load_cast(src_ap, dst_ap, ncols, dkey, scale_ap=None):
            i = nst[0] % 2
            nst[0] += 1
            skey = 'stage%d' % i
            S.add('sp', lambda e: e.dma_start(out=stage[i][:, 0:ncols], in_=src_ap), writes=[skey], dma=True)
            if scale_ap is not None:
                S.add('dve', lambda e: e.tensor_scalar(out=dst_ap, in0=stage[i][:, 0:ncols], scalar1=scale_ap, scalar2=None,
                                                        op0=ALU.mult), reads=[skey, 'g2t'], writes=[dkey])
            else:
                S.add('act', lambda e: e.activation(out=dst_ap, in_=stage[i][:, 0:ncols], func=AF.Identity),
                      reads=[skey], writes=[dkey])

        pieces = [(0, 1024), (1024, 1024), (2048, DFF - 2048)]
        for k in range(NK):
            for (c0, cn) in pieces:
                load_cast(wg[k * 128:(k + 1) * 128, c0:c0 + cn], Wg[:, k, c0:c0 + cn], cn, 'Wg', g2t[:, k:k + 1])
                load_cast(wu[k * 128:(k + 1) * 128, c0:c0 + cn], Wu[:, k, c0:c0 + cn], cn, 'Wu', g2t[:, k:k + 1])
        for f in range(NFF):
            load_cast(wd[f * 128:(f + 1) * 128, :], Wd[:, f, :], D, 'Wd')

        nps = [0, 0, 0]
        for g in range(ngroups):
            X = xt[g % 2]
            xk = 'xt%d' % (g % 2)
            S.add('sp', lambda e, X=X, g=g: e.dma_start(out=X[:], in_=xg[g]), writes=[xk], dma=True)
            for s in range(NSUB):
                S.add('act', lambda e, X=X, s=s: e.activation(out=hb[:, s, :], in_=X[:, s, :], func=AF.Square,
                                                              accum_out=ss[:, s:s + 1]),
                      reads=[xk], writes=['hb', 'ss'])
            S.add('act', lambda e: e.activation(out=ss[:, NSUB:2 * NSUB], in_=ss[:, 0:NSUB], func=AF.Ln,
                                                scale=1.0 / D, bias=NORM_EPS), reads=[], writes=['ss'])
            S.add('act', lambda e: e.activation(out=ss[:, 0:NSUB], in_=ss[:, NSUB:2 * NSUB], func=AF.Exp, scale=-0.5),
                  reads=[], writes=['ss'])
            for s in range(NSUB):
                S.add('dve', lambda e, X=X, s=s: e.tensor_scalar(out=hb[:, s, :], in0=X[:, s, :], scalar1=ss[:, s:s + 1],
                                                                 scalar2=None, op0=ALU.mult),
                      reads=[xk, 'ss'], writes=['hb'])
            for k in range(NK):
                P = pst[nps[0] % 2]
                pk = 'pst%d' % (nps[0] % 2)
                nps[0] += 1
                for s in range(NSUB):
                    S.add('pe', lambda e, P=P, s=s, k=k: e.transpose(P[:, s * 128:(s + 1) * 128],
                                                                      hb[:, s, k * 128:(k + 1) * 128], idb[:]),
                          reads=['hb', 'idb'], writes=[pk])
                if k % 2 == 0:
                    S.add('dve', lambda e, P=P, k=k: e.tensor_copy(out=hT[:, k, :], in_=P[:]), writes=[pk, 'hT'])
                else:
                    S.add('act', lambda e, P=P, k=k: e.activation(out=hT[:, k, :], in_=P[:], func=AF.Identity),
                          writes=[pk, 'hT'])
            for f in range(NFF):
                i = nps[1] % 2
                nps[1] += 1
                G, U, T = psg[i], psu[i], tmp[i]
                for k in range(NK):
                    S.add('pe', lambda e, G=G, k=k, f=f: e.matmul(G[:, :], Wg[:, k, f * 128:(f + 1) * 128], hT[:, k, :],
                                                                  start=(k == 0), stop=(k == NK - 1)),
                          reads=['Wg', 'hT'], writes=['psg%d' % i])
                for k in range(NK):
                    S.add('pe', lambda e, U=U, k=k, f=f: e.matmul(U[:, :], Wu[:, k, f * 128:(f + 1) * 128], hT[:, k, :],
                                                                  start=(k == 0), stop=(k == NK - 1)),
                          reads=['Wu', 'hT'], writes=['psu%d' % i])
                S.add('act', lambda e, G=G, T=T: e.activation(out=T[:], in_=G[:], func=AF.Silu),
                      writes=['psg%d' % i, 'tmp%d' % i])
                S.add('dve', lambda e, U=U, T=T, f=f: e.tensor_tensor(out=actT[:, f, :], in0=U[:], in1=T[:], op=ALU.mult),
                      reads=['tmp%d' % i], writes=['psu%d' % i, 'actT'])
            for s in range(NSUB):
                for c in range(2):
                    i = nps[2] % 2
                    nps[2] += 1
                    Pd = psd[i]
                    for f in range(NFF):
                        S.add('pe', lambda e, Pd=Pd, f=f, s=s, c=c: e.matmul(Pd[:, :], actT[:, f, s * 128:(s + 1) * 128],
                                                                            Wd[:, f, c * 512:(c + 1) * 512],
                                                                            start=(f == 0), stop=(f == NFF - 1)),
                              reads=['actT', 'Wd'], writes=['psd%d' % i])
                    S.add('dve', lambda e, Pd=Pd, X=X, s=s, c=c: e.tensor_tensor(out=X[:, s, c * 512:(c + 1) * 512],
                                                                               in0=Pd[:], in1=X[:, s, c * 512:(c + 1) * 512],
                                                                               op=ALU.add),
                          writes=['psd%d' % i, xk])
            for s in range(NSUB):
                S.add('act', lambda e, X=X, s=s: e.activation(out=hb[:, s, :], in_=X[:, s, :], func=AF.Square,
                                                              accum_out=ss[:, 2 * NSUB + s:2 * NSUB + s + 1]),
                      reads=[xk], writes=['hb', 'ss'])
            S.add('act', lambda e: e.activation(out=ss[:, 3 * NSUB:4 * NSUB], in_=ss[:, 2 * NSUB:3 * NSUB], func=AF.Ln,
                                                scale=1.0 / D, bias=NORM_EPS), writes=['ss'])
            S.add('act', lambda e: e.activation(out=ss[:, 2 * NSUB:3 * NSUB], in_=ss[:, 3 * NSUB:4 * NSUB], func=AF.Exp,
                                                scale=-0.5), writes=['ss'])
            for s in range(NSUB):
                S.add('dve', lambda e, X=X, s=s: e.scalar_tensor_tensor(out=ot[:, s, :], in0=X[:, s, :],
                                                                        scalar=ss[:, 2 * NSUB + s:2 * NSUB + s + 1],
                                                                        in1=gft[:], op0=ALU.mult, op1=ALU.mult),
                      reads=[xk, 'ss', 'gft'], writes=['ot'])
            S.add('pool', lambda e, g=g: e.dma_start(out=yg[g], in_=ot[:]), reads=['ot'], dma=True)
        S.emit()


def build_program(T, NPQ, ST):
    Q = ST // 4
    ntok = NPQ * T + Q
    nc = bass.Bass("TRN2", target_bir_lowering=False)
    W = declare_weights(nc)
    xp = nc.dram_tensor("xp", [NPQ, T + 256, D], F32, kind="ExternalInput").ap()
    vones = nc.dram_tensor("vones", [1, T], F32, kind="ExternalInput").ap()
    xsw = nc.dram_tensor("xsw", [7 * Q + 256, D], F32, kind="ExternalInput").ap()
    xsk = nc.dram_tensor("xsk", [ST, D], F32, kind="ExternalInput").ap()
    vms = nc.dram_tensor("vms", [1, 7 * Q], F32, kind="ExternalInput").ap()
    qr0s = nc.dram_tensor("qr0s", [1, 1], F32, kind="ExternalInput").ap()
    qr0p = nc.dram_tensor("qr0p", [1, 1], F32, kind="ExternalInput").ap()
    y = nc.dram_tensor("y", [ntok, D], F32, kind="ExternalOutput").ap()
    x1 = nc.dram_tensor("x1", [ntok, D], F32, kind="Internal").ap()

    def scratch(tag, n):
        return dict(yatt=nc.dram_tensor(tag + "yatt", [4, 128, n], BF16, kind="Internal").ap(),
                    yf=nc.dram_tensor(tag + "yf", [n, 512], F32, kind="Internal").ap(),
                    yb=nc.dram_tensor(tag + "yb", [n, 512], F32, kind="Internal").ap(),
                    sf=nc.dram_tensor(tag + "sf", [n, 8], F32, kind="Internal").ap(),
                    sb=nc.dram_tensor(tag + "sb", [n, 8], F32, kind="Internal").ap(),
                    g=nc.dram_tensor(tag + "g", [n, 512], F32, kind="Internal").ap(),
                    v=nc.dram_tensor(tag + "v", [n, 512], BF16, kind="Internal").ap())
    _skip = ''
    _es = contextlib.ExitStack()
    SemPool.current = SemPool(nc, _es)
    scr = scratch("s_", Q)
    if 's' not in _skip:
      mixer_job(nc, W, "s", xsw, vms, 4 * Q, (3 * Q, 4 * Q), 4 * Q, (0, Q), 0, 3 * Q, xsk, ST,
                xsw[128 + 3 * Q:128 + 4 * Q, :], Q, qr0s, x1[NPQ * T:NPQ * T + Q, :], scr)
    for i in range(0 if 'p' not in _skip else NPQ, NPQ):
        scr = scratch("p%d_" % i, T)
        xown = xp[i, 128:128 + T, :]
        mixer_job(nc, W, "p%d" % i, xp[i], vones, T, (0, T), T, (0, T), 0, 0, xown, T, xown, T, qr0p,
                  x1[i * T:(i + 1) * T, :], scr)
    if 'f' not in _skip:
      phase_ffn(nc, x1, y, W['g2c'], W['gf'], W['wg'], W['wu'], W['wd'], ntok)
    SemPool.current = None
    _es.close()
    return nc


_NC_CACHE = {}


def kernel(x_prompt, x_sample, norm1_g, w_in, mu_prev, mu_next, k_k, k_a, r_k, w0_f, w_lora_f, w0_b, w_lora_b,
           a0_f, a_lora_f, a0_b, a_lora_b, g_lora, lnx_w, lnx_b, q_gain, k_gain, w_out, norm2_g, ffn_gate,
           ffn_up, ffn_down, norm_f_g):
    params = dict(norm1_g=norm1_g, w_in=w_in, mu_prev=mu_prev, mu_next=mu_next, k_k=k_k, k_a=k_a, r_k=r_k, w0_f=w0_f,
                  w_lora_f=w_lora_f, w0_b=w0_b, w_lora_b=w_lora_b, a0_f=a0_f, a_lora_f=a_lora_f, a0_b=a0_b,
                  a_lora_b=a_lora_b, g_lora=g_lora, lnx_w=lnx_w, lnx_b=lnx_b, q_gain=q_gain, k_gain=k_gain, w_out=w_out,
                  norm2_g=norm2_g, ffn_gate=ffn_gate, ffn_up=ffn_up, ffn_down=ffn_down, norm_f_g=norm_f_g)
    x_prompt = np.asarray(x_prompt, np.float32)
    x_sample = np.asarray(x_sample, np.float32)
    B, T, _ = x_prompt.shape
    SB, ST, _ = x_sample.shape
    NPQ = B // NCORES
    Q = ST // 4
    key = (T, NPQ, ST)
    if key not in _NC_CACHE:
        _NC_CACHE[key] = build_program(T, NPQ, ST)
    nc = _NC_CACHE[key]
    Wh = host_prep(params)
    in_maps = []
    for c in range(NCORES):
        s, j = c // 4, c % 4
        m = {n: Wh[n] for n, _ in WEIGHT_SPECS}
        xp = np.zeros((NPQ, T + 256, D), np.float32)
        xp[:, 128:128 + T] = x_prompt[c * NPQ:(c + 1) * NPQ]
        m["xp"] = xp
        m["vones"] = np.ones((1, T), np.float32)
        xsw = np.zeros((7 * Q + 256, D), np.float32)
        tlo = j * Q - 3 * Q - 128
        lo, hi = max(0, tlo), min(ST, tlo + 7 * Q + 256)
        xsw[lo - tlo:hi - tlo] = x_sample[s, lo:hi]
        m["xsw"] = xsw
        vm = np.zeros((1, 7 * Q), np.float32)
        t_first = j * Q - 3 * Q
        lo, hi = max(0, t_first), min(ST, t_first + 7 * Q)
        vm[0, lo - t_first:hi - t_first] = 1.0
        m["vms"] = vm
        m["xsk"] = np.ascontiguousarray(x_sample[s])
        m["qr0s"] = np.full((1, 1), float(j * Q // 64), np.float32)
        m["qr0p"] = np.zeros((1, 1), np.float32)
        in_maps.append(m)
    res = run_bass_kernel_spmd(nc, in_maps, core_ids=list(range(NCORES)))
    y_prompt = np.empty((B, T, D), np.float32)
    y_sample = np.empty((SB, ST, D), np.float32)
    for c in range(NCORES):
        yc = np.asarray(res.results[c]["y"])
        y_prompt[c * NPQ:(c + 1) * NPQ] = yc[:NPQ * T].reshape(NPQ, T, D)
        y_sample[c // 4, (c % 4) * Q:(c % 4 + 1) * Q] = yc[NPQ * T:]
    return (y_prompt, y_sample)
```

```python
import contextlib
import numpy as np
import concourse.bass as bass
import concourse.mybir as mybir
from concourse.bass_utils import run_bass_kernel_spmd

F32 = mybir.dt.float32
BF16 = mybir.dt.bfloat16
AF = mybir.ActivationFunctionType
ALU = mybir.AluOpType
AX = mybir.AxisListType
I32 = mybir.dt.int32

D = 1024
DFF = 2816
NFF = DFF // 128
NK = D // 128
NCORES = 8
TOK_PER_CORE = 4 * 2048 + 4096
GT = 256
NSUB = GT // 128
NORM_EPS = 1e-6
LNX_EPS = 64e-5
HEAD_DIM = 64
ROPE_THETA = 10000.0
ROPE_PAIRS = 16
QPERM = (0, 4, 1, 5, 2, 6, 3, 7)


SAME_ENGINE_SYNC = ('act', 'dve', 'pool')


class Sched:
    ENG = ('pe', 'act', 'dve', 'pool', 'sp')
    NDMA = 12

    def __init__(self, nc, same_engine_sync=SAME_ENGINE_SYNC):
        self.nc = nc
        self.ops = []
        self.last_w = {}
        self.readers = {}
        self.same = set(same_engine_sync)

    def add(self, eng, fn, reads=(), writes=(), dma=False):
        i = len(self.ops)
        deps = set()
        for r in reads:
            j = self.last_w.get(r)
            if j is not None:
                deps.add(j)
        for w in writes:
            j = self.last_w.get(w)
            if j is not None:
                deps.add(j)
            for j in self.readers.get(w, {}).values():
                if isinstance(j, list):
                    deps.update(j)
                else:
                    deps.add(j)
        for w in writes:
            self.last_w[w] = i
            self.readers[w] = {}
        for r in reads:
            if r not in writes:
                rd = self.readers.setdefault(r, {})
                if dma:
                    rd.setdefault(('dma', eng), []).append(i)
                else:
                    rd[eng] = i
        self.ops.append(dict(eng=eng, fn=fn, deps=deps, dma=dma, needs_inc=False))
        return i

    def emit(self):
        nc = self.nc
        ops = self.ops
        for op in ops:
            for j in op['deps']:
                oj = ops[j]
                if oj['dma']:
                    oj['needs_inc'] = True
                elif oj['eng'] != op['eng'] or op['dma'] or (op['eng'] in self.same):
                    oj['needs_inc'] = True
        with contextlib.ExitStack() as es:
            pool = SemPool.current
            if pool is None:
                pool = SemPool(nc, es)
            sems, dsems, cnt, dma_cnt = pool.sems, pool.dsems, pool.cnt, pool.dcnt
            dma_rr = {e: 0 for e in self.ENG}
            for op in ops:
                if op['dma']:
                    k = dma_rr[op['eng']] % self.NDMA
                    dma_rr[op['eng']] += 1
                    key = (op['eng'], k)
                    prev = dma_cnt.get(key, 0)
                    op['dsem'] = key
                    op['dprev'] = prev
                    dma_cnt[key] = prev + 16
                    op['ticket'] = prev + 16
                elif op['needs_inc']:
                    cnt[op['eng']] += 1
                    op['ticket'] = cnt[op['eng']]
            block = es.enter_context(nc.Block())

            def run(ename):
                def body(eng):
                    waited = {}

                    def wait(sem_key, sem, val):
                        if waited.get(sem_key, 0) >= val:
                            return
                        waited[sem_key] = val
                        eng.wait_ge(sem, val)
                    last_dma = {}
                    for op in ops:
                        if op['eng'] != ename:
                            continue
                        if op['dma'] and op['dprev'] > 0:
                            wait(op['dsem'], dsems[op['dsem']], op['dprev'])
                        for j in sorted(op['deps']):
                            oj = ops[j]
                            if oj['dma']:
                                wait(oj['dsem'], dsems[oj['dsem']], oj['ticket'])
                            elif oj['eng'] != ename or op['dma'] or (ename in self.same):
                                wait(oj['eng'], sems[oj['eng']], oj['ticket'])
                        ins = op['fn'](eng)
                        if op['dma']:
                            ins.then_inc(dsems[op['dsem']], 16)
                            last_dma[op['dsem']] = op['ticket']
                        elif op['needs_inc']:
                            ins.then_inc(sems[ename], 1)
                    for key, t in last_dma.items():
                        wait(key, dsems[key], t)
                return body
            block.tensor(run('pe'))
            block.scalar(run('act'))
            block.vector(run('dve'))
            block.gpsimd(run('pool'))
            block.sync(run('sp'))


class SemPool:
    current = None

    def __init__(self, nc, es):
        self.sems = {e: es.enter_context(nc.semaphore('s_' + e)) for e in Sched.ENG}
        self.dsems = {}
        for e in ('sp', 'pool'):
            for k in range(Sched.NDMA):
                self.dsems[(e, k)] = es.enter_context(nc.semaphore('d_%s_%d' % (e, k)))
        self.cnt = {e: 0 for e in Sched.ENG}
        self.dcnt = {}


STATS = []


class Phase:
    _n = [0]

    def __init__(self, nc, tag):
        self.nc = nc
        Phase._n[0] += 1
        self.tag = "%s%d_" % (tag, Phase._n[0])
        self.es = contextlib.ExitStack()
        self.S = Sched(nc)

    def sb(self, name, shape, dt=F32):
        return self.es.enter_context(self.nc.sbuf_tensor(self.tag + name, shape, dt))

    def ps(self, name, shape, dt=F32):
        return self.es.enter_context(self.nc.psum_tensor(self.tag + name, shape, dt))

    def add(self, *a, **k):
        return self.S.add(*a, **k)

    def close(self):
        self.S.emit()
        self.es.close()
        import collections
        cnt = collections.Counter(o['eng'] + ('_dma' if o['dma'] else '') for o in self.S.ops)
        STATS.append((self.tag, len(self.S.ops), dict(cnt)))


def emit_identity(P, idb, idf):
    P.add('pool', lambda e: e.memset(idf[:], 1.0), writes=['idf'])
    P.add('pool', lambda e: e.affine_select(out=idf[:], in_=idf[:], pattern=[[-1, 128]], compare_op=ALU.is_equal,
                                            fill=0.0, base=0, channel_multiplier=1), writes=['idf'])
    P.add('dve', lambda e: e.tensor_copy(out=idb[:], in_=idf[:]), reads=['idf'], writes=['idb'])


def load_weight_bf16(P, src, dst, nrows_chunks, ncols, stage, dkey, scale_tile=None, col_piece=1024, eng_alt=True):
    n = [0]
    for k in range(nrows_chunks):
        c0 = 0
        while c0 < ncols:
            cn = min(col_piece, ncols - c0)
            i = n[0] % len(stage)
            n[0] += 1
            st = stage[i]
            skey = 'stage%d' % i
            P.add('sp', lambda e, st=st, k=k, c0=c0, cn=cn: e.dma_start(out=st[:, 0:cn],
                                                                       in_=src[k * 128:(k + 1) * 128, c0:c0 + cn]),
                  writes=[skey], dma=True)
            if scale_tile is not None:
                P.add('dve', lambda e, st=st, k=k, c0=c0, cn=cn: e.tensor_scalar(
                    out=dst[:, k, c0:c0 + cn], in0=st[:, 0:cn], scalar1=scale_tile[:, k:k + 1], scalar2=None,
                    op0=ALU.mult), reads=[skey, 'wscale'], writes=[dkey])
            else:
                P.add('act', lambda e, st=st, k=k, c0=c0, cn=cn: e.activation(
                    out=dst[:, k, c0:c0 + cn], in_=st[:, 0:cn], func=AF.Identity), reads=[skey], writes=[dkey])
            c0 += cn


def emit_rstd(P, ss_in, tmp, out, scale, eps, keys_r, keys_w):
    P.add('act', lambda e: e.activation(out=tmp, in_=ss_in, func=AF.Ln, scale=scale, bias=eps),
          reads=keys_r, writes=keys_w)
    P.add('act', lambda e: e.activation(out=out, in_=tmp, func=AF.Exp, scale=-0.5), reads=[], writes=keys_w)


def emit_norm_hT(P, X, xkey, hb, hT_dst, hT_key, pT, pT_key, ss, idb, extra_scale=None):
    P.add('act', lambda e: e.activation(out=hb[:], in_=X, func=AF.Square, accum_out=ss[:, 0:1]),
          reads=[xkey], writes=['hb', 'ss'])
    emit_rstd(P, ss[:, 0:1], ss[:, 1:2], ss[:, 2:3], 1.0 / D, NORM_EPS, [], ['ss'])
    P.add('dve', lambda e: e.tensor_scalar(out=hb[:], in0=X, scalar1=ss[:, 2:3], scalar2=None, op0=ALU.mult),
          reads=[xkey, 'ss'], writes=['hb'])
    for k in range(NK):
        P.add('pe', lambda e, k=k: e.transpose(pT[:, k * 128:(k + 1) * 128], hb[:, k * 128:(k + 1) * 128], idb[:]),
              reads=['hb', 'idb'], writes=[pT_key])
    P.add('dve', lambda e: e.tensor_copy(out=hT_dst, in_=pT[:].rearrange("p (k t) -> p k t", k=NK)),
          writes=[pT_key, hT_key])


def emit_rope_tables(P, Crow, Srow, ntiles, row0_tile, wk, Ccol=None, Scol=None):
    pi_i, pf, inv, rowi, rowf, ang, t0, t1, ti = (wk['pi_i'], wk['pf'], wk['inv'], wk['rowi'], wk['rowf'],
                                                  wk['ang'], wk['t0'], wk['t1'], wk['ti'])
    P.add('pool', lambda e: e.iota(pi_i[:], pattern=[[0, 1]], base=0, channel_multiplier=1), writes=['pi_i'])
    P.add('dve', lambda e: e.tensor_copy(out=pf[:, 0:1], in_=pi_i[:]), reads=['pi_i'], writes=['pf'])
    P.add('dve', lambda e: e.tensor_scalar(out=pf[:, 1:2], in0=pf[:, 0:1], scalar1=64.0, scalar2=None, op0=ALU.is_ge),
          writes=['pf'])
    P.add('dve', lambda e: e.scalar_tensor_tensor(out=pf[:, 2:3], in0=pf[:, 1:2], scalar=-64.0, in1=pf[:, 0:1],
                                                  op0=ALU.mult, op1=ALU.add), writes=['pf'])
    for i in range(ROPE_PAIRS):
        v = float(ROPE_THETA ** (-i / ROPE_PAIRS)) / (2.0 * np.pi)
        P.add('pool', lambda e, i=i, v=v: e.memset(inv[:, i:i + 1], v), writes=['inv'])

    def sin_turns(dst, n, shift, key):
        P.add('dve', lambda e: e.tensor_scalar(out=t0[:, 0:n], in0=ang[:, 0:n], scalar1=shift, scalar2=None, op0=ALU.add),
              reads=['ang'], writes=['t0'])
        P.add('dve', lambda e: e.tensor_copy(out=ti[:, 0:n], in_=t0[:, 0:n]), reads=['t0'], writes=['ti'])
        P.add('dve', lambda e: e.tensor_copy(out=t1[:, 0:n], in_=ti[:, 0:n]), reads=['ti'], writes=['t1'])
        P.add('dve', lambda e: e.tensor_tensor(out=t0[:, 0:n], in0=t0[:, 0:n], in1=t1[:, 0:n], op=ALU.subtract),
              reads=['t1'], writes=['t0'])
        P.add('dve', lambda e: e.tensor_scalar(out=t1[:, 0:n], in0=t0[:, 0:n], scalar1=0.5, scalar2=None, op0=ALU.is_ge),
              reads=['t0'], writes=['t1'])
        P.add('dve', lambda e: e.tensor_tensor(out=t0[:, 0:n], in0=t0[:, 0:n], in1=t1[:, 0:n], op=ALU.subtract),
              reads=['t1'], writes=['t0'])
        P.add('dve', lambda e: e.tensor_scalar(out=t1[:, 0:n], in0=t0[:, 0:n], scalar1=-0.5, scalar2=None, op0=ALU.is_lt),
              reads=['t0'], writes=['t1'])
        P.add('dve', lambda e: e.tensor_tensor(out=t0[:, 0:n], in0=t0[:, 0:n], in1=t1[:, 0:n], op=ALU.add),
              reads=['t1'], writes=['t0'])
        P.add('act', lambda e: e.activation(out=dst, in_=t0[:, 0:n], func=AF.Sin, scale=6.28318),
              reads=['t0'], writes=[key])

    if Ccol is not None:
        P.add('dve', lambda e: e.tensor_scalar(out=ang[:, 0:16], in0=inv[:, 0:16], scalar1=pf[:, 2:3], scalar2=None,
                                               op0=ALU.mult), reads=['pf', 'inv'], writes=['ang'])
        sin_turns(Scol[:, 0:16], 16, 0.0, 'Stab')
        sin_turns(Ccol[:, 0:16], 16, 0.25, 'Ctab')
    CH = 32
    for n0 in range(0, ntiles, CH):
        NT = min(CH, ntiles - n0)
        P.add('pool', lambda e, n0=n0, NT=NT: e.iota(rowi[:, 0:NT], pattern=[[2, NT]], base=2 * n0, channel_multiplier=0),
              writes=['rowi'])
        P.add('dve', lambda e, NT=NT: e.tensor_copy(out=rowf[:, 0:NT], in_=rowi[:, 0:NT]), reads=['rowi'], writes=['rowf'])
        P.add('dve', lambda e, NT=NT: e.tensor_scalar(out=rowf[:, 0:NT], in0=rowf[:, 0:NT], scalar1=pf[:, 1:2], scalar2=None,
                                                      op0=ALU.add), reads=['pf'], writes=['rowf'])
        if row0_tile is not None:
            P.add('dve', lambda e, NT=NT: e.tensor_scalar(out=rowf[:, 0:NT], in0=rowf[:, 0:NT], scalar1=row0_tile,
                                                          scalar2=None, op0=ALU.add), reads=['row0'], writes=['rowf'])
        P.add('dve', lambda e, NT=NT: e.tensor_tensor(
            out=ang[:, 0:NT * 16].rearrange("p (n c) -> p n c", c=16),
            in0=rowf[:, 0:NT].unsqueeze(2).broadcast_to([128, NT, 16]),
            in1=inv[:, 0:16].unsqueeze(1).broadcast_to([128, NT, 16]), op=ALU.mult),
            reads=['rowf', 'inv'], writes=['ang'])
        sin_turns(Srow[:, n0:n0 + NT, :].rearrange("p n c -> p (n c)"), NT * 16, 0.0, 'Stab')
        sin_turns(Crow[:, n0:n0 + NT, :].rearrange("p n c -> p (n c)"), NT * 16, 0.25, 'Ctab')


def emit_qk_norm_rope(P, src_ps, src_key, nheads, gain_b, Cr, Sr, Cc, Sc, scale, out_bf, out_key, wk, tkeys):
    H = nheads
    W = H * 64
    sq, qn, ta, tb, st = wk['sq'], wk['qn'], wk['ta'], wk['tb'], wk['st']
    P.add('act', lambda e: e.activation(out=qn[:, 0:W], in_=src_ps, func=AF.Identity), writes=[src_key, 'qn'])
    P.add('dve', lambda e: e.tensor_tensor(out=sq[:, 0:W], in0=qn[:, 0:W], in1=qn[:, 0:W], op=ALU.mult),
          reads=['qn'], writes=['sq'])
    P.add('dve', lambda e: e.tensor_reduce(out=st[:, 0:H], in_=sq[:, 0:W].rearrange("p (h c) -> p h c", c=64),
                                           axis=AX.X, op=ALU.add), reads=['sq'], writes=['st'])
    emit_rstd(P, st[:, 0:H], st[:, 8:8 + H], st[:, 16:16 + H], 1.0 / 64, NORM_EPS, ['st'], ['st'])
    q3 = qn[:, 0:W].rearrange("p (h c) -> p h c", c=64)
    P.add('dve', lambda e: e.tensor_tensor(out=q3, in0=q3, in1=st[:, 16:16 + H].unsqueeze(2).broadcast_to([128, H, 64]),
                                           op=ALU.mult), reads=['st'], writes=['qn'])
    P.add('dve', lambda e: e.scalar_tensor_tensor(out=q3, in0=q3, scalar=float(scale),
                                                  in1=gain_b.unsqueeze(1).broadcast_to([128, H, 64]),
                                                  op0=ALU.mult, op1=ALU.mult), reads=['gains'], writes=['qn'])
    q5 = qn[:, 0:W].rearrange("p (h a b i) -> p h a b i", a=2, b=2, i=16)
    o5 = out_bf.rearrange("p (h a b i) -> p h a b i", a=2, b=2, i=16)
    A3 = ta[:, 0:H * 16].rearrange("p (h i) -> p h i", i=16)
    B3 = tb[:, 0:H * 16].rearrange("p (h i) -> p h i", i=16)
    for a, (Ct, St) in enumerate(((Cr, Sr), (Cc, Sc))):
        x1, x2 = q5[:, :, a, 0, :], q5[:, :, a, 1, :]
        C3 = Ct.unsqueeze(1).broadcast_to([128, H, 16])
        S3 = St.unsqueeze(1).broadcast_to([128, H, 16])
        P.add('dve', lambda e, x1=x1, C3=C3: e.tensor_tensor(out=A3, in0=x1, in1=C3, op=ALU.mult),
              reads=['qn'] + tkeys, writes=['ta'])
        P.add('pool', lambda e, x2=x2, S3=S3: e.tensor_tensor(out=B3, in0=x2, in1=S3, op=ALU.mult),
              reads=['qn'] + tkeys, writes=['tb'])
        P.add('dve', lambda e, a=a: e.tensor_tensor(out=o5[:, :, a, 0, :], in0=A3, in1=B3, op=ALU.subtract),
              reads=['ta', 'tb'], writes=[out_key])
        P.add('dve', lambda e, x1=x1, S3=S3: e.tensor_tensor(out=A3, in0=x1, in1=S3, op=ALU.mult),
              reads=['qn'] + tkeys, writes=['ta'])
        P.add('pool', lambda e, x2=x2, C3=C3: e.tensor_tensor(out=B3, in0=x2, in1=C3, op=ALU.mult),
              reads=['qn'] + tkeys, writes=['tb'])
        P.add('dve', lambda e, a=a: e.tensor_tensor(out=o5[:, :, a, 1, :], in0=A3, in1=B3, op=ALU.add),
              reads=['ta', 'tb'], writes=[out_key])


def phase_attention(nc, xk, xq, winA, g1c, qg, kg, qrow0, yatt, Tk, Town):
    P = Phase(nc, "att")
    NB = Tk // 128
    NQT = Town // 512
    WA = P.sb("WA", [128, NK, 768], BF16)
    stage = [P.sb("stage%d" % i, [128, 768]) for i in range(2)]
    g1t = P.sb("g1t", [128, NK])
    gq = P.sb("gq", [128, 64]); gk = P.sb("gk", [128, 64])
    r0t = P.sb("r0t", [128, 1])
    negm = P.sb("negm", [128, 4])
    idf = P.sb("idf", [128, 128]); idb = P.sb("idb", [128, 128], BF16)
    CtK = P.sb("CtK", [128, NB, 16]); StK = P.sb("StK", [128, NB, 16])
    NQ128 = Town // 128
    CtQ = P.sb("CtQ", [128, NQ128, 16]); StQ = P.sb("StQ", [128, NQ128, 16])
    Ccol = P.sb("Ccol", [128, 16]); Scol = P.sb("Scol", [128, 16])
    wk = dict(pi_i=P.sb("pi_i", [128, 1], I32), pf=P.sb("pf", [128, 4]), inv=P.sb("inv", [128, 16]),
              rowi=P.sb("rowi", [128, 32], I32), rowf=P.sb("rowf", [128, 32]), ang=P.sb("ang", [128, 512]),
              t0=P.sb("t0", [128, 512]), t1=P.sb("t1", [128, 512]), ti=P.sb("ti", [128, 512], I32),
              sq=P.sb("sq", [128, 512]), qn=P.sb("qn", [128, 512]), ta=P.sb("ta", [128, 256]), tb=P.sb("tb", [128, 256]),
              st=P.sb("st", [128, 24]))
    KT = P.sb("KT", [128, Tk], BF16)
    V3 = P.sb("V3", [128, NB, 192], BF16)
    xt = [P.sb("xt%d" % i, [128, D]) for i in range(2)]
    hb = P.sb("hb", [128, D], BF16)
    ss = P.sb("ss", [128, 4])
    hT = P.sb("hT", [128, NK, 128], BF16)
    ko = P.sb("ko", [128, 128], BF16)
    qo = P.sb("qo", [128, 512], BF16)
    QT = P.sb("QT", [128, 4, 512], BF16)
    PT = [P.sb("PT%d" % i, [128, 512], BF16) for i in range(4)]
    rl = P.sb("rl", [128, 1024])
    Yt = [P.sb("Yt%d" % i, [128, 512], BF16) for i in range(2)]
    pT = P.ps("pT", [128, 1024], BF16)
    pJ = P.ps("pJ", [128, 512])
    pS = [P.ps("pS%d" % i, [128, 512]) for i in range(4)]
    pO = [P.ps("pO%d" % i, [128, 512]) for i in range(2)]

    P.add('sp', lambda e: e.dma_start(out=g1t[:], in_=g1c[:, :]), writes=['wscale'], dma=True)
    P.add('sp', lambda e: e.dma_start(out=gq[:], in_=qg.partition_broadcast(128)), writes=['gains'], dma=True)
    P.add('sp', lambda e: e.dma_start(out=gk[:], in_=kg.partition_broadcast(128)), writes=['gains'], dma=True)
    P.add('sp', lambda e: e.dma_start(out=r0t[:], in_=qrow0.partition_broadcast(128)), writes=['row0'], dma=True)
    emit_identity(P, idb, idf)
    load_weight_bf16(P, winA, WA, NK, 768, stage, 'WA', scale_tile=g1t, col_piece=768)
    emit_rope_tables(P, CtK, StK, NB, None, wk, Ccol, Scol)
    emit_rope_tables(P, CtQ, StQ, NQ128, r0t[:, 0:1], wk)
    P.add('dve', lambda e: e.tensor_reduce(out=negm[:, 0:1], in_=gq[:], axis=AX.X, op=ALU.max, apply_absolute_value=True),
          reads=['gains'], writes=['negm'])
    P.add('dve', lambda e: e.tensor_reduce(out=negm[:, 1:2], in_=gk[:], axis=AX.X, op=ALU.max, apply_absolute_value=True),
          reads=['gains'], writes=['negm'])
    P.add('dve', lambda e: e.scalar_tensor_tensor(out=negm[:, 2:3], in0=negm[:, 0:1], scalar=-8.0 * 1.0001, in1=negm[:, 1:2],
                                                  op0=ALU.mult, op1=ALU.mult), writes=['negm'])
    P.add('pool', lambda e: e.memset(V3[:, :, 64:128], 1.0), writes=['V3'])

    for n in range(NB):
        X = xt[n % 2]
        xkey = 'xt%d' % (n % 2)
        P.add('sp', lambda e, X=X, n=n: e.dma_start(out=X[:], in_=xk[n * 128:(n + 1) * 128, :]), writes=[xkey], dma=True)
        emit_norm_hT(P, X[:], xkey, hb, hT[:], 'hT', pT, 'pT', ss, idb)
        for k in range(NK):
            P.add('pe', lambda e, k=k: e.matmul(pJ[:, 0:256], hT[:, k, :], WA[:, k, 512:768], start=(k == 0), stop=(k == NK - 1)),
                  reads=['hT', 'WA'], writes=['pJ'])
        P.add('act', lambda e, n=n: e.activation(out=V3[:, n, :].rearrange("p (a b) -> p a b", b=64)[:, 0:3:2, :],
                                                 in_=pJ[:, 128:256].rearrange("p (a b) -> p a b", b=64), func=AF.Identity),
              writes=['pJ', 'V3'])
        emit_qk_norm_rope(P, pJ[:, 0:128], 'pJ', 2, gk[:], CtK[:, n, :], StK[:, n, :], Ccol[:], Scol[:], 1.0, ko[:], 'ko', wk, ['Ctab', 'Stab'])
        P.add('pe', lambda e: e.transpose(pT[:, 0:128], ko[:], idb[:]), reads=['ko', 'idb'], writes=['pT'])
        P.add('act', lambda e, n=n: e.activation(out=KT[:, n * 128:(n + 1) * 128], in_=pT[:, 0:128], func=AF.Identity),
              writes=['pT', 'KT'])

    npt = [0]
    for qt in range(NQT):
        for j in range(4):
            n = qt * 4 + j
            X = xt[n % 2]
            xkey = 'xt%d' % (n % 2)
            P.add('sp', lambda e, X=X, n=n: e.dma_start(out=X[:], in_=xq[n * 128:(n + 1) * 128, :]), writes=[xkey], dma=True)
            emit_norm_hT(P, X[:], xkey, hb, hT[:], 'hT', pT, 'pT', ss, idb)
            for k in range(NK):
                P.add('pe', lambda e, k=k: e.matmul(pJ[:, :], hT[:, k, :], WA[:, k, 0:512], start=(k == 0), stop=(k == NK - 1)),
                      reads=['hT', 'WA'], writes=['pJ'])
            emit_qk_norm_rope(P, pJ[:, :], 'pJ', 8, gq[:], CtQ[:, n, :], StQ[:, n, :], Ccol[:], Scol[:], HEAD_DIM ** -0.5,
                              qo[:], 'qo', wk, ['Ctab', 'Stab'])
            for g in range(4):
                P.add('pe', lambda e, g=g: e.transpose(pT[:, g * 128:(g + 1) * 128], qo[:, g * 128:(g + 1) * 128], idb[:]),
                      reads=['qo', 'idb'], writes=['pT'])
            P.add('dve', lambda e, j=j: e.tensor_copy(out=QT[:, :, j * 128:(j + 1) * 128],
                                                      in_=pT[:, 0:512].rearrange("p (g t) -> p g t", g=4)),
                  writes=['pT', 'QT'])
        for g in range(4):
            for n in range(NB):
                ia, ib = npt[0] % 4, (npt[0] + 1) % 4
                npt[0] += 2
                P.add('pe', lambda e, g=g, n=n, ia=ia: e.matmul(pS[ia][:, :], KT[0:64, n * 128:(n + 1) * 128], QT[0:64, g, :],
                                                                start=True, stop=True),
                      reads=['KT', 'QT'], writes=['pS%d' % ia])
                P.add('pe', lambda e, g=g, n=n, ib=ib: e.matmul(pS[ib][:, :], KT[64:128, n * 128:(n + 1) * 128], QT[64:128, g, :],
                                                                start=True, stop=True),
                      reads=['KT', 'QT'], writes=['pS%d' % ib])
                P.add('act', lambda e, ia=ia: e.activation(out=PT[ia][:], in_=pS[ia][:], func=AF.Exp, bias=negm[:, 2:3], scale=1.0),
                      reads=['negm'], writes=['pS%d' % ia, 'PT%d' % ia])
                P.add('act', lambda e, ib=ib: e.activation(out=PT[ib][:], in_=pS[ib][:], func=AF.Exp, bias=negm[:, 2:3], scale=1.0),
                      reads=['negm'], writes=['pS%d' % ib, 'PT%d' % ib])
                P.add('pe', lambda e, n=n, ia=ia: e.matmul(pO[0][:, :], V3[:, n, 0:128], PT[ia][:], start=(n == 0), stop=(n == NB - 1)),
                      reads=['V3', 'PT%d' % ia], writes=['pO0'])
                P.add('pe', lambda e, n=n, ib=ib: e.matmul(pO[1][:, :], V3[:, n, 64:192], PT[ib][:], start=(n == 0), stop=(n == NB - 1)),
                      reads=['V3', 'PT%d' % ib], writes=['pO1'])
            Y = Yt[g % 2]
            ykey = 'Yt%d' % (g % 2)
            P.add('dve', lambda e: e.reciprocal(out=rl[64:128, 0:512], in_=pO[0][64:128, :]), writes=['pO0', 'rlA'])
            P.add('dve', lambda e: e.reciprocal(out=rl[0:64, 512:1024], in_=pO[1][0:64, :]), writes=['pO1', 'rlB'])
            P.add('dve', lambda e, Y=Y: e.tensor_tensor(out=Y[0:64, :], in0=pO[0][0:64, :], in1=rl[64:128, 0:512], op=ALU.mult),
                  reads=['rlA'], writes=['pO0', ykey])
            P.add('dve', lambda e, Y=Y: e.tensor_tensor(out=Y[64:128, :], in0=pO[1][64:128, :], in1=rl[0:64, 512:1024], op=ALU.mult),
                  reads=['rlB'], writes=['pO1', ykey])
            P.add('pool', lambda e, Y=Y, g=g, qt=qt: e.dma_start(out=yatt[g, :, qt * 512:(qt + 1) * 512], in_=Y[:]),
                  reads=[ykey], dma=True)
    P.close()


def emit_norm_hT_n(P, X, xkey, np_, hb, hT_dst, hT_key, pT, pT_key, ss, idb):
    P.add('act', lambda e: e.activation(out=hb[0:np_, :], in_=X, func=AF.Square, accum_out=ss[0:np_, 0:1]),
          reads=[xkey], writes=['hb', 'ss'])
    emit_rstd(P, ss[0:np_, 0:1], ss[0:np_, 1:2], ss[0:np_, 2:3], 1.0 / D, NORM_EPS, [], ['ss'])
    P.add('dve', lambda e: e.tensor_scalar(out=hb[0:np_, :], in0=X, scalar1=ss[0:np_, 2:3], scalar2=None, op0=ALU.mult),
          reads=[xkey, 'ss'], writes=['hb'])
    for k in range(NK):
        P.add('pe', lambda e, k=k: e.transpose(pT[:, k * np_:(k + 1) * np_], hb[0:np_, k * 128:(k + 1) * 128],
                                               idb[0:np_, 0:np_]),
              reads=['hb', 'idb'], writes=[pT_key])
    P.add('dve', lambda e: e.tensor_copy(out=hT_dst, in_=pT[:, 0:NK * np_].rearrange("p (k t) -> p k t", k=NK)),
          writes=[pT_key, hT_key])


DECAY_C = float(np.exp(-0.5))


def phase_rwkv(nc, fwd, xw, vmask, Tw, own0, own1, WRd, nch, mupc, munc, g1c, Wld, w0c, a0c, kkc, kac, rkc,
               y_out, s_out, Gld=None, g_out=None, v_out=None):
    P = Phase(nc, "rwf" if fwd else "rwb")
    NSC = Tw // 512
    CL, CG = 12, 13
    WR = P.sb("WR", [128, NK, nch * 128], BF16)
    Wl = P.sb("Wl", [128, 512], BF16)
    Gl = P.sb("Gl", [128, 512], BF16) if fwd else None
    g1t = P.sb("g1t", [128, NK])
    mp = P.sb("mp", [128, 16]); mn = P.sb("mn", [128, 16]); c0 = P.sb("c0", [128, 16])
    cv = P.sb("cv", [128, 20])
    idf = P.sb("idf", [128, 128]); idb = P.sb("idb", [128, 128], BF16)
    ones_bd = P.sb("ones_bd", [128, 128])
    E2 = P.sb("E2", [128, 2], BF16)
    Ms = P.sb("Ms", [128, 128], BF16); Mi = P.sb("Mi", [128, 128], BF16); Mt = P.sb("Mt", [128, 128], BF16)
    rmask = P.sb("rmask", [128, 512])
    vmt = P.sb("vmt", [128, 512])
    xt = [P.sb("xt%d" % i, [128, D]) for i in range(4)]
    xh = P.sb("xh", [2, D])
    hb = P.sb("hb", [128, D], BF16)
    ss = P.sb("ss", [128, 4])
    hTw = P.sb("hTw", [128, NK, 514], BF16)
    tl = P.sb("tl", [128, 32])
    T = {n: P.sb(n, [128, 512]) for n in ("zr", "zk", "zv", "zL", "sw", "aa", "kkr", "sq", "rn", "kd", "kka", "cl", "ex",
                                          "rem", "remx", "ea", "eb")}
    LW = P.sb("LW", [128, 512], BF16)
    sg = P.sb("sg", [128, 512], BF16) if fwd else None
    ARt = [P.sb("ARt%d" % c, [128, 4, 256], BF16) for c in range(4)]
    rk = [P.sb("rk%d" % c, [128, 512], BF16) for c in range(4)]
    wc = P.sb("wc", [128, 4, 4])
    Bt2 = [P.sb("Bt%d" % i, [128, 512], BF16) for i in range(2)]
    Kt2 = [P.sb("Kt%d" % i, [128, 512], BF16) for i in range(2)]
    Bh = P.sb("Bh", [128, 512], BF16); Kh = P.sb("Kh", [128, 512], BF16); vb = P.sb("vb", [128, 512], BF16)
    tok = P.sb("tok", [128, 4, 4, 384], BF16)
    SC_LM = P.sb("SC_LM", [128, 32, 256], BF16)
    SC_MP = P.sb("SC_MP", [128, 32, 256], BF16)
    QP = [P.sb("QP%d" % i, [128, 4, 256], BF16) for i in range(2)]
    QTt = [P.sb("QT%d" % i, [128, 4, 128], BF16) for i in range(2)]
    S32 = P.sb("S32", [128, 4, 128])
    Sbf = P.sb("Sbf", [128, 4, 128], BF16)
    RHSb = P.sb("RHSb", [128, 4, 128], BF16)
    Ub = P.sb("Ub", [128, 4, 128], BF16)
    yt = P.sb("yt", [128, 512])
    st8 = P.sb("st8", [128, 8])
    gt = P.sb("gt", [128, 512]) if fwd else None
    pm = [P.ps("pm%d" % i, [128, 512]) for i in range(2)]
    pc2 = P.ps("pc2", [128, 512])
    pT = P.ps("pT", [128, 1024], BF16)
    pc = [None] * 4 + [P.ps("pc%d" % i, [128, 512]) for i in range(4, 8)]

    P.add('sp', lambda e: e.dma_start(out=g1t[:], in_=g1c[:, :]), writes=['wscale'], dma=True)
    P.add('sp', lambda e: e.dma_start(out=mp[:, 0:nch], in_=mupc[:, :]), writes=['mu'], dma=True)
    P.add('sp', lambda e: e.dma_start(out=mn[:, 0:nch], in_=munc[:, :]), writes=['mu'], dma=True)
    for i, src in enumerate((w0c, a0c, kkc, kac, rkc)):
        P.add('sp', lambda e, i=i, src=src: e.dma_start(out=cv[:, 4 * i:4 * i + 4], in_=src[:, :]), writes=['cv'], dma=True)
    P.add('dve', lambda e: e.tensor_tensor(out=c0[:, 0:nch], in0=mp[:, 0:nch], in1=mn[:, 0:nch], op=ALU.add),
          reads=['mu'], writes=['c0'])
    P.add('dve', lambda e: e.tensor_scalar(out=c0[:, 0:nch], in0=c0[:, 0:nch], scalar1=-1.0, scalar2=1.0, op0=ALU.mult,
                                           op1=ALU.add), writes=['c0'])
    emit_identity(P, idb, idf)
    P.add('pool', lambda e: e.memset(ones_bd[:], 0.0), writes=['ones_bd'])
    P.add('pool', lambda e: e.memset(ones_bd[0:64, 0:64], 1.0), writes=['ones_bd'])
    P.add('pool', lambda e: e.memset(ones_bd[64:128, 64:128], 1.0), writes=['ones_bd'])
    P.add('pool', lambda e: e.memset(E2[:], 0.0), writes=['E2'])
    P.add('pool', lambda e: e.memset(E2[0:64, 0:1], 1.0), writes=['E2'])
    P.add('pool', lambda e: e.memset(E2[64:128, 1:2], 1.0), writes=['E2'])
    for M, strict, transposed in ((Ms, True, False), (Mi, False, False), (Mt, True, True)):
        key = 'masks'
        P.add('pool', lambda e, M=M: e.memset(idf[:], 1.0), writes=['idf'])
        sgn = 1 if (fwd != transposed) else -1
        P.add('pool', lambda e, M=M, sgn=sgn, strict=strict: e.affine_select(
            out=idf[:], in_=idf[:], pattern=[[sgn, 128]], compare_op=(ALU.is_gt if strict else ALU.is_ge), fill=0.0, base=0,
            channel_multiplier=-sgn), writes=['idf'])
        P.add('dve', lambda e, M=M: e.tensor_copy(out=M[:], in_=idf[:]), reads=['idf'], writes=[key])
    P.add('pool', lambda e: e.memset(rmask[:], 1.0), writes=['rmask'])
    P.add('pool', lambda e: e.memset(rmask[:].rearrange("p (j t) -> p j t", t=128)[:, :, 0:1], 0.0), writes=['rmask'])
    for c in range(4):
        P.add('pool', lambda e, c=c: e.memset(S32[:, c, :], 0.0), writes=['S32_%d' % c])
        P.add('pool', lambda e, c=c: e.memset(Sbf[:, c, :], 0.0), writes=['Sbf%d' % c])
    stage = [T["zr"], T["zk"]]
    load_weight_bf16(P, WRd, WR, NK, nch * 128, stage, 'WR', scale_tile=g1t, col_piece=512)
    P.add('sp', lambda e: e.dma_start(out=T["zv"][:], in_=Wld[:, :]), writes=['zv'], dma=True)
    P.add('act', lambda e: e.activation(out=Wl[:], in_=T["zv"][:], func=AF.Identity), reads=['zv'], writes=['Wl'])
    if fwd:
        P.add('sp', lambda e: e.dma_start(out=T["zL"][:], in_=Gld[:, :]), writes=['zL'], dma=True)
        P.add('act', lambda e: e.activation(out=Gl[:], in_=T["zL"][:], func=AF.Identity), reads=['zL'], writes=['Gl'])

    npm = [0]

    def inproj_shift(ci, dst, dkey):
        b = npm[0] % 2
        npm[0] += 1
        pmb, pk = pm[b], 'pm%d' % b
        for k in range(NK):
            P.add('pe', lambda e, k=k: e.matmul(pmb[:, :], WR[:, k, ci * 128:(ci + 1) * 128], hTw[:, k, 0:512],
                                                start=(k == 0), stop=(k == NK - 1)), reads=['WR', 'hTw'], writes=[pk])
        P.add('act', lambda e: e.activation(out=dst[:], in_=pmb[:], func=AF.Identity, scale=c0[:, ci:ci + 1]),
              reads=['c0'], writes=[pk, dkey])
        P.add('dve', lambda e: e.scalar_tensor_tensor(out=dst[:, 1:512], in0=pmb[:, 0:511], scalar=mp[:, ci:ci + 1],
                                                      in1=dst[:, 1:512], op0=ALU.mult, op1=ALU.add),
              reads=['mu'], writes=[pk, dkey])
        P.add('dve', lambda e: e.scalar_tensor_tensor(out=dst[:, 0:511], in0=pmb[:, 1:512], scalar=mn[:, ci:ci + 1],
                                                      in1=dst[:, 0:511], op0=ALU.mult, op1=ALU.add),
              reads=['mu'], writes=[pk, dkey])
        P.add('dve', lambda e: e.scalar_tensor_tensor(out=dst[:, 0:1], in0=tl[:, 2 * ci:2 * ci + 1], scalar=mp[:, ci:ci + 1],
                                                      in1=dst[:, 0:1], op0=ALU.mult, op1=ALU.add),
              reads=['mu', 'tl'], writes=[dkey])
        P.add('dve', lambda e: e.scalar_tensor_tensor(out=dst[:, 511:512], in0=tl[:, 2 * ci + 1:2 * ci + 2],
                                                      scalar=mn[:, ci:ci + 1], in1=dst[:, 511:512], op0=ALU.mult, op1=ALU.add),
              reads=['mu', 'tl'], writes=[dkey])

    order = list(range(NSC)) if fwd else list(range(NSC - 1, -1, -1))
    chunk_order = [0, 1, 2, 3] if fwd else [3, 2, 1, 0]
    def early(sc):
        t0 = sc * 512
        own = (own0 <= t0 < own1)
        for j in range(4):
            r0 = 128 + t0 + j * 128
            P.add('sp', lambda e, j=j, r0=r0: e.dma_start(out=xt[j][:], in_=xw[r0:r0 + 128, :]), writes=['xt%d' % j], dma=True)
        P.add('sp', lambda e: e.dma_start(out=xh[0:1, :], in_=xw[127 + t0:128 + t0, :]), writes=['xh'], dma=True)
        P.add('sp', lambda e: e.dma_start(out=xh[1:2, :], in_=xw[128 + t0 + 512:129 + t0 + 512, :]), writes=['xh'], dma=True)
        P.add('sp', lambda e: e.dma_start(out=vmt[:], in_=vmask[0:1, t0:t0 + 512].partition_broadcast(128)),
              writes=['vmt'], dma=True)
        yield
        for j in range(4):
            emit_norm_hT_n(P, xt[j][:], 'xt%d' % j, 128, hb, hTw[:, :, j * 128:(j + 1) * 128], 'hTw', pT, 'pT', ss, idb)
            yield
        emit_norm_hT_n(P, xh[0:2, :], 'xh', 2, hb, hTw[:, :, 512:514], 'hTw', pT, 'pT', ss, idb)
        yield
        chunks = list(range(13)) + ([CG] if (fwd and own) else [])
        for ci in chunks:
            for k in range(NK):
                P.add('pe', lambda e, k=k, ci=ci: e.matmul(pc2[:, 2 * ci:2 * ci + 2], WR[:, k, ci * 128:(ci + 1) * 128],
                                                           hTw[:, k, 512:514], start=(k == 0), stop=(k == NK - 1)),
                      reads=['WR', 'hTw'], writes=['pc2'])
            yield
        P.add('dve', lambda e: e.tensor_copy(out=tl[:, 0:28], in_=pc2[:, 0:28]), writes=['pc2', 'tl'])
        yield
        inproj_shift(CL, T["zL"], 'zL')
        P.add('act', lambda e: e.activation(out=LW[0:64, :], in_=T["zL"][0:64, :], func=AF.Tanh), reads=['zL'], writes=['LW'])
        P.add('act', lambda e: e.activation(out=LW[64:128, :], in_=T["zL"][64:128, :], func=AF.Identity), reads=['zL'],
              writes=['LW'])
        yield

    def mid(sc):
        t0 = sc * 512
        own = (own0 <= t0 < own1)
        if fwd and own:
            inproj_shift(CG, T["zL"], 'zL')
            P.add('act', lambda e: e.activation(out=sg[:], in_=T["zL"][:], func=AF.Sigmoid), reads=['zL'], writes=['sg'])
        def pair(c):
            zr, zk, zv = T["zr"], T["zk"], T["zv"]
            Bt, Kt, btk, ktk = Bt2[c % 2], Kt2[c % 2], 'Bt%d' % (c % 2), 'Kt%d' % (c % 2)
            inproj_shift(c, zr, 'zr')
            yield
            inproj_shift(4 + c, zk, 'zk')
            yield
            inproj_shift(8 + c, zv, 'zv')
            yield
            cs = slice(c * 128, (c + 1) * 128)
            P.add('pe', lambda e, cs=cs: e.matmul(pm[0][:, :], Wl[0:64, cs], LW[0:64, :], start=True, stop=True),
                  reads=['Wl', 'LW'], writes=['pm0'])
            P.add('pe', lambda e, cs=cs: e.matmul(pm[1][:, :], Wl[64:128, cs], LW[64:128, :], start=True, stop=True),
                  reads=['Wl', 'LW'], writes=['pm1'])
            yield
            sw, aa, kkr, sq, rn, kd, kka, cl, ex, rem, remx, ea, eb = (T[n] for n in (
                "sw", "aa", "kkr", "sq", "rn", "kd", "kka", "cl", "ex", "rem", "remx", "ea", "eb"))
            P.add('act', lambda e, c=c: e.activation(out=sw[:], in_=pm[0][:], func=AF.Sigmoid, bias=cv[:, c:c + 1], scale=1.0),
                  reads=['cv'], writes=['pm0', 'sw'])
            P.add('act', lambda e, c=c: e.activation(out=aa[:], in_=pm[1][:], func=AF.Sigmoid, bias=cv[:, 4 + c:5 + c], scale=1.0),
                  reads=['cv'], writes=['pm1', 'aa'])
            yield
            P.add('pool', lambda e, c=c: e.tensor_scalar(out=kkr[:], in0=zk[:], scalar1=cv[:, 8 + c:9 + c], scalar2=None,
                                                         op0=ALU.mult), reads=['zk', 'cv'], writes=['kkr'])
            P.add('pool', lambda e: e.tensor_tensor(out=sq[:], in0=kkr[:], in1=kkr[:], op=ALU.mult), reads=['kkr'], writes=['sq'])
            P.add('pe', lambda e: e.matmul(pm[0][:, :], ones_bd[:], sq[:], start=True, stop=True),
                  reads=['ones_bd', 'sq'], writes=['pm0'])
            yield
            P.add('act', lambda e: e.activation(out=rn[:], in_=pm[0][:], func=AF.Ln, bias=1e-12, scale=1.0),
                  writes=['pm0', 'rn'])
            P.add('act', lambda e: e.activation(out=rn[:], in_=rn[:], func=AF.Exp, scale=-0.5), writes=['rn'])
            P.add('dve', lambda e: e.tensor_tensor(out=kkr[:], in0=kkr[:], in1=rn[:], op=ALU.mult), reads=['rn'], writes=['kkr'])
            yield
            P.add('dve', lambda e, c=c: e.tensor_scalar(out=kd[:], in0=aa[:], scalar1=-1.0, scalar2=cv[:, 12 + c:13 + c],
                                                        op0=ALU.add, op1=ALU.mult), reads=['aa', 'cv'], writes=['kd'])
            P.add('dve', lambda e: e.scalar_tensor_tensor(out=kd[:], in0=kd[:], scalar=1.0, in1=zk[:], op0=ALU.add,
                                                          op1=ALU.mult), reads=['zk'], writes=['kd'])
            P.add('pool', lambda e: e.tensor_tensor(out=kka[:], in0=kkr[:], in1=aa[:], op=ALU.mult),
                  reads=['kkr', 'aa'], writes=['kka'])
            yield
            P.add('dve', lambda e: e.tensor_tensor_scan(out=cl[:], data0=rmask[:], data1=sw[:], initial=0.0, op0=ALU.mult,
                                                        op1=ALU.add), reads=['rmask', 'sw'], writes=['cl'])
            cl3 = cl[:].rearrange("p (j t) -> p j t", t=128)
            totb = cl3[:, :, 127:128].broadcast_to([128, 4, 128])
            P.add('pool', lambda e: e.tensor_tensor(out=ex[:], in0=cl[:], in1=sw[:], op=ALU.subtract),
                  reads=['cl', 'sw'], writes=['ex'])
            P.add('dve', lambda e: e.tensor_tensor(out=rem[:].rearrange("p (j t) -> p j t", t=128), in0=totb, in1=cl3,
                                                   op=ALU.subtract), reads=['cl'], writes=['rem'])
            if fwd:
                uA, uR, uB, uH = ex, cl, cl, rem
                kA, kR, kB, kH = 'ex', 'cl', 'cl', 'rem'
            else:
                P.add('dve', lambda e: e.tensor_tensor(out=remx[:].rearrange("p (j t) -> p j t", t=128), in0=totb,
                                                       in1=ex[:].rearrange("p (j t) -> p j t", t=128), op=ALU.subtract),
                      reads=['cl', 'ex'], writes=['remx'])
                uA, uR, uB, uH = rem, remx, remx, ex
                kA, kR, kB, kH = 'rem', 'remx', 'remx', 'ex'
            P.add('act', lambda e, c=c: e.activation(out=wc[:, c, :], in_=cl3[:, :, 127], func=AF.Exp, scale=-DECAY_C),
                  reads=['cl'], writes=['wc'])
            yield
            AR = ARt[c]
            arkey = 'ARt%d' % c
            P.add('act', lambda e: e.activation(out=ea[:], in_=uA[:], func=AF.Exp, scale=-DECAY_C), reads=[kA], writes=['ea'])
            P.add('dve', lambda e: e.scalar_tensor_tensor(out=AR[:, :, 0:128], in0=kkr[:].rearrange("p (j t) -> p j t", t=128),
                                                          scalar=-1.0, in1=ea[:].rearrange("p (j t) -> p j t", t=128),
                                                          op0=ALU.mult, op1=ALU.mult), reads=['kkr', 'ea'], writes=[arkey])
            P.add('act', lambda e: e.activation(out=eb[:], in_=uR[:], func=AF.Exp, scale=-DECAY_C), reads=[kR], writes=['eb'])
            P.add('pool', lambda e: e.tensor_tensor(out=AR[:, :, 128:256], in0=zr[:].rearrange("p (j t) -> p j t", t=128),
                                                    in1=eb[:].rearrange("p (j t) -> p j t", t=128), op=ALU.mult),
                  reads=['zr', 'eb'], writes=[arkey])
            yield
            P.add('act', lambda e: e.activation(out=ea[:], in_=uB[:], func=AF.Exp, scale=DECAY_C), reads=[kB], writes=['ea'])
            P.add('pool', lambda e: e.tensor_tensor(out=Bt[:], in0=kka[:], in1=ea[:], op=ALU.mult), reads=['kka', 'ea'], writes=[btk])
            P.add('dve', lambda e: e.tensor_tensor(out=Kt[:], in0=kd[:], in1=ea[:], op=ALU.mult), reads=['kd', 'ea'], writes=[ktk])
            P.add('act', lambda e: e.activation(out=eb[:], in_=uH[:], func=AF.Exp, scale=-DECAY_C), reads=[kH], writes=['eb'])
            P.add('pool', lambda e: e.tensor_tensor(out=Bh[:], in0=kka[:], in1=eb[:], op=ALU.mult), reads=['kka', 'eb'], writes=['Bh'])
            P.add('dve', lambda e: e.tensor_tensor(out=Kh[:], in0=kd[:], in1=eb[:], op=ALU.mult), reads=['kd', 'eb'], writes=['Kh'])
            yield
            P.add('pool', lambda e: e.tensor_tensor(out=vb[:], in0=zv[:], in1=vmt[:], op=ALU.mult), reads=['zv', 'vmt'], writes=['vb'])
            if own:
                P.add('dve', lambda e, c=c: e.scalar_tensor_tensor(out=rk[c][:], in0=zr[:], scalar=cv[:, 16 + c:17 + c],
                                                                   in1=kd[:], op0=ALU.mult, op1=ALU.mult),
                      reads=['zr', 'kd', 'cv'], writes=['rk%d' % c])
            for j in range(4):
                js = slice(j * 128, (j + 1) * 128)
                for q, (src, sk) in enumerate(((vb, 'vb'), (Bh, 'Bh'), (Kh, 'Kh'))):
                    P.add('pe', lambda e, q=q, src=src, js=js: e.transpose(pT[:, q * 128:(q + 1) * 128], src[:, js], idb[:]),
                          reads=[sk, 'idb'], writes=['pT'])
                P.add('dve', lambda e, c=c, j=j: e.tensor_copy(out=tok[:, c, j, :], in_=pT[:, 0:384]),
                      writes=['pT', 'tok%d' % c])
                yield
            yield
        def head(c, hh):
            Bt, Kt, btk, ktk = Bt2[c % 2], Kt2[c % 2], 'Bt%d' % (c % 2), 'Kt%d' % (c % 2)
            AR = ARt[c]
            arkey = 'ARt%d' % c
            rows = slice(hh * 64, (hh + 1) * 64)
            u0 = (c * 2 + hh) * 4
            for j in range(4):
                js = slice(j * 128, (j + 1) * 128)
                P.add('pe', lambda e, j=j, js=js: e.matmul(pc[4][:, js], Bt[rows, js], AR[rows, j, 0:128], start=True, stop=True),
                      reads=[btk, arkey], writes=['pc4'])
                P.add('pe', lambda e, j=j, js=js: e.matmul(pc[5][:, js], Bt[rows, js], AR[rows, j, 128:256], start=True, stop=True),
                      reads=[btk, arkey], writes=['pc5'])
                P.add('pe', lambda e, j=j, js=js: e.matmul(pc[6 + j // 2][:, (j % 2) * 256:(j % 2) * 256 + 256], Kt[rows, js],
                                                           AR[rows, j, :], start=True, stop=True),
                      reads=[ktk, arkey], writes=['pc%d' % (6 + j // 2)])
                P.add('pe', lambda e, j=j, js=js: e.matmul(pc2[:, js], AR[rows, j, 0:128], Bt[rows, js], start=True, stop=True),
                      reads=[btk, arkey], writes=['pc2'])
            yield
            Msb = Ms[:].unsqueeze(1).broadcast_to([128, 4, 128])
            Mib = Mi[:].unsqueeze(1).broadcast_to([128, 4, 128])
            Mtb = Mt[:].unsqueeze(1).broadcast_to([128, 4, 128])
            P.add('dve', lambda e: e.tensor_tensor(out=QP[0][:, :, 0:128], in0=pc[4][:].rearrange("p (u t) -> p u t", t=128),
                                                   in1=Msb, op=ALU.mult), reads=['masks'], writes=['pc4', 'QP0_0', 'QP0_1'])
            P.add('dve', lambda e, u0=u0: e.tensor_tensor(out=SC_MP[:, u0:u0 + 4, 0:128],
                                                          in0=pc[5][:].rearrange("p (u t) -> p u t", t=128), in1=Mib, op=ALU.mult),
                  reads=['masks'], writes=['pc5', 'SC_MP'])
            for half in range(2):
                M2s = Ms[:].unsqueeze(1).broadcast_to([128, 2, 128])
                M2i = Mi[:].unsqueeze(1).broadcast_to([128, 2, 128])
                pcb = pc[6 + half]
                P.add('dve', lambda e, u0=u0, half=half, pcb=pcb, M2s=M2s: e.tensor_tensor(
                    out=SC_LM[:, u0 + 2 * half:u0 + 2 * half + 2, 0:128],
                    in0=pcb[:].rearrange("p (u t) -> p u t", t=256)[:, :, 0:128], in1=M2s, op=ALU.mult),
                    reads=['masks'], writes=['pc%d' % (6 + half), 'SC_LM'])
                P.add('dve', lambda e, u0=u0, half=half, pcb=pcb, M2i=M2i: e.tensor_tensor(
                    out=SC_LM[:, u0 + 2 * half:u0 + 2 * half + 2, 128:256],
                    in0=pcb[:].rearrange("p (u t) -> p u t", t=256)[:, :, 128:256], in1=M2i, op=ALU.mult),
                    reads=['masks'], writes=['pc%d' % (6 + half), 'SC_LM'])
            P.add('dve', lambda e: e.tensor_tensor(out=QTt[0][:], in0=pc2[:].rearrange("p (u t) -> p u t", t=128), in1=Mtb,
                                                   op=ALU.mult), reads=['masks'], writes=['pc2', 'QT0_0', 'QT0_1'])
            yield
            P.add('pool', lambda e: e.tensor_tensor(out=QP[1][:, :, 128:256], in0=QP[0][:, :, 0:128],
                                                    in1=idb[:].unsqueeze(1).broadcast_to([128, 4, 128]), op=ALU.add),
                  reads=['idb', 'QP0_0', 'QP0_1'], writes=['QP1_0', 'QP1_1'])
            pB = (pc2, pc[5])
            pBk = ('pc2', 'pc5')

            def level(lvl, cur, hf):
                nxt = 1 - cur
                ck, nk = 'QP%d_%d' % (cur, hf), 'QP%d_%d' % (nxt, hf)
                ctk, ntk = 'QT%d_%d' % (cur, hf), 'QT%d_%d' % (nxt, hf)
                pa, pak = pc[6 + hf], 'pc%d' % (6 + hf)
                us = slice(2 * hf, 2 * hf + 2)
                for uu in range(2):
                    u = 2 * hf + uu
                    if lvl == 0:
                        P.add('pe', lambda e, u=u, uu=uu: e.matmul(pa[:, uu * 256:uu * 256 + 128], QTt[0][:, u, :], QP[0][:, u, 0:128],
                                                                   start=True, stop=True), reads=[ck, ctk], writes=[pak])
                        P.add('pe', lambda e, u=u, uu=uu: e.matmul(pB[hf][:, uu * 128:(uu + 1) * 128], QP[0][:, u, 0:128], QTt[0][:, u, :],
                                                                   start=True, stop=True), reads=[ck, ctk], writes=[pBk[hf]])
                    elif lvl < 6:
                        P.add('pe', lambda e, u=u, uu=uu: e.matmul(pa[:, uu * 256:uu * 256 + 256], QTt[cur][:, u, :], QP[cur][:, u, :],
                                                                   start=True, stop=True), reads=[ck, ctk], writes=[pak])
                        P.add('pe', lambda e, u=u, uu=uu: e.matmul(pB[hf][:, uu * 128:(uu + 1) * 128], QP[cur][:, u, 0:128],
                                                                   QTt[cur][:, u, :], start=True, stop=True),
                              reads=[ck, ctk], writes=[pBk[hf]])
                    else:
                        P.add('pe', lambda e, u=u, uu=uu: e.matmul(pa[:, uu * 256 + 128:uu * 256 + 256], QTt[cur][:, u, :],
                                                                   QP[cur][:, u, 128:256], start=True, stop=True),
                              reads=[ck, ctk], writes=[pak])
                yield
                pv = pa[:].rearrange("p (u t) -> p u t", t=256)
                if lvl == 0:
                    P.add('act', lambda e: e.activation(out=QP[1][:, us, 0:128], in_=pv[:, :, 0:128], func=AF.Identity),
                          writes=[pak, nk])
                    P.add('act', lambda e: e.activation(out=QTt[1][:, us, :], in_=pB[hf][:, 0:256].rearrange("p (u t) -> p u t", t=128),
                                                        func=AF.Identity), writes=[pBk[hf], ntk])
                elif lvl < 6:
                    P.add('act', lambda e: e.activation(out=QP[nxt][:, us, 0:128], in_=pv[:, :, 0:128], func=AF.Identity),
                          writes=[pak, nk])
                    P.add('dve', lambda e: e.tensor_tensor(out=QP[nxt][:, us, 128:256], in0=pv[:, :, 128:256],
                                                           in1=QP[cur][:, us, 128:256], op=ALU.add), reads=[ck], writes=[pak, nk])
                    P.add('act', lambda e: e.activation(out=QTt[nxt][:, us, :], in_=pB[hf][:, 0:256].rearrange("p (u t) -> p u t", t=128),
                                                        func=AF.Identity), writes=[pBk[hf], ntk])
                else:
                    P.add('dve', lambda e: e.tensor_tensor(out=SC_MP[:, u0 + 2 * hf:u0 + 2 * hf + 2, 128:256], in0=pv[:, :, 128:256],
                                                           in1=QP[cur][:, us, 128:256], op=ALU.add), reads=[ck], writes=[pak, 'SC_MP'])
                yield

            def half_chain(hf):
                cur = 0
                for lvl in range(7):
                    yield from level(lvl, cur, hf)
                    cur = 1 - cur
            g0, g1 = half_chain(0), half_chain(1)
            live = [g0, g1]
            while live:
                for g_ in list(live):
                    try:
                        next(g_)
                    except StopIteration:
                        live.remove(g_)
                yield

        def run_interleaved(gens):
            gens = list(gens)
            while gens:
                for g_ in list(gens):
                    try:
                        next(g_)
                    except StopIteration:
                        gens.remove(g_)

        def chain_b(c):
            yield from head(c, 0)
            yield from head(c, 1)
        run_interleaved([pair(0)])
        for c in range(1, 4):
            run_interleaved([pair(c), chain_b(c - 1)])
        run_interleaved([chain_b(3)])

    def state(sc):
        t0 = sc * 512
        own = (own0 <= t0 < own1)
        for j in chunk_order:
            js = slice(j * 128, (j + 1) * 128)
            for c in range(4):
                pcc, pk = pc[4 + c], 'pc%d' % (4 + c)
                P.add('pe', lambda e, c=c, j=j, pcc=pcc: e.matmul(pcc[:, 0:128], ARt[c][:, j, 0:128], Sbf[:, c, :], start=True, stop=False),
                      reads=['ARt%d' % c, 'Sbf%d' % c], writes=[pk])
                for hh in range(2):
                    u = (c * 2 + hh) * 4 + j
                    hs = slice(hh * 64, (hh + 1) * 64)
                    P.add('pe', lambda e, c=c, j=j, u=u, hs=hs, hh=hh, pcc=pcc: e.matmul(pcc[:, hs], SC_LM[:, u, 0:128], tok[:, c, j, hs],
                                                                                 start=False, stop=(hh == 1)),
                          reads=['SC_LM', 'tok%d' % c], writes=[pk])
                P.add('act', lambda e, c=c, pcc=pcc: e.activation(out=RHSb[:, c, :], in_=pcc[:, 0:128], func=AF.Identity),
                      writes=[pk, 'RHSb%d' % c])
                yield
            for c in range(4):
                pcc, pk = pc[4 + c], 'pc%d' % (4 + c)
                for hh in range(2):
                    u = (c * 2 + hh) * 4 + j
                    hs = slice(hh * 64, (hh + 1) * 64)
                    P.add('pe', lambda e, c=c, u=u, hs=hs, hh=hh, pcc=pcc: e.matmul(pcc[:, 128 + hh * 64:192 + hh * 64], SC_MP[:, u, 128:256],
                                                                            RHSb[:, c, hs], start=True, stop=True),
                          reads=['SC_MP', 'RHSb%d' % c], writes=[pk])
                P.add('dve', lambda e, c=c, pcc=pcc: e.tensor_copy(out=Ub[:, c, :], in_=pcc[:, 128:256]), writes=[pk, 'Ub%d' % c])
                yield
            for c in range(4):
                pcc, pk = pc[4 + c], 'pc%d' % (4 + c)
                if own:
                    P.add('pe', lambda e, c=c, j=j, pcc=pcc: e.matmul(pcc[:, 256:384], ARt[c][:, j, 128:256], Sbf[:, c, :], start=True, stop=False),
                          reads=['ARt%d' % c, 'Sbf%d' % c], writes=[pk])
                    for hh in range(2):
                        u = (c * 2 + hh) * 4 + j
                        hs = slice(hh * 64, (hh + 1) * 64)
                        os_ = slice(256 + hh * 64, 320 + hh * 64)
                        P.add('pe', lambda e, c=c, u=u, hs=hs, os_=os_, pcc=pcc: e.matmul(pcc[:, os_], SC_MP[:, u, 0:128], Ub[:, c, hs],
                                                                                  start=False, stop=False),
                              reads=['SC_MP', 'Ub%d' % c], writes=[pk])
                        P.add('pe', lambda e, c=c, j=j, u=u, hs=hs, os_=os_, hh=hh, pcc=pcc: e.matmul(pcc[:, os_], SC_LM[:, u, 128:256], tok[:, c, j, hs],
                                                                                              start=False, stop=(hh == 1)),
                              reads=['SC_LM', 'tok%d' % c], writes=[pk])
                P.add('pe', lambda e, c=c, j=j, pcc=pcc: e.matmul(pcc[:, 384:512], tok[:, c, j, 128:256], Ub[:, c, :], start=True, stop=False),
                      reads=['tok%d' % c, 'Ub%d' % c], writes=[pk])
                P.add('pe', lambda e, c=c, j=j, pcc=pcc: e.matmul(pcc[:, 384:512], tok[:, c, j, 256:384], tok[:, c, j, 0:128], start=False, stop=True),
                      reads=['tok%d' % c], writes=[pk])
                if own:
                    P.add('act', lambda e, c=c, pcc=pcc: e.activation(out=yt[:, c * 128:(c + 1) * 128], in_=pcc[:, 256:384], func=AF.Identity),
                          writes=[pk, 'yt'])
                for hh in range(2):
                    hs = slice(hh * 64, (hh + 1) * 64)
                    P.add('dve', lambda e, c=c, j=j, hs=hs, hh=hh, pcc=pcc: e.scalar_tensor_tensor(
                        out=S32[hs, c, hs], in0=S32[hs, c, hs], scalar=wc[hs, c, j:j + 1], in1=pcc[hs, 384 + hh * 64:448 + hh * 64],
                        op0=ALU.mult, op1=ALU.add), reads=['wc'], writes=[pk, 'S32_%d' % c])
                P.add('act', lambda e, c=c: e.activation(out=Sbf[:, c, :], in_=S32[:, c, :], func=AF.Identity),
                      reads=['S32_%d' % c], writes=['Sbf%d' % c])
                yield
            if own:
                tr = t0 + j * 128 - own0
                for c in range(4):
                    P.add('pe', lambda e, c=c, js=js: e.matmul(pc2[:, 32 + 2 * c:34 + 2 * c], rk[c][:, js], E2[:], start=True, stop=True),
                          reads=['rk%d' % c, 'E2'], writes=['pc2'])
                P.add('dve', lambda e: e.tensor_copy(out=st8[:], in_=pc2[:, 32:40]), writes=['pc2', 'st8'])
                P.add('pool', lambda e, tr=tr: e.dma_start(out=y_out[tr:tr + 128, :], in_=yt[:]), reads=['yt'], dma=True)
                P.add('pool', lambda e, tr=tr: e.dma_start(out=s_out[tr:tr + 128, :], in_=st8[:]), reads=['st8'], dma=True)
                if fwd:
                    P.add('pe', lambda e, js=js: e.matmul(pm[0][:, :], sg[:, js], Gl[:], start=True, stop=True),
                          reads=['sg', 'Gl'], writes=['pm0'])
                    P.add('act', lambda e: e.activation(out=gt[:], in_=pm[0][:], func=AF.Identity), writes=['pm0', 'gt'])
                    P.add('pool', lambda e, tr=tr: e.dma_start(out=g_out[tr:tr + 128, :], in_=gt[:]), reads=['gt'], dma=True)
                    P.add('pool', lambda e, tr=tr, j=j: e.dma_start(
                        out=v_out[tr:tr + 128, :].rearrange("t (c v) -> t c v", c=4), in_=tok[:, :, j, 0:128]),
                        reads=['tok0', 'tok1', 'tok2', 'tok3'], dma=True)
            yield

    def run_il(gens):
        gens = list(gens)
        while gens:
            for g_ in list(gens):
                try:
                    next(g_)
                except StopIteration:
                    gens.remove(g_)
    prev = None
    for sc in order:
        run_il([early(sc)] if prev is None else [state(prev), early(sc)])
        mid(sc)
        prev = sc
    run_il([state(prev)])
    P.close()


def phase_assembly(nc, Town, yf, yb, sf, sbk, gd_, vd, yatt, xown, woutd, lnxw, lnxb, x1out):
    P = Phase(nc, "asm")
    NT = Town // 128
    Wo = P.sb("Wo", [128, 8, D], BF16)
    stage = [P.sb("stage%d" % i, [128, D]) for i in range(2)]
    lw = P.sb("lw", [128, 512]); lb = P.sb("lb", [128, 512])
    idf = P.sb("idf", [128, 128]); idb = P.sb("idb", [128, 128], BF16)
    yft = P.sb("yft", [128, 512]); ybt = P.sb("ybt", [128, 512]); gt = P.sb("gt", [128, 512])
    sq = P.sb("sq", [128, 512]); bon = P.sb("bon", [128, 512])
    vt = P.sb("vt", [128, 512], BF16)
    s8 = P.sb("s8", [128, 48])
    xo = P.sb("xo", [128, D])
    yr = P.sb("yr", [128, 512], BF16)
    ycT = P.sb("ycT", [128, 8, 128], BF16)
    pT = P.ps("pT", [128, 1024], BF16)
    pO = [P.ps("pO%d" % i, [128, 512]) for i in range(2)]
    P.add('sp', lambda e: e.dma_start(out=lw[:], in_=lnxw.partition_broadcast(128)), writes=['lw'], dma=True)
    P.add('sp', lambda e: e.dma_start(out=lb[:], in_=lnxb.partition_broadcast(128)), writes=['lb'], dma=True)
    emit_identity(P, idb, idf)
    load_weight_bf16(P, woutd, Wo, 8, D, stage, 'Wo', scale_tile=None, col_piece=D)

    def tile(n):
        r = slice(n * 128, (n + 1) * 128)
        P.add('sp', lambda e: e.dma_start(out=yft[:], in_=yf[r, :]), writes=['yft'], dma=True)
        P.add('sp', lambda e: e.dma_start(out=ybt[:], in_=yb[r, :]), writes=['ybt'], dma=True)
        P.add('sp', lambda e: e.dma_start(out=gt[:], in_=gd_[r, :]), writes=['gt'], dma=True)
        P.add('sp', lambda e: e.dma_start(out=vt[:], in_=vd[r, :]), writes=['vt'], dma=True)
        P.add('sp', lambda e: e.dma_start(out=s8[:, 0:8], in_=sf[r, :]), writes=['s8a'], dma=True)
        P.add('sp', lambda e: e.dma_start(out=s8[:, 8:16], in_=sbk[r, :]), writes=['s8b'], dma=True)
        P.add('sp', lambda e: e.dma_start(out=xo[:], in_=xown[r, :]), writes=['xo'], dma=True)
        P.add('sp', lambda e: e.dma_start(out=ycT[:, 4:8, :], in_=yatt[:, :, r].rearrange("g p t -> p g t")),
              writes=['ycTa'], dma=True)
        y3 = yft[:].rearrange("p (h c) -> p h c", c=64)

        def b8(col):
            return s8[:, col:col + 8].unsqueeze(2).broadcast_to([128, 8, 64])
        P.add('dve', lambda e: e.tensor_tensor(out=yft[:], in0=yft[:], in1=ybt[:], op=ALU.add), reads=['ybt'], writes=['yft'])
        P.add('dve', lambda e: e.tensor_reduce(out=s8[:, 16:24], in_=y3, axis=AX.X, op=ALU.add), reads=['yft'], writes=['s8c'])
        P.add('dve', lambda e: e.tensor_scalar(out=s8[:, 16:24], in0=s8[:, 16:24], scalar1=1.0 / 64, scalar2=None, op0=ALU.mult),
              writes=['s8c'])
        P.add('dve', lambda e: e.tensor_tensor(out=y3, in0=y3, in1=b8(16), op=ALU.subtract), reads=['s8c'], writes=['yft'])
        P.add('pool', lambda e: e.tensor_tensor(out=sq[:], in0=yft[:], in1=yft[:], op=ALU.mult), reads=['yft'], writes=['sq'])
        P.add('dve', lambda e: e.tensor_reduce(out=s8[:, 24:32], in_=sq[:].rearrange("p (h c) -> p h c", c=64), axis=AX.X,
                                               op=ALU.add), reads=['sq'], writes=['s8d'])
        emit_rstd(P, s8[:, 24:32], s8[:, 32:40], s8[:, 40:48], 1.0 / 64, LNX_EPS, ['s8d'], ['s8e'])
        P.add('dve', lambda e: e.tensor_tensor(out=y3, in0=y3, in1=b8(40), op=ALU.mult), reads=['s8e'], writes=['yft'])
        P.add('dve', lambda e: e.tensor_tensor(out=yft[:], in0=yft[:], in1=lw[:], op=ALU.mult), reads=['lw'], writes=['yft'])
        P.add('dve', lambda e: e.tensor_tensor(out=yft[:], in0=yft[:], in1=lb[:], op=ALU.add), reads=['lb'], writes=['yft'])
        P.add('dve', lambda e: e.tensor_tensor(out=s8[:, 0:8], in0=s8[:, 0:8], in1=s8[:, 8:16], op=ALU.add), reads=['s8b'],
              writes=['s8a'])
        P.add('dve', lambda e: e.scalar_tensor_tensor(out=bon[:].rearrange("p (h c) -> p h c", c=64),
                                                      in0=vt[:].rearrange("p (h c) -> p h c", c=64), scalar=0.5, in1=b8(0),
                                                      op0=ALU.mult, op1=ALU.mult), reads=['vt', 's8a'], writes=['bon'])
        P.add('pool', lambda e: e.tensor_tensor(out=yft[:], in0=yft[:], in1=bon[:], op=ALU.add), reads=['bon'], writes=['yft'])
        P.add('dve', lambda e: e.tensor_tensor(out=yr[:], in0=yft[:], in1=gt[:], op=ALU.mult), reads=['yft', 'gt'], writes=['yr'])
        for c in range(4):
            P.add('pe', lambda e, c=c: e.transpose(pT[:, c * 128:(c + 1) * 128], yr[:, c * 128:(c + 1) * 128], idb[:]),
                  reads=['yr', 'idb'], writes=['pT'])
        P.add('act', lambda e: e.activation(out=ycT[:, 0:4, :], in_=pT[:, 0:512].rearrange("p (c t) -> p c t", c=4),
                                            func=AF.Identity), writes=['pT', 'ycTr'])
        for half in range(2):
            for ch in range(8):
                P.add('pe', lambda e, half=half, ch=ch: e.matmul(pO[half][:, :], ycT[:, ch, :], Wo[:, ch, half * 512:(half + 1) * 512],
                                                                 start=(ch == 0), stop=(ch == 7)),
                      reads=['ycTr', 'ycTa', 'Wo'], writes=['pO%d' % half])
            P.add('dve', lambda e, half=half: e.tensor_tensor(out=xo[:, half * 512:(half + 1) * 512], in0=pO[half][:],
                                                              in1=xo[:, half * 512:(half + 1) * 512], op=ALU.add),
                  writes=['pO%d' % half, 'xo'])
        P.add('pool', lambda e: e.dma_start(out=x1out[r, :], in_=xo[:]), reads=['xo'], dma=True)
    for n in range(NT):
        tile(n)
    P.close()


def _cols(vec, nchunk):
    return np.ascontiguousarray(np.asarray(vec, np.float32).reshape(nchunk, 128).T)


def host_prep(p):
    f = lambda a: np.asarray(a, np.float32)
    w_in = f(p['w_in'])[0]
    mu_p = f(p['mu_prev'])[0]
    mu_n = f(p['mu_next'])[0]
    r_, k_, v_ = slice(0, 512), slice(512, 1024), slice(1024, 1536)
    wdf, wdb, adf, adb, gdc = slice(1536, 1600), slice(1600, 1664), slice(1664, 1728), slice(1728, 1792), slice(1792, 1920)

    def cat(a, sl):
        return np.concatenate([a[..., s] for s in sl], axis=-1)
    slf = [r_, k_, v_, wdf, adf, gdc]
    slb = [r_, k_, v_, wdb, adb]
    qcols = np.concatenate([np.arange(1920 + h * 64, 1920 + (h + 1) * 64) for h in QPERM])
    out = {}
    out['WRf'] = np.ascontiguousarray(cat(w_in, slf)); out['WRb'] = np.ascontiguousarray(cat(w_in, slb))
    out['mupf'] = _cols(cat(mu_p, slf), 14); out['munf'] = _cols(cat(mu_n, slf), 14)
    out['mupb'] = _cols(cat(mu_p, slb), 13); out['munb'] = _cols(cat(mu_n, slb), 13)
    out['winA'] = np.ascontiguousarray(np.concatenate([w_in[:, qcols], w_in[:, 2432:2688]], axis=1))
    out['g1c'] = _cols(f(p['norm1_g'])[0], 8)
    out['Wlf'] = np.ascontiguousarray(np.concatenate([f(p['w_lora_f'])[0], f(p['a_lora_f'])[0]], axis=0))
    out['Wlb'] = np.ascontiguousarray(np.concatenate([f(p['w_lora_b'])[0], f(p['a_lora_b'])[0]], axis=0))
    out['Gl'] = np.ascontiguousarray(f(p['g_lora'])[0])
    for nm in ('w0_f', 'w0_b', 'a0_f', 'a0_b', 'k_k', 'k_a'):
        out[nm] = _cols(f(p[nm])[0], 4)
    out['r_k'] = _cols(f(p['r_k'])[0].reshape(512), 4)
    out['lnxw'] = np.ascontiguousarray(f(p['lnx_w'])[0].reshape(1, 512)); out['lnxb'] = np.ascontiguousarray(f(p['lnx_b'])[0].reshape(1, 512))
    out['qg'] = np.ascontiguousarray(f(p['q_gain'])[0].reshape(1, 64)); out['kg'] = np.ascontiguousarray(f(p['k_gain'])[0].reshape(1, 64))
    w_out = f(p['w_out'])[0]
    arows = np.concatenate([np.arange(512 + h * 64, 512 + (h + 1) * 64) for h in QPERM])
    out['wout'] = np.ascontiguousarray(np.concatenate([w_out[0:512], w_out[arows]], axis=0))
    out['g2c'] = _cols(f(p['norm2_g'])[0], 8)
    out['gf'] = np.ascontiguousarray(f(p['norm_f_g']).reshape(1, D))
    out['wg'] = np.ascontiguousarray(f(p['ffn_gate'])[0]); out['wu'] = np.ascontiguousarray(f(p['ffn_up'])[0])
    out['wd'] = np.ascontiguousarray(f(p['ffn_down'])[0])
    return out


WEIGHT_SPECS = [('WRf', [D, 1792]), ('WRb', [D, 1664]), ('mupf', [128, 14]), ('munf', [128, 14]), ('mupb', [128, 13]),
                ('munb', [128, 13]), ('winA', [D, 768]), ('g1c', [128, 8]), ('Wlf', [128, 512]), ('Wlb', [128, 512]),
                ('Gl', [128, 512]), ('w0_f', [128, 4]), ('w0_b', [128, 4]), ('a0_f', [128, 4]), ('a0_b', [128, 4]),
                ('k_k', [128, 4]), ('k_a', [128, 4]), ('r_k', [128, 4]), ('lnxw', [1, 512]), ('lnxb', [1, 512]),
                ('qg', [1, 64]), ('kg', [1, 64]), ('wout', [D, D]), ('g2c', [128, 8]), ('gf', [1, D]),
                ('wg', [D, DFF]), ('wu', [D, DFF]), ('wd', [DFF, D])]


def declare_weights(nc):
    return {n: nc.dram_tensor(n, s, F32, kind="ExternalInput").ap() for n, s in WEIGHT_SPECS}


def mixer_job(nc, W, tag, xw, vmask, Tw_f, own_f, Tw_b, own_b, xw_f_off, xw_b_off, xk, Tk, xown, Town, qrow0, x1rows, scr):
    phase_attention(nc, xk, xown, W['winA'], W['g1c'], W['qg'], W['kg'], qrow0, scr['yatt'], Tk, Town)
    phase_rwkv(nc, False, xw[xw_b_off:xw_b_off + Tw_b + 256, :], vmask[:, xw_b_off:xw_b_off + Tw_b], Tw_b, own_b[0], own_b[1],
               W['WRb'], 13, W['mupb'], W['munb'], W['g1c'], W['Wlb'], W['w0_b'], W['a0_b'], W['k_k'], W['k_a'], W['r_k'],
               scr['yb'], scr['sb'])
    phase_rwkv(nc, True, xw[xw_f_off:xw_f_off + Tw_f + 256, :], vmask[:, xw_f_off:xw_f_off + Tw_f], Tw_f, own_f[0], own_f[1],
               W['WRf'], 14, W['mupf'], W['munf'], W['g1c'], W['Wlf'], W['w0_f'], W['a0_f'], W['k_k'], W['k_a'], W['r_k'],
               scr['yf'], scr['sf'], Gld=W['Gl'], g_out=scr['g'], v_out=scr['v'])
    phase_assembly(nc, Town, scr['yf'], scr['yb'], scr['sf'], scr['sb'], scr['g'], scr['v'], scr['yatt'], xown, W['wout'],
                   W['lnxw'], W['lnxb'], x1rows)


def phase_ffn(nc, x, y, g2c, gf, wg, wu, wd, ntok):
    Phase._n[0] += 1
    tagp = "ffn%d_" % Phase._n[0]
    ngroups = ntok // GT
    xg = x.rearrange("(n s p) d -> n p s d", s=NSUB, p=128)
    yg = y.rearrange("(n s p) d -> n p s d", s=NSUB, p=128)

    es = contextlib.ExitStack()
    with es:
        def sb(name, shape, dt=F32):
            return es.enter_context(nc.sbuf_tensor(tagp + name, shape, dt))

        def pt(name, shape, dt=F32):
            return es.enter_context(nc.psum_tensor(tagp + name, shape, dt))

        Wg = sb("Wg", [128, NK, DFF], BF16)
        Wu = sb("Wu", [128, NK, DFF], BF16)
        Wd = sb("Wd", [128, NFF, D], BF16)
        stage = [sb("stage%d" % i, [128, 1024]) for i in range(2)]
        g2t = sb("g2t", [128, NK])
        gft = sb("gft", [128, D])
        idf = sb("idf", [128, 128])
        idb = sb("idb", [128, 128], BF16)
        xt = [sb("xt%d" % i, [128, NSUB, D]) for i in range(2)]
        hb = sb("hb", [128, NSUB, D], BF16)
        hT = sb("hT", [128, NK, GT], BF16)
        actT = sb("actT", [128, NFF, GT], BF16)
        tmp = [sb("tmp%d" % i, [128, GT]) for i in range(2)]
        ot = sb("ot", [128, NSUB, D])
        ss = sb("ss", [128, 4 * NSUB])
        pst = [pt("pst%d" % i, [128, 1024], BF16)[:, 0:GT] for i in range(2)]
        psg = [pt("psg%d" % i, [128, 512])[:, 0:GT] for i in range(2)]
        psu = [pt("psu%d" % i, [128, 512])[:, 0:GT] for i in range(2)]
        psd = [pt("psd%d" % i, [128, 512]) for i in range(2)]

        S = Sched(nc)

        S.add('sp', lambda e: e.dma_start(out=g2t[:], in_=g2c[:, :]), writes=['g2t'], dma=True)
        S.add('sp', lambda e: e.dma_start(out=gft[:], in_=gf.partition_broadcast(128)), writes=['gft'], dma=True)
        S.add('pool', lambda e: e.memset(idf[:], 1.0), writes=['idf'])
        S.add('pool', lambda e: e.affine_select(out=idf[:], in_=idf[:], pattern=[[-1, 128]], compare_op=ALU.is_equal,
                                                 fill=0.0, base=0, channel_multiplier=1), writes=['idf'])
        S.add('dve', lambda e: e.tensor_copy(out=idb[:], in_=idf[:]), reads=['idf'], writes=['idb'])

        nst = [0]

        def load_cast(src_ap, dst_ap, ncols, dkey, scale_ap=None):
            i = nst[0] % 2
            nst[0] += 1
            skey = 'stage%d' % i
            S.add('sp', lambda e: e.dma_start(out=stage[i][:, 0:ncols], in_=src_ap), writes=[skey], dma=True)
            if scale_ap is not None:
                S.add('dve', lambda e: e.tensor_scalar(out=dst_ap, in0=stage[i][:, 0:ncols], scalar1=scale_ap, scalar2=None,
                                                        op0=ALU.mult), reads=[skey, 'g2t'], writes=[dkey])
            else:
                S.add('act', lambda e: e.activation(out=dst_ap, in_=stage[i][:, 0:ncols], func=AF.Identity),
                      reads=[skey], writes=[dkey])

        pieces = [(0, 1024), (1024, 1024), (2048, DFF - 2048)]
        for k in range(NK):
            for (c0, cn) in pieces:
                load_cast(wg[k * 128:(k + 1) * 128, c0:c0 + cn], Wg[:, k, c0:c0 + cn], cn, 'Wg', g2t[:, k:k + 1])
                load_cast(wu[k * 128:(k + 1) * 128, c0:c0 + cn], Wu[:, k, c0:c0 + cn], cn, 'Wu', g2t[:, k:k + 1])
        for f in range(NFF):
            load_cast(wd[f * 128:(f + 1) * 128, :], Wd[:, f, :], D, 'Wd')

        nps = [0, 0, 0]
        for g in range(ngroups):
            X = xt[g % 2]
            xk = 'xt%d' % (g % 2)
            S.add('sp', lambda e, X=X, g=g: e.dma_start(out=X[:], in_=xg[g]), writes=[xk], dma=True)
            for s in range(NSUB):
                S.add('act', lambda e, X=X, s=s: e.activation(out=hb[:, s, :], in_=X[:, s, :], func=AF.Square,
                                                              accum_out=ss[:, s:s + 1]),
                      reads=[xk], writes=['hb', 'ss'])
            S.add('act', lambda e: e.activation(out=ss[:, NSUB:2 * NSUB], in_=ss[:, 0:NSUB], func=AF.Ln,
                                                scale=1.0 / D, bias=NORM_EPS), reads=[], writes=['ss'])
            S.add('act', lambda e: e.activation(out=ss[:, 0:NSUB], in_=ss[:, NSUB:2 * NSUB], func=AF.Exp, scale=-0.5),
                  reads=[], writes=['ss'])
            for s in range(NSUB):
                S.add('dve', lambda e, X=X, s=s: e.tensor_scalar(out=hb[:, s, :], in0=X[:, s, :], scalar1=ss[:, s:s + 1],
                                                                 scalar2=None, op0=ALU.mult),
                      reads=[xk, 'ss'], writes=['hb'])
            for k in range(NK):
                P = pst[nps[0] % 2]
                pk = 'pst%d' % (nps[0] % 2)
                nps[0] += 1
                for s in range(NSUB):
                    S.add('pe', lambda e, P=P, s=s, k=k: e.transpose(P[:, s * 128:(s + 1) * 128],
                                                                      hb[:, s, k * 128:(k + 1) * 128], idb[:]),
                          reads=['hb', 'idb'], writes=[pk])
                if k % 2 == 0:
                    S.add('dve', lambda e, P=P, k=k: e.tensor_copy(out=hT[:, k, :], in_=P[:]), writes=[pk, 'hT'])
                else:
                    S.add('act', lambda e, P=P, k=k: e.activation(out=hT[:, k, :], in_=P[:], func=AF.Identity),
                          writes=[pk, 'hT'])
            for f in range(NFF):
                i = nps[1] % 2
                nps[1] += 1
                G, U, T = psg[i], psu[i], tmp[i]
                for k in range(NK):
                    S.add('pe', lambda e, G=G, k=k, f=f: e.matmul(G[:, :], Wg[:, k, f * 128:(f + 1) * 128], hT[:, k, :],
                                                                  start=(k == 0), stop=(k == NK - 1)),
                          reads=['Wg', 'hT'], writes=['psg%d' % i])
                for k in range(NK):
                    S.add('pe', lambda e, U=U, k=k, f=f: e.matmul(U[:, :], Wu[:, k, f * 128:(f + 1) * 128], hT[:, k, :],
                                                                  start=(k == 0), stop=(k == NK - 1)),
                          reads=['Wu', 'hT'], writes=['psu%d' % i])
                S.add('act', lambda e, G=G, T=T: e.activation(out=T[:], in_=G[:], func=AF.Silu),
                      writes=['psg%d' % i, 'tmp%d' % i])
                S.add('dve', lambda e, U=U, T=T, f=f: e.tensor_tensor(out=actT[:, f, :], in0=U[:], in1=T[:], op=ALU.mult),
                      reads=['tmp%d' % i], writes=['psu%d' % i, 'actT'])
            for s in range(NSUB):
                for c in range(2):
                    i = nps[2] % 2
                    nps[2] += 1
                    Pd = psd[i]
                    for f in range(NFF):
                        S.add('pe', lambda e, Pd=Pd, f=f, s=s, c=c: e.matmul(Pd[:, :], actT[:, f, s * 128:(s + 1) * 128],
                                                                            Wd[:, f, c * 512:(c + 1) * 512],
                                                                            start=(f == 0), stop=(f == NFF - 1)),
                              reads=['actT', 'Wd'], writes=['psd%d' % i])
                    S.add('dve', lambda e, Pd=Pd, X=X, s=s, c=c: e.tensor_tensor(out=X[:, s, c * 512:(c + 1) * 512],
                                                                               in0=Pd[:], in1=X[:, s, c * 512:(c + 1) * 512],
                                                                               op=ALU.add),
                          writes=['psd%d' % i, xk])
            for s in range(NSUB):
                S.add('act', lambda e, X=X, s=s: e.activation(out=hb[:, s, :], in_=X[:, s, :], func=AF.Square,
                                                              accum_out=ss[:, 2 * NSUB + s:2 * NSUB + s + 1]),
                      reads=[xk], writes=['hb', 'ss'])
            S.add('act', lambda e: e.activation(out=ss[:, 3 * NSUB:4 * NSUB], in_=ss[:, 2 * NSUB:3 * NSUB], func=AF.Ln,
                                                scale=1.0 / D, bias=NORM_EPS), writes=['ss'])
            S.add('act', lambda e: e.activation(out=ss[:, 2 * NSUB:3 * NSUB], in_=ss[:, 3 * NSUB:4 * NSUB], func=AF.Exp,
                                                scale=-0.5), writes=['ss'])
            for s in range(NSUB):
                S.add('dve', lambda e, X=X, s=s: e.scalar_tensor_tensor(out=ot[:, s, :], in0=X[:, s, :],
                                                                        scalar=ss[:, 2 * NSUB + s:2 * NSUB + s + 1],
                                                                        in1=gft[:], op0=ALU.mult, op1=ALU.mult),
                      reads=[xk, 'ss', 'gft'], writes=['ot'])
            S.add('pool', lambda e, g=g: e.dma_start(out=yg[g], in_=ot[:]), reads=['ot'], dma=True)
        S.emit()


def build_program(T, NPQ, ST):
    Q = ST // 4
    ntok = NPQ * T + Q
    nc = bass.Bass("TRN2", target_bir_lowering=False)
    W = declare_weights(nc)
    xp = nc.dram_tensor("xp", [NPQ, T + 256, D], F32, kind="ExternalInput").ap()
    vones = nc.dram_tensor("vones", [1, T], F32, kind="ExternalInput").ap()
    xsw = nc.dram_tensor("xsw", [7 * Q + 256, D], F32, kind="ExternalInput").ap()
    xsk = nc.dram_tensor("xsk", [ST, D], F32, kind="ExternalInput").ap()
    vms = nc.dram_tensor("vms", [1, 7 * Q], F32, kind="ExternalInput").ap()
    qr0s = nc.dram_tensor("qr0s", [1, 1], F32, kind="ExternalInput").ap()
    qr0p = nc.dram_tensor("qr0p", [1, 1], F32, kind="ExternalInput").ap()
    y = nc.dram_tensor("y", [ntok, D], F32, kind="ExternalOutput").ap()
    x1 = nc.dram_tensor("x1", [ntok, D], F32, kind="Internal").ap()

    def scratch(tag, n):
        return dict(yatt=nc.dram_tensor(tag + "yatt", [4, 128, n], BF16, kind="Internal").ap(),
                    yf=nc.dram_tensor(tag + "yf", [n, 512], F32, kind="Internal").ap(),
                    yb=nc.dram_tensor(tag + "yb", [n, 512], F32, kind="Internal").ap(),
                    sf=nc.dram_tensor(tag + "sf", [n, 8], F32, kind="Internal").ap(),
                    sb=nc.dram_tensor(tag + "sb", [n, 8], F32, kind="Internal").ap(),
                    g=nc.dram_tensor(tag + "g", [n, 512], F32, kind="Internal").ap(),
                    v=nc.dram_tensor(tag + "v", [n, 512], BF16, kind="Internal").ap())
    _skip = ''
    _es = contextlib.ExitStack()
    SemPool.current = SemPool(nc, _es)
    scr = scratch("s_", Q)
    if 's' not in _skip:
      mixer_job(nc, W, "s", xsw, vms, 4 * Q, (3 * Q, 4 * Q), 4 * Q, (0, Q), 0, 3 * Q, xsk, ST,
                xsw[128 + 3 * Q:128 + 4 * Q, :], Q, qr0s, x1[NPQ * T:NPQ * T + Q, :], scr)
    for i in range(0 if 'p' not in _skip else NPQ, NPQ):
        scr = scratch("p%d_" % i, T)
        xown = xp[i, 128:128 + T, :]
        mixer_job(nc, W, "p%d" % i, xp[i], vones, T, (0, T), T, (0, T), 0, 0, xown, T, xown, T, qr0p,
                  x1[i * T:(i + 1) * T, :], scr)
    if 'f' not in _skip:
      phase_ffn(nc, x1, y, W['g2c'], W['gf'], W['wg'], W['wu'], W['wd'], ntok)
    SemPool.current = None
    _es.close()
    return nc


_NC_CACHE = {}


def kernel(x_prompt, x_sample, norm1_g, w_in, mu_prev, mu_next, k_k, k_a, r_k, w0_f, w_lora_f, w0_b, w_lora_b,
           a0_f, a_lora_f, a0_b, a_lora_b, g_lora, lnx_w, lnx_b, q_gain, k_gain, w_out, norm2_g, ffn_gate,
           ffn_up, ffn_down, norm_f_g):
    params = dict(norm1_g=norm1_g, w_in=w_in, mu_prev=mu_prev, mu_next=mu_next, k_k=k_k, k_a=k_a, r_k=r_k, w0_f=w0_f,
                  w_lora_f=w_lora_f, w0_b=w0_b, w_lora_b=w_lora_b, a0_f=a0_f, a_lora_f=a_lora_f, a0_b=a0_b,
                  a_lora_b=a_lora_b, g_lora=g_lora, lnx_w=lnx_w, lnx_b=lnx_b, q_gain=q_gain, k_gain=k_gain, w_out=w_out,
                  norm2_g=norm2_g, ffn_gate=ffn_gate, ffn_up=ffn_up, ffn_down=ffn_down, norm_f_g=norm_f_g)
    x_prompt = np.asarray(x_prompt, np.float32)
    x_sample = np.asarray(x_sample, np.float32)
    B, T, _ = x_prompt.shape
    SB, ST, _ = x_sample.shape
    NPQ = B // NCORES
    Q = ST // 4
    key = (T, NPQ, ST)
    if key not in _NC_CACHE:
        _NC_CACHE[key] = build_program(T, NPQ, ST)
    nc = _NC_CACHE[key]
    Wh = host_prep(params)
    in_maps = []
    for c in range(NCORES):
        s, j = c // 4, c % 4
        m = {n: Wh[n] for n, _ in WEIGHT_SPECS}
        xp = np.zeros((NPQ, T + 256, D), np.float32)
        xp[:, 128:128 + T] = x_prompt[c * NPQ:(c + 1) * NPQ]
        m["xp"] = xp
        m["vones"] = np.ones((1, T), np.float32)
        xsw = np.zeros((7 * Q + 256, D), np.float32)
        tlo = j * Q - 3 * Q - 128
        lo, hi = max(0, tlo), min(ST, tlo + 7 * Q + 256)
        xsw[lo - tlo:hi - tlo] = x_sample[s, lo:hi]
        m["xsw"] = xsw
        vm = np.zeros((1, 7 * Q), np.float32)
        t_first = j * Q - 3 * Q
        lo, hi = max(0, t_first), min(ST, t_first + 7 * Q)
        vm[0, lo - t_first:hi - t_first] = 1.0
        m["vms"] = vm
        m["xsk"] = np.ascontiguousarray(x_sample[s])
        m["qr0s"] = np.full((1, 1), float(j * Q // 64), np.float32)
        m["qr0p"] = np.zeros((1, 1), np.float32)
        in_maps.append(m)
    res = run_bass_kernel_spmd(nc, in_maps, core_ids=list(range(NCORES)))
    y_prompt = np.empty((B, T, D), np.float32)
    y_sample = np.empty((SB, ST, D), np.float32)
    for c in range(NCORES):
        yc = np.asarray(res.results[c]["y"])
        y_prompt[c * NPQ:(c + 1) * NPQ] = yc[:NPQ * T].reshape(NPQ, T, D)
        y_sample[c // 4, (c % 4) * Q:(c % 4 + 1) * Q] = yc[NPQ * T:]
    return (y_prompt, y_sample)
```

```python
import contextlib
import numpy as np
import concourse.bass as bass
import concourse.mybir as mybir
from concourse.bass_utils import run_bass_kernel_spmd

F32 = mybir.dt.float32
BF16 = mybir.dt.bfloat16
AF = mybir.ActivationFunctionType
ALU = mybir.AluOpType
AX = mybir.AxisListType
I32 = mybir.dt.int32

D = 1024
DFF = 2816
NFF = DFF // 128
NK = D // 128
NCORES = 8
TOK_PER_CORE = 4 * 2048 + 4096
GT = 256
NSUB = GT // 128
NORM_EPS = 1e-6
LNX_EPS = 64e-5
HEAD_DIM = 64
ROPE_THETA = 10000.0
ROPE_PAIRS = 16
QPERM = (0, 4, 1, 5, 2, 6, 3, 7)


SAME_ENGINE_SYNC = ('act', 'dve', 'pool')


class Sched:
    ENG = ('pe', 'act', 'dve', 'pool', 'sp')
    NDMA = 12

    def __init__(self, nc, same_engine_sync=SAME_ENGINE_SYNC):
        self.nc = nc
        self.ops = []
        self.last_w = {}
        self.readers = {}
        self.same = set(same_engine_sync)

    def add(self, eng, fn, reads=(), writes=(), dma=False):
        i = len(self.ops)
        deps = set()
        for r in reads:
            j = self.last_w.get(r)
            if j is not None:
                deps.add(j)
        for w in writes:
            j = self.last_w.get(w)
            if j is not None:
                deps.add(j)
            for j in self.readers.get(w, {}).values():
                if isinstance(j, list):
                    deps.update(j)
                else:
                    deps.add(j)
        for w in writes:
            self.last_w[w] = i
            self.readers[w] = {}
        for r in reads:
            if r not in writes:
                rd = self.readers.setdefault(r, {})
                if dma:
                    rd.setdefault(('dma', eng), []).append(i)
                else:
                    rd[eng] = i
        self.ops.append(dict(eng=eng, fn=fn, deps=deps, dma=dma, needs_inc=False))
        return i

    def emit(self):
        nc = self.nc
        ops = self.ops
        for op in ops:
            for j in op['deps']:
                oj = ops[j]
                if oj['dma']:
                    oj['needs_inc'] = True
                elif oj['eng'] != op['eng'] or op['dma'] or (op['eng'] in self.same):
                    oj['needs_inc'] = True
        with contextlib.ExitStack() as es:
            pool = SemPool.current
            if pool is None:
                pool = SemPool(nc, es)
            sems, dsems, cnt, dma_cnt = pool.sems, pool.dsems, pool.cnt, pool.dcnt
            dma_rr = {e: 0 for e in self.ENG}
            for op in ops:
                if op['dma']:
                    k = dma_rr[op['eng']] % self.NDMA
                    dma_rr[op['eng']] += 1
                    key = (op['eng'], k)
                    prev = dma_cnt.get(key, 0)
                    op['dsem'] = key
                    op['dprev'] = prev
                    dma_cnt[key] = prev + 16
                    op['ticket'] = prev + 16
                elif op['needs_inc']:
                    cnt[op['eng']] += 1
                    op['ticket'] = cnt[op['eng']]
            block = es.enter_context(nc.Block())

            def run(ename):
                def body(eng):
                    waited = {}

                    def wait(sem_key, sem, val):
                        if waited.get(sem_key, 0) >= val:
                            return
                        waited[sem_key] = val
                        eng.wait_ge(sem, val)
                    last_dma = {}
                    for op in ops:
                        if op['eng'] != ename:
                            continue
                        if op['dma'] and op['dprev'] > 0:
                            wait(op['dsem'], dsems[op['dsem']], op['dprev'])
                        for j in sorted(op['deps']):
                            oj = ops[j]
                            if oj['dma']:
                                wait(oj['dsem'], dsems[oj['dsem']], oj['ticket'])
                            elif oj['eng'] != ename or op['dma'] or (ename in self.same):
                                wait(oj['eng'], sems[oj['eng']], oj['ticket'])
                        ins = op['fn'](eng)
                        if op['dma']:
                            ins.then_inc(dsems[op['dsem']], 16)
                            last_dma[op['dsem']] = op['ticket']
                        elif op['needs_inc']:
                            ins.then_inc(sems[ename], 1)
                    for key, t in last_dma.items():
                        wait(key, dsems[key], t)
                return body
            block.tensor(run('pe'))
            block.scalar(run('act'))
            block.vector(run('dve'))
            block.gpsimd(run('pool'))
            block.sync(run('sp'))


class SemPool:
    current = None

    def __init__(self, nc, es):
        self.sems = {e: es.enter_context(nc.semaphore('s_' + e)) for e in Sched.ENG}
        self.dsems = {}
        for e in ('sp', 'pool'):
            for k in range(Sched.NDMA):
                self.dsems[(e, k)] = es.enter_context(nc.semaphore('d_%s_%d' % (e, k)))
        self.cnt = {e: 0 for e in Sched.ENG}
        self.dcnt = {}


STATS = []


class Phase:
    _n = [0]

    def __init__(self, nc, tag):
        self.nc = nc
        Phase._n[0] += 1
        self.tag = "%s%d_" % (tag, Phase._n[0])
        self.es = contextlib.ExitStack()
        self.S = Sched(nc)

    def sb(self, name, shape, dt=F32):
        return self.es.enter_context(self.nc.sbuf_tensor(self.tag + name, shape, dt))

    def ps(self, name, shape, dt=F32):
        return self.es.enter_context(self.nc.psum_tensor(self.tag + name, shape, dt))

    def add(self, *a, **k):
        return self.S.add(*a, **k)

    def close(self):
        self.S.emit()
        self.es.close()
        import collections
        cnt = collections.Counter(o['eng'] + ('_dma' if o['dma'] else '') for o in self.S.ops)
        STATS.append((self.tag, len(self.S.ops), dict(cnt)))


def emit_identity(P, idb, idf):
    P.add('pool', lambda e: e.memset(idf[:], 1.0), writes=['idf'])
    P.add('pool', lambda e: e.affine_select(out=idf[:], in_=idf[:], pattern=[[-1, 128]], compare_op=ALU.is_equal,
                                            fill=0.0, base=0, channel_multiplier=1), writes=['idf'])
    P.add('dve', lambda e: e.tensor_copy(out=idb[:], in_=idf[:]), reads=['idf'], writes=['idb'])


def load_weight_bf16(P, src, dst, nrows_chunks, ncols, stage, dkey, scale_tile=None, col_piece=1024, eng_alt=True):
    n = [0]
    for k in range(nrows_chunks):
        c0 = 0
        while c0 < ncols:
            cn = min(col_piece, ncols - c0)
            i = n[0] % len(stage)
            n[0] += 1
            st = stage[i]
            skey = 'stage%d' % i
            P.add('sp', lambda e, st=st, k=k, c0=c0, cn=cn: e.dma_start(out=st[:, 0:cn],
                                                                       in_=src[k * 128:(k + 1) * 128, c0:c0 + cn]),
                  writes=[skey], dma=True)
            if scale_tile is not None:
                P.add('dve', lambda e, st=st, k=k, c0=c0, cn=cn: e.tensor_scalar(
                    out=dst[:, k, c0:c0 + cn], in0=st[:, 0:cn], scalar1=scale_tile[:, k:k + 1], scalar2=None,
                    op0=ALU.mult), reads=[skey, 'wscale'], writes=[dkey])
            else:
                P.add('act', lambda e, st=st, k=k, c0=c0, cn=cn: e.activation(
                    out=dst[:, k, c0:c0 + cn], in_=st[:, 0:cn], func=AF.Identity), reads=[skey], writes=[dkey])
            c0 += cn


def emit_rstd(P, ss_in, tmp, out, scale, eps, keys_r, keys_w):
    P.add('act', lambda e: e.activation(out=tmp, in_=ss_in, func=AF.Ln, scale=scale, bias=eps),
          reads=keys_r, writes=keys_w)
    P.add('act', lambda e: e.activation(out=out, in_=tmp, func=AF.Exp, scale=-0.5), reads=[], writes=keys_w)


def emit_norm_hT(P, X, xkey, hb, hT_dst, hT_key, pT, pT_key, ss, idb, extra_scale=None):
    P.add('act', lambda e: e.activation(out=hb[:], in_=X, func=AF.Square, accum_out=ss[:, 0:1]),
          reads=[xkey], writes=['hb', 'ss'])
    emit_rstd(P, ss[:, 0:1], ss[:, 1:2], ss[:, 2:3], 1.0 / D, NORM_EPS, [], ['ss'])
    P.add('dve', lambda e: e.tensor_scalar(out=hb[:], in0=X, scalar1=ss[:, 2:3], scalar2=None, op0=ALU.mult),
          reads=[xkey, 'ss'], writes=['hb'])
    for k in range(NK):
        P.add('pe', lambda e, k=k: e.transpose(pT[:, k * 128:(k + 1) * 128], hb[:, k * 128:(k + 1) * 128], idb[:]),
              reads=['hb', 'idb'], writes=[pT_key])
    P.add('dve', lambda e: e.tensor_copy(out=hT_dst, in_=pT[:].rearrange("p (k t) -> p k t", k=NK)),
          writes=[pT_key, hT_key])


def emit_rope_tables(P, Crow, Srow, ntiles, row0_tile, wk, Ccol=None, Scol=None):
    pi_i, pf, inv, rowi, rowf, ang, t0, t1, ti = (wk['pi_i'], wk['pf'], wk['inv'], wk['rowi'], wk['rowf'],
                                                  wk['ang'], wk['t0'], wk['t1'], wk['ti'])
    P.add('pool', lambda e: e.iota(pi_i[:], pattern=[[0, 1]], base=0, channel_multiplier=1), writes=['pi_i'])
    P.add('dve', lambda e: e.tensor_copy(out=pf[:, 0:1], in_=pi_i[:]), reads=['pi_i'], writes=['pf'])
    P.add('dve', lambda e: e.tensor_scalar(out=pf[:, 1:2], in0=pf[:, 0:1], scalar1=64.0, scalar2=None, op0=ALU.is_ge),
          writes=['pf'])
    P.add('dve', lambda e: e.scalar_tensor_tensor(out=pf[:, 2:3], in0=pf[:, 1:2], scalar=-64.0, in1=pf[:, 0:1],
                                                  op0=ALU.mult, op1=ALU.add), writes=['pf'])
    for i in range(ROPE_PAIRS):
        v = float(ROPE_THETA ** (-i / ROPE_PAIRS)) / (2.0 * np.pi)
        P.add('pool', lambda e, i=i, v=v: e.memset(inv[:, i:i + 1], v), writes=['inv'])

    def sin_turns(dst, n, shift, key):
        P.add('dve', lambda e: e.tensor_scalar(out=t0[:, 0:n], in0=ang[:, 0:n], scalar1=shift, scalar2=None, op0=ALU.add),
              reads=['ang'], writes=['t0'])
        P.add('dve', lambda e: e.tensor_copy(out=ti[:, 0:n], in_=t0[:, 0:n]), reads=['t0'], writes=['ti'])
        P.add('dve', lambda e: e.tensor_copy(out=t1[:, 0:n], in_=ti[:, 0:n]), reads=['ti'], writes=['t1'])
        P.add('dve', lambda e: e.tensor_tensor(out=t0[:, 0:n], in0=t0[:, 0:n], in1=t1[:, 0:n], op=ALU.subtract),
              reads=['t1'], writes=['t0'])
        P.add('dve', lambda e: e.tensor_scalar(out=t1[:, 0:n], in0=t0[:, 0:n], scalar1=0.5, scalar2=None, op0=ALU.is_ge),
              reads=['t0'], writes=['t1'])
        P.add('dve', lambda e: e.tensor_tensor(out=t0[:, 0:n], in0=t0[:, 0:n], in1=t1[:, 0:n], op=ALU.subtract),
              reads=['t1'], writes=['t0'])
        P.add('dve', lambda e: e.tensor_scalar(out=t1[:, 0:n], in0=t0[:, 0:n], scalar1=-0.5, scalar2=None, op0=ALU.is_lt),
              reads=['t0'], writes=['t1'])
        P.add('dve', lambda e: e.tensor_tensor(out=t0[:, 0:n], in0=t0[:, 0:n], in1=t1[:, 0:n], op=ALU.add),
              reads=['t1'], writes=['t0'])
        P.add('act', lambda e: e.activation(out=dst, in_=t0[:, 0:n], func=AF.Sin, scale=6.28318),
              reads=['t0'], writes=[key])

    if Ccol is not None:
        P.add('dve', lambda e: e.tensor_scalar(out=ang[:, 0:16], in0=inv[:, 0:16], scalar1=pf[:, 2:3], scalar2=None,
                                               op0=ALU.mult), reads=['pf', 'inv'], writes=['ang'])
        sin_turns(Scol[:, 0:16], 16, 0.0, 'Stab')
        sin_turns(Ccol[:, 0:16], 16, 0.25, 'Ctab')
    CH = 32
    for n0 in range(0, ntiles, CH):
        NT = min(CH, ntiles - n0)
        P.add('pool', lambda e, n0=n0, NT=NT: e.iota(rowi[:, 0:NT], pattern=[[2, NT]], base=2 * n0, channel_multiplier=0),
              writes=['rowi'])
        P.add('dve', lambda e, NT=NT: e.tensor_copy(out=rowf[:, 0:NT], in_=rowi[:, 0:NT]), reads=['rowi'], writes=['rowf'])
        P.add('dve', lambda e, NT=NT: e.tensor_scalar(out=rowf[:, 0:NT], in0=rowf[:, 0:NT], scalar1=pf[:, 1:2], scalar2=None,
                                                      op0=ALU.add), reads=['pf'], writes=['rowf'])
        if row0_tile is not None:
            P.add('dve', lambda e, NT=NT: e.tensor_scalar(out=rowf[:, 0:NT], in0=rowf[:, 0:NT], scalar1=row0_tile,
                                                          scalar2=None, op0=ALU.add), reads=['row0'], writes=['rowf'])
        P.add('dve', lambda e, NT=NT: e.tensor_tensor(
            out=ang[:, 0:NT * 16].rearrange("p (n c) -> p n c", c=16),
            in0=rowf[:, 0:NT].unsqueeze(2).broadcast_to([128, NT, 16]),
            in1=inv[:, 0:16].unsqueeze(1).broadcast_to([128, NT, 16]), op=ALU.mult),
            reads=['rowf', 'inv'], writes=['ang'])
        sin_turns(Srow[:, n0:n0 + NT, :].rearrange("p n c -> p (n c)"), NT * 16, 0.0, 'Stab')
        sin_turns(Crow[:, n0:n0 + NT, :].rearrange("p n c -> p (n c)"), NT * 16, 0.25, 'Ctab')


def emit_qk_norm_rope(P, src_ps, src_key, nheads, gain_b, Cr, Sr, Cc, Sc, scale, out_bf, out_key, wk, tkeys):
    H = nheads
    W = H * 64
    sq, qn, ta, tb, st = wk['sq'], wk['qn'], wk['ta'], wk['tb'], wk['st']
    P.add('act', lambda e: e.activation(out=qn[:, 0:W], in_=src_ps, func=AF.Identity), writes=[src_key, 'qn'])
    P.add('dve', lambda e: e.tensor_tensor(out=sq[:, 0:W], in0=qn[:, 0:W], in1=qn[:, 0:W], op=ALU.mult),
          reads=['qn'], writes=['sq'])
    P.add('dve', lambda e: e.tensor_reduce(out=st[:, 0:H], in_=sq[:, 0:W].rearrange("p (h c) -> p h c", c=64),
                                           axis=AX.X, op=ALU.add), reads=['sq'], writes=['st'])
    emit_rstd(P, st[:, 0:H], st[:, 8:8 + H], st[:, 16:16 + H], 1.0 / 64, NORM_EPS, ['st'], ['st'])
    q3 = qn[:, 0:W].rearrange("p (h c) -> p h c", c=64)
    P.add('dve', lambda e: e.tensor_tensor(out=q3, in0=q3, in1=st[:, 16:16 + H].unsqueeze(2).broadcast_to([128, H, 64]),
                                           op=ALU.mult), reads=['st'], writes=['qn'])
    P.add('dve', lambda e: e.scalar_tensor_tensor(out=q3, in0=q3, scalar=float(scale),
                                                  in1=gain_b.unsqueeze(1).broadcast_to([128, H, 64]),
                                                  op0=ALU.mult, op1=ALU.mult), reads=['gains'], writes=['qn'])
    q5 = qn[:, 0:W].rearrange("p (h a b i) -> p h a b i", a=2, b=2, i=16)
    o5 = out_bf.rearrange("p (h a b i) -> p h a b i", a=2, b=2, i=16)
    A3 = ta[:, 0:H * 16].rearrange("p (h i) -> p h i", i=16)
    B3 = tb[:, 0:H * 16].rearrange("p (h i) -> p h i", i=16)
    for a, (Ct, St) in enumerate(((Cr, Sr), (Cc, Sc))):
        x1, x2 = q5[:, :, a, 0, :], q5[:, :, a, 1, :]
        C3 = Ct.unsqueeze(1).broadcast_to([128, H, 16])
        S3 = St.unsqueeze(1).broadcast_to([128, H, 16])
        P.add('dve', lambda e, x1=x1, C3=C3: e.tensor_tensor(out=A3, in0=x1, in1=C3, op=ALU.mult),
              reads=['qn'] + tkeys, writes=['ta'])
        P.add('pool', lambda e, x2=x2, S3=S3: e.tensor_tensor(out=B3, in0=x2, in1=S3, op=ALU.mult),
              reads=['qn'] + tkeys, writes=['tb'])
        P.add('dve', lambda e, a=a: e.tensor_tensor(out=o5[:, :, a, 0, :], in0=A3, in1=B3, op=ALU.subtract),
              reads=['ta', 'tb'], writes=[out_key])
        P.add('dve', lambda e, x1=x1, S3=S3: e.tensor_tensor(out=A3, in0=x1, in1=S3, op=ALU.mult),
              reads=['qn'] + tkeys, writes=['ta'])
        P.add('pool', lambda e, x2=x2, C3=C3: e.tensor_tensor(out=B3, in0=x2, in1=C3, op=ALU.mult),
              reads=['qn'] + tkeys, writes=['tb'])
        P.add('dve', lambda e, a=a: e.tensor_tensor(out=o5[:, :, a, 1, :], in0=A3, in1=B3, op=ALU.add),
              reads=['ta', 'tb'], writes=[out_key])


def phase_attention(nc, xk, xq, winA, g1c, qg, kg, qrow0, yatt, Tk, Town):
    P = Phase(nc, "att")
    NB = Tk // 128
    NQT = Town // 512
    WA = P.sb("WA", [128, NK, 768], BF16)
    stage = [P.sb("stage%d" % i, [128, 768]) for i in range(2)]
    g1t = P.sb("g1t", [128, NK])
    gq = P.sb("gq", [128, 64]); gk = P.sb("gk", [128, 64])
    r0t = P.sb("r0t", [128, 1])
    negm = P.sb("negm", [128, 4])
    idf = P.sb("idf", [128, 128]); idb = P.sb("idb", [128, 128], BF16)
    CtK = P.sb("CtK", [128, NB, 16]); StK = P.sb("StK", [128, NB, 16])
    NQ128 = Town // 128
    CtQ = P.sb("CtQ", [128, NQ128, 16]); StQ = P.sb("StQ", [128, NQ128, 16])
    Ccol = P.sb("Ccol", [128, 16]); Scol = P.sb("Scol", [128, 16])
    wk = dict(pi_i=P.sb("pi_i", [128, 1], I32), pf=P.sb("pf", [128, 4]), inv=P.sb("inv", [128, 16]),
              rowi=P.sb("rowi", [128, 32], I32), rowf=P.sb("rowf", [128, 32]), ang=P.sb("ang", [128, 512]),
              t0=P.sb("t0", [128, 512]), t1=P.sb("t1", [128, 512]), ti=P.sb("ti", [128, 512], I32),
              sq=P.sb("sq", [128, 512]), qn=P.sb("qn", [128, 512]), ta=P.sb("ta", [128, 256]), tb=P.sb("tb", [128, 256]),
              st=P.sb("st", [128, 24]))
    KT = P.sb("KT", [128, Tk], BF16)
    V3 = P.sb("V3", [128, NB, 192], BF16)
    xt = [P.sb("xt%d" % i, [128, D]) for i in range(2)]
    hb = P.sb("hb", [128, D], BF16)
    ss = P.sb("ss", [128, 4])
    hT = P.sb("hT", [128, NK, 128], BF16)
    ko = P.sb("ko", [128, 128], BF16)
    qo = P.sb("qo", [128, 512], BF16)
    QT = P.sb("QT", [128, 4, 512], BF16)
    PT = [P.sb("PT%d" % i, [128, 512], BF16) for i in range(4)]
    rl = P.sb("rl", [128, 1024])
    Yt = [P.sb("Yt%d" % i, [128, 512], BF16) for i in range(2)]
    pT = P.ps("pT", [128, 1024], BF16)
    pJ = P.ps("pJ", [128, 512])
    pS = [P.ps("pS%d" % i, [128, 512]) for i in range(4)]
    pO = [P.ps("pO%d" % i, [128, 512]) for i in range(2)]

    P.add('sp', lambda e: e.dma_start(out=g1t[:], in_=g1c[:, :]), writes=['wscale'], dma=True)
    P.add('sp', lambda e: e.dma_start(out=gq[:], in_=qg.partition_broadcast(128)), writes=['gains'], dma=True)
    P.add('sp', lambda e: e.dma_start(out=gk[:], in_=kg.partition_broadcast(128)), writes=['gains'], dma=True)
    P.add('sp', lambda e: e.dma_start(out=r0t[:], in_=qrow0.partition_broadcast(128)), writes=['row0'], dma=True)
    emit_identity(P, idb, idf)
    load_weight_bf16(P, winA, WA, NK, 768, stage, 'WA', scale_tile=g1t, col_piece=768)
    emit_rope_tables(P, CtK, StK, NB, None, wk, Ccol, Scol)
    emit_rope_tables(P, CtQ, StQ, NQ128, r0t[:, 0:1], wk)
    P.add('dve', lambda e: e.tensor_reduce(out=negm[:, 0:1], in_=gq[:], axis=AX.X, op=ALU.max, apply_absolute_value=True),
          reads=['gains'], writes=['negm'])
    P.add('dve', lambda e: e.tensor_reduce(out=negm[:, 1:2], in_=gk[:], axis=AX.X, op=ALU.max, apply_absolute_value=True),
          reads=['gains'], writes=['negm'])
    P.add('dve', lambda e: e.scalar_tensor_tensor(out=negm[:, 2:3], in0=negm[:, 0:1], scalar=-8.0 * 1.0001, in1=negm[:, 1:2],
                                                  op0=ALU.mult, op1=ALU.mult), writes=['negm'])
    P.add('pool', lambda e: e.memset(V3[:, :, 64:128], 1.0), writes=['V3'])

    for n in range(NB):
        X = xt[n % 2]
        xkey = 'xt%d' % (n % 2)
        P.add('sp', lambda e, X=X, n=n: e.dma_start(out=X[:], in_=xk[n * 128:(n + 1) * 128, :]), writes=[xkey], dma=True)
        emit_norm_hT(P, X[:], xkey, hb, hT[:], 'hT', pT, 'pT', ss, idb)
        for k in range(NK):
            P.add('pe', lambda e, k=k: e.matmul(pJ[:, 0:256], hT[:, k, :], WA[:, k, 512:768], start=(k == 0), stop=(k == NK - 1)),
                  reads=['hT', 'WA'], writes=['pJ'])
        P.add('act', lambda e, n=n: e.activation(out=V3[:, n, :].rearrange("p (a b) -> p a b", b=64)[:, 0:3:2, :],
                                                 in_=pJ[:, 128:256].rearrange("p (a b) -> p a b", b=64), func=AF.Identity),
              writes=['pJ', 'V3'])
        emit_qk_norm_rope(P, pJ[:, 0:128], 'pJ', 2, gk[:], CtK[:, n, :], StK[:, n, :], Ccol[:], Scol[:], 1.0, ko[:], 'ko', wk, ['Ctab', 'Stab'])
        P.add('pe', lambda e: e.transpose(pT[:, 0:128], ko[:], idb[:]), reads=['ko', 'idb'], writes=['pT'])
        P.add('act', lambda e, n=n: e.activation(out=KT[:, n * 128:(n + 1) * 128], in_=pT[:, 0:128], func=AF.Identity),
              writes=['pT', 'KT'])

    npt = [0]
    for qt in range(NQT):
        for j in range(4):
            n = qt * 4 + j
            X = xt[n % 2]
            xkey = 'xt%d' % (n % 2)
            P.add('sp', lambda e, X=X, n=n: e.dma_start(out=X[:], in_=xq[n * 128:(n + 1) * 128, :]), writes=[xkey], dma=True)
            emit_norm_hT(P, X[:], xkey, hb, hT[:], 'hT', pT, 'pT', ss, idb)
            for k in range(NK):
                P.add('pe', lambda e, k=k: e.matmul(pJ[:, :], hT[:, k, :], WA[:, k, 0:512], start=(k == 0), stop=(k == NK - 1)),
                      reads=['hT', 'WA'], writes=['pJ'])
            emit_qk_norm_rope(P, pJ[:, :], 'pJ', 8, gq[:], CtQ[:, n, :], StQ[:, n, :], Ccol[:], Scol[:], HEAD_DIM ** -0.5,
                              qo[:], 'qo', wk, ['Ctab', 'Stab'])
            for g in range(4):
                P.add('pe', lambda e, g=g: e.transpose(pT[:, g * 128:(g + 1) * 128], qo[:, g * 128:(g + 1) * 128], idb[:]),
                      reads=['qo', 'idb'], writes=['pT'])
            P.add('dve', lambda e, j=j: e.tensor_copy(out=QT[:, :, j * 128:(j + 1) * 128],
                                                      in_=pT[:, 0:512].rearrange("p (g t) -> p g t", g=4)),
                  writes=['pT', 'QT'])
        for g in range(4):
            for n in range(NB):
                ia, ib = npt[0] % 4, (npt[0] + 1) % 4
                npt[0] += 2
                P.add('pe', lambda e, g=g, n=n, ia=ia: e.matmul(pS[ia][:, :], KT[0:64, n * 128:(n + 1) * 128], QT[0:64, g, :],
                                                                start=True, stop=True),
                      reads=['KT', 'QT'], writes=['pS%d' % ia])
                P.add('pe', lambda e, g=g, n=n, ib=ib: e.matmul(pS[ib][:, :], KT[64:128, n * 128:(n + 1) * 128], QT[64:128, g, :],
                                                                start=True, stop=True),
                      reads=['KT', 'QT'], writes=['pS%d' % ib])
                P.add('act', lambda e, ia=ia: e.activation(out=PT[ia][:], in_=pS[ia][:], func=AF.Exp, bias=negm[:, 2:3], scale=1.0),
                      reads=['negm'], writes=['pS%d' % ia, 'PT%d' % ia])
                P.add('act', lambda e, ib=ib: e.activation(out=PT[ib][:], in_=pS[ib][:], func=AF.Exp, bias=negm[:, 2:3], scale=1.0),
                      reads=['negm'], writes=['pS%d' % ib, 'PT%d' % ib])
                P.add('pe', lambda e, n=n, ia=ia: e.matmul(pO[0][:, :], V3[:, n, 0:128], PT[ia][:], start=(n == 0), stop=(n == NB - 1)),
                      reads=['V3', 'PT%d' % ia], writes=['pO0'])
                P.add('pe', lambda e, n=n, ib=ib: e.matmul(pO[1][:, :], V3[:, n, 64:192], PT[ib][:], start=(n == 0), stop=(n == NB - 1)),
                      reads=['V3', 'PT%d' % ib], writes=['pO1'])
            Y = Yt[g % 2]
            ykey = 'Yt%d' % (g % 2)
            P.add('dve', lambda e: e.reciprocal(out=rl[64:128, 0:512], in_=pO[0][64:128, :]), writes=['pO0', 'rlA'])
            P.add('dve', lambda e: e.reciprocal(out=rl[0:64, 512:1024], in_=pO[1][0:64, :]), writes=['pO1', 'rlB'])
            P.add('dve', lambda e, Y=Y: e.tensor_tensor(out=Y[0:64, :], in0=pO[0][0:64, :], in1=rl[64:128, 0:512], op=ALU.mult),
                  reads=['rlA'], writes=['pO0', ykey])
            P.add('dve', lambda e, Y=Y: e.tensor_tensor(out=Y[64:128, :], in0=pO[1][64:128, :], in1=rl[0:64, 512:1024], op=ALU.mult),
                  reads=['rlB'], writes=['pO1', ykey])
            P.add('pool', lambda e, Y=Y, g=g, qt=qt: e.dma_start(out=yatt[g, :, qt * 512:(qt + 1) * 512], in_=Y[:]),
                  reads=[ykey], dma=True)
    P.close()


def emit_norm_hT_n(P, X, xkey, np_, hb, hT_dst, hT_key, pT, pT_key, ss, idb):
    P.add('act', lambda e: e.activation(out=hb[0:np_, :], in_=X, func=AF.Square, accum_out=ss[0:np_, 0:1]),
          reads=[xkey], writes=['hb', 'ss'])
    emit_rstd(P, ss[0:np_, 0:1], ss[0:np_, 1:2], ss[0:np_, 2:3], 1.0 / D, NORM_EPS, [], ['ss'])
    P.add('dve', lambda e: e.tensor_scalar(out=hb[0:np_, :], in0=X, scalar1=ss[0:np_, 2:3], scalar2=None, op0=ALU.mult),
          reads=[xkey, 'ss'], writes=['hb'])
    for k in range(NK):
        P.add('pe', lambda e, k=k: e.transpose(pT[:, k * np_:(k + 1) * np_], hb[0:np_, k * 128:(k + 1) * 128],
                                               idb[0:np_, 0:np_]),
              reads=['hb', 'idb'], writes=[pT_key])
    P.add('dve', lambda e: e.tensor_copy(out=hT_dst, in_=pT[:, 0:NK * np_].rearrange("p (k t) -> p k t", k=NK)),
          writes=[pT_key, hT_key])


DECAY_C = float(np.exp(-0.5))


def phase_rwkv(nc, fwd, xw, vmask, Tw, own0, own1, WRd, nch, mupc, munc, g1c, Wld, w0c, a0c, kkc, kac, rkc,
               y_out, s_out, Gld=None, g_out=None, v_out=None):
    P = Phase(nc, "rwf" if fwd else "rwb")
    NSC = Tw // 512
    CL, CG = 12, 13
    WR = P.sb("WR", [128, NK, nch * 128], BF16)
    Wl = P.sb("Wl", [128, 512], BF16)
    Gl = P.sb("Gl", [128, 512], BF16) if fwd else None
    g1t = P.sb("g1t", [128, NK])
    mp = P.sb("mp", [128, 16]); mn = P.sb("mn", [128, 16]); c0 = P.sb("c0", [128, 16])
    cv = P.sb("cv", [128, 20])
    idf = P.sb("idf", [128, 128]); idb = P.sb("idb", [128, 128], BF16)
    ones_bd = P.sb("ones_bd", [128, 128])
    E2 = P.sb("E2", [128, 2], BF16)
    Ms = P.sb("Ms", [128, 128], BF16); Mi = P.sb("Mi", [128, 128], BF16); Mt = P.sb("Mt", [128, 128], BF16)
    rmask = P.sb("rmask", [128, 512])
    vmt = P.sb("vmt", [128, 512])
    xt = [P.sb("xt%d" % i, [128, D]) for i in range(4)]
    xh = P.sb("xh", [2, D])
    hb = P.sb("hb", [128, D], BF16)
    ss = P.sb("ss", [128, 4])
    hTw = P.sb("hTw", [128, NK, 514], BF16)
    tl = P.sb("tl", [128, 32])
    T = {n: P.sb(n, [128, 512]) for n in ("zr", "zk", "zv", "zL", "sw", "aa", "kkr", "sq", "rn", "kd", "kka", "cl", "ex",
                                          "rem", "remx", "ea", "eb")}
    LW = P.sb("LW", [128, 512], BF16)
    sg = P.sb("sg", [128, 512], BF16) if fwd else None
    ARt = [P.sb("ARt%d" % c, [128, 4, 256], BF16) for c in range(4)]
    rk = [P.sb("rk%d" % c, [128, 512], BF16) for c in range(4)]
    wc = P.sb("wc", [128, 4, 4])
    Bt2 = [P.sb("Bt%d" % i, [128, 512], BF16) for i in range(2)]
    Kt2 = [P.sb("Kt%d" % i, [128, 512], BF16) for i in range(2)]
    Bh = P.sb("Bh", [128, 512], BF16); Kh = P.sb("Kh", [128, 512], BF16); vb = P.sb("vb", [128, 512], BF16)
    tok = P.sb("tok", [128, 4, 4, 384], BF16)
    SC_LM = P.sb("SC_LM", [128, 32, 256], BF16)
    SC_MP = P.sb("SC_MP", [128, 32, 256], BF16)
    QP = [P.sb("QP%d" % i, [128, 4, 256], BF16) for i in range(2)]
    QTt = [P.sb("QT%d" % i, [128, 4, 128], BF16) for i in range(2)]
    S32 = P.sb("S32", [128, 4, 128])
    Sbf = P.sb("Sbf", [128, 4, 128], BF16)
    RHSb = P.sb("RHSb", [128, 4, 128], BF16)
    Ub = P.sb("Ub", [128, 4, 128], BF16)
    yt = P.sb("yt", [128, 512])
    st8 = P.sb("st8", [128, 8])
    gt = P.sb("gt", [128, 512]) if fwd else None
    pm = [P.ps("pm%d" % i, [128, 512]) for i in range(2)]
    pc2 = P.ps("pc2", [128, 512])
    pT = P.ps("pT", [128, 1024], BF16)
    pc = [None] * 4 + [P.ps("pc%d" % i, [128, 512]) for i in range(4, 8)]

    P.add('sp', lambda e: e.dma_start(out=g1t[:], in_=g1c[:, :]), writes=['wscale'], dma=True)
    P.add('sp', lambda e: e.dma_start(out=mp[:, 0:nch], in_=mupc[:, :]), writes=['mu'], dma=True)
    P.add('sp', lambda e: e.dma_start(out=mn[:, 0:nch], in_=munc[:, :]), writes=['mu'], dma=True)
    for i, src in enumerate((w0c, a0c, kkc, kac, rkc)):
        P.add('sp', lambda e, i=i, src=src: e.dma_start(out=cv[:, 4 * i:4 * i + 4], in_=src[:, :]), writes=['cv'], dma=True)
    P.add('dve', lambda e: e.tensor_tensor(out=c0[:, 0:nch], in0=mp[:, 0:nch], in1=mn[:, 0:nch], op=ALU.add),
          reads=['mu'], writes=['c0'])
    P.add('dve', lambda e: e.tensor_scalar(out=c0[:, 0:nch], in0=c0[:, 0:nch], scalar1=-1.0, scalar2=1.0, op0=ALU.mult,
                                           op1=ALU.add), writes=['c0'])
    emit_identity(P, idb, idf)
    P.add('pool', lambda e: e.memset(ones_bd[:], 0.0), writes=['ones_bd'])
    P.add('pool', lambda e: e.memset(ones_bd[0:64, 0:64], 1.0), writes=['ones_bd'])
    P.add('pool', lambda e: e.memset(ones_bd[64:128, 64:128], 1.0), writes=['ones_bd'])
    P.add('pool', lambda e: e.memset(E2[:], 0.0), writes=['E2'])
    P.add('pool', lambda e: e.memset(E2[0:64, 0:1], 1.0), writes=['E2'])
    P.add('pool', lambda e: e.memset(E2[64:128, 1:2], 1.0), writes=['E2'])
    for M, strict, transposed in ((Ms, True, False), (Mi, False, False), (Mt, True, True)):
        key = 'masks'
        P.add('pool', lambda e, M=M: e.memset(idf[:], 1.0), writes=['idf'])
        sgn = 1 if (fwd != transposed) else -1
        P.add('pool', lambda e, M=M, sgn=sgn, strict=strict: e.affine_select(
            out=idf[:], in_=idf[:], pattern=[[sgn, 128]], compare_op=(ALU.is_gt if strict else ALU.is_ge), fill=0.0, base=0,
            channel_multiplier=-sgn), writes=['idf'])
        P.add('dve', lambda e, M=M: e.tensor_copy(out=M[:], in_=idf[:]), reads=['idf'], writes=[key])
    P.add('pool', lambda e: e.memset(rmask[:], 1.0), writes=['rmask'])
    P.add('pool', lambda e: e.memset(rmask[:].rearrange("p (j t) -> p j t", t=128)[:, :, 0:1], 0.0), writes=['rmask'])
    for c in range(4):
        P.add('pool', lambda e, c=c: e.memset(S32[:, c, :], 0.0), writes=['S32_%d' % c])
        P.add('pool', lambda e, c=c: e.memset(Sbf[:, c, :], 0.0), writes=['Sbf%d' % c])
    stage = [T["zr"], T["zk"]]
    load_weight_bf16(P, WRd, WR, NK, nch * 128, stage, 'WR', scale_tile=g1t, col_piece=512)
    P.add('sp', lambda e: e.dma_start(out=T["zv"][:], in_=Wld[:, :]), writes=['zv'], dma=True)
    P.add('act', lambda e: e.activation(out=Wl[:], in_=T["zv"][:], func=AF.Identity), reads=['zv'], writes=['Wl'])
    if fwd:
        P.add('sp', lambda e: e.dma_start(out=T["zL"][:], in_=Gld[:, :]), writes=['zL'], dma=True)
        P.add('act', lambda e: e.activation(out=Gl[:], in_=T["zL"][:], func=AF.Identity), reads=['zL'], writes=['Gl'])

    npm = [0]

    def inproj_shift(ci, dst, dkey):
        b = npm[0] % 2
        npm[0] += 1
        pmb, pk = pm[b], 'pm%d' % b
        for k in range(NK):
            P.add('pe', lambda e, k=k: e.matmul(pmb[:, :], WR[:, k, ci * 128:(ci + 1) * 128], hTw[:, k, 0:512],
                                                start=(k == 0), stop=(k == NK - 1)), reads=['WR', 'hTw'], writes=[pk])
        P.add('act', lambda e: e.activation(out=dst[:], in_=pmb[:], func=AF.Identity, scale=c0[:, ci:ci + 1]),
              reads=['c0'], writes=[pk, dkey])
        P.add('dve', lambda e: e.scalar_tensor_tensor(out=dst[:, 1:512], in0=pmb[:, 0:511], scalar=mp[:, ci:ci + 1],
                                                      in1=dst[:, 1:512], op0=ALU.mult, op1=ALU.add),
              reads=['mu'], writes=[pk, dkey])
        P.add('dve', lambda e: e.scalar_tensor_tensor(out=dst[:, 0:511], in0=pmb[:, 1:512], scalar=mn[:, ci:ci + 1],
                                                      in1=dst[:, 0:511], op0=ALU.mult, op1=ALU.add),
              reads=['mu'], writes=[pk, dkey])
        P.add('dve', lambda e: e.scalar_tensor_tensor(out=dst[:, 0:1], in0=tl[:, 2 * ci:2 * ci + 1], scalar=mp[:, ci:ci + 1],
                                                      in1=dst[:, 0:1], op0=ALU.mult, op1=ALU.add),
              reads=['mu', 'tl'], writes=[dkey])
        P.add('dve', lambda e: e.scalar_tensor_tensor(out=dst[:, 511:512], in0=tl[:, 2 * ci + 1:2 * ci + 2],
                                                      scalar=mn[:, ci:ci + 1], in1=dst[:, 511:512], op0=ALU.mult, op1=ALU.add),
              reads=['mu', 'tl'], writes=[dkey])

    order = list(range(NSC)) if fwd else list(range(NSC - 1, -1, -1))
    chunk_order = [0, 1, 2, 3] if fwd else [3, 2, 1, 0]
    def early(sc):
        t0 = sc * 512
        own = (own0 <= t0 < own1)
        for j in range(4):
            r0 = 128 + t0 + j * 128
            P.add('sp', lambda e, j=j, r0=r0: e.dma_start(out=xt[j][:], in_=xw[r0:r0 + 128, :]), writes=['xt%d' % j], dma=True)
        P.add('sp', lambda e: e.dma_start(out=xh[0:1, :], in_=xw[127 + t0:128 + t0, :]), writes=['xh'], dma=True)
        P.add('sp', lambda e: e.dma_start(out=xh[1:2, :], in_=xw[128 + t0 + 512:129 + t0 + 512, :]), writes=['xh'], dma=True)
        P.add('sp', lambda e: e.dma_start(out=vmt[:], in_=vmask[0:1, t0:t0 + 512].partition_broadcast(128)),
              writes=['vmt'], dma=True)
        yield
        for j in range(4):
            emit_norm_hT_n(P, xt[j][:], 'xt%d' % j, 128, hb, hTw[:, :, j * 128:(j + 1) * 128], 'hTw', pT, 'pT', ss, idb)
            yield
        emit_norm_hT_n(P, xh[0:2, :], 'xh', 2, hb, hTw[:, :, 512:514], 'hTw', pT, 'pT', ss, idb)
        yield
        chunks = list(range(13)) + ([CG] if (fwd and own) else [])
        for ci in chunks:
            for k in range(NK):
                P.add('pe', lambda e, k=k, ci=ci: e.matmul(pc2[:, 2 * ci:2 * ci + 2], WR[:, k, ci * 128:(ci + 1) * 128],
                                                           hTw[:, k, 512:514], start=(k == 0), stop=(k == NK - 1)),
                      reads=['WR', 'hTw'], writes=['pc2'])
            yield
        P.add('dve', lambda e: e.tensor_copy(out=tl[:, 0:28], in_=pc2[:, 0:28]), writes=['pc2', 'tl'])
        yield
        inproj_shift(CL, T["zL"], 'zL')
        P.add('act', lambda e: e.activation(out=LW[0:64, :], in_=T["zL"][0:64, :], func=AF.Tanh), reads=['zL'], writes=['LW'])
        P.add('act', lambda e: e.activation(out=LW[64:128, :], in_=T["zL"][64:128, :], func=AF.Identity), reads=['zL'],
              writes=['LW'])
        yield

    def mid(sc):
        t0 = sc * 512
        own = (own0 <= t0 < own1)
        if fwd and own:
            inproj_shift(CG, T["zL"], 'zL')
            P.add('act', lambda e: e.activation(out=sg[:], in_=T["zL"][:], func=AF.Sigmoid), reads=['zL'], writes=['sg'])
        def pair(c):
            zr, zk, zv = T["zr"], T["zk"], T["zv"]
            Bt, Kt, btk, ktk = Bt2[c % 2], Kt2[c % 2], 'Bt%d' % (c % 2), 'Kt%d' % (c % 2)
            inproj_shift(c, zr, 'zr')
            yield
            inproj_shift(4 + c, zk, 'zk')
            yield
            inproj_shift(8 + c, zv, 'zv')
            yield
            cs = slice(c * 128, (c + 1) * 128)
            P.add('pe', lambda e, cs=cs: e.matmul(pm[0][:, :], Wl[0:64, cs], LW[0:64, :], start=True, stop=True),
                  reads=['Wl', 'LW'], writes=['pm0'])
            P.add('pe', lambda e, cs=cs: e.matmul(pm[1][:, :], Wl[64:128, cs], LW[64:128, :], start=True, stop=True),
                  reads=['Wl', 'LW'], writes=['pm1'])
            yield
            sw, aa, kkr, sq, rn, kd, kka, cl, ex, rem, remx, ea, eb = (T[n] for n in (
                "sw", "aa", "kkr", "sq", "rn", "kd", "kka", "cl", "ex", "rem", "remx", "ea", "eb"))
            P.add('act', lambda e, c=c: e.activation(out=sw[:], in_=pm[0][:], func=AF.Sigmoid, bias=cv[:, c:c + 1], scale=1.0),
                  reads=['cv'], writes=['pm0', 'sw'])
            P.add('act', lambda e, c=c: e.activation(out=aa[:], in_=pm[1][:], func=AF.Sigmoid, bias=cv[:, 4 + c:5 + c], scale=1.0),
                  reads=['cv'], writes=['pm1', 'aa'])
            yield
            P.add('act', lambda e, c=c: e.activation(out=sq[:], in_=zk[:], func=AF.Square, scale=cv[:, 8 + c:9 + c]),
                  reads=['zk', 'cv'], writes=['sq'])
            P.add('act', lambda e, c=c: e.activation(out=kkr[:], in_=zk[:], func=AF.Identity, scale=cv[:, 8 + c:9 + c]),
                  reads=['zk', 'cv'], writes=['kkr'])
            P.add('pe', lambda e: e.matmul(pm[0][:, :], ones_bd[:], sq[:], start=True, stop=True),
                  reads=['ones_bd', 'sq'], writes=['pm0'])
            yield
            P.add('act', lambda e: e.activation(out=rn[:], in_=pm[0][:], func=AF.Ln, bias=1e-12, scale=1.0),
                  writes=['pm0', 'rn'])
            P.add('act', lambda e: e.activation(out=rn[:], in_=rn[:], func=AF.Exp, scale=-0.5), writes=['rn'])
            P.add('dve', lambda e: e.tensor_tensor(out=kkr[:], in0=kkr[:], in1=rn[:], op=ALU.mult), reads=['rn'], writes=['kkr'])
            yield
            P.add('dve', lambda e, c=c: e.tensor_scalar(out=kd[:], in0=aa[:], scalar1=-1.0, scalar2=cv[:, 12 + c:13 + c],
                                                        op0=ALU.add, op1=ALU.mult), reads=['aa', 'cv'], writes=['kd'])
            P.add('dve', lambda e: e.scalar_tensor_tensor(out=kd[:], in0=kd[:], scalar=1.0, in1=zk[:], op0=ALU.add,
                                                          op1=ALU.mult), reads=['zk'], writes=['kd'])
            P.add('pool', lambda e: e.tensor_tensor(out=kka[:], in0=kkr[:], in1=aa[:], op=ALU.mult),
                  reads=['kkr', 'aa'], writes=['kka'])
            yield
            P.add('dve', lambda e: e.tensor_tensor_scan(out=cl[:], data0=rmask[:], data1=sw[:], initial=0.0, op0=ALU.mult,
                                                        op1=ALU.add), reads=['rmask', 'sw'], writes=['cl'])
            cl3 = cl[:].rearrange("p (j t) -> p j t", t=128)
            totb = cl3[:, :, 127:128].broadcast_to([128, 4, 128])
            P.add('pool', lambda e: e.tensor_tensor(out=ex[:], in0=cl[:], in1=sw[:], op=ALU.subtract),
                  reads=['cl', 'sw'], writes=['ex'])
            P.add('dve', lambda e: e.tensor_tensor(out=rem[:].rearrange("p (j t) -> p j t", t=128), in0=totb, in1=cl3,
                                                   op=ALU.subtract), reads=['cl'], writes=['rem'])
            if fwd:
                uA, uR, uB, uH = ex, cl, cl, rem
                kA, kR, kB, kH = 'ex', 'cl', 'cl', 'rem'
            else:
                P.add('dve', lambda e: e.tensor_tensor(out=remx[:].rearrange("p (j t) -> p j t", t=128), in0=totb,
                                                       in1=ex[:].rearrange("p (j t) -> p j t", t=128), op=ALU.subtract),
                      reads=['cl', 'ex'], writes=['remx'])
                uA, uR, uB, uH = rem, remx, remx, ex
                kA, kR, kB, kH = 'rem', 'remx', 'remx', 'ex'
            P.add('act', lambda e, c=c: e.activation(out=wc[:, c, :], in_=cl3[:, :, 127], func=AF.Exp, scale=-DECAY_C),
                  reads=['cl'], writes=['wc'])
            yield
            AR = ARt[c]
            arkey = 'ARt%d' % c
            P.add('act', lambda e: e.activation(out=ea[:], in_=uA[:], func=AF.Exp, scale=-DECAY_C), reads=[kA], writes=['ea'])
            P.add('dve', lambda e: e.scalar_tensor_tensor(out=AR[:, :, 0:128], in0=kkr[:].rearrange("p (j t) -> p j t", t=128),
                                                          scalar=-1.0, in1=ea[:].rearrange("p (j t) -> p j t", t=128),
                                                          op0=ALU.mult, op1=ALU.mult), reads=['kkr', 'ea'], writes=[arkey])
            P.add('act', lambda e: e.activation(out=eb[:], in_=uR[:], func=AF.Exp, scale=-DECAY_C), reads=[kR], writes=['eb'])
            P.add('pool', lambda e: e.tensor_tensor(out=AR[:, :, 128:256], in0=zr[:].rearrange("p (j t) -> p j t", t=128),
                                                    in1=eb[:].rearrange("p (j t) -> p j t", t=128), op=ALU.mult),
                  reads=['zr', 'eb'], writes=[arkey])
            yield
            P.add('act', lambda e: e.activation(out=ea[:], in_=uB[:], func=AF.Exp, scale=DECAY_C), reads=[kB], writes=['ea'])
            P.add('pool', lambda e: e.tensor_tensor(out=Bt[:], in0=kka[:], in1=ea[:], op=ALU.mult), reads=['kka', 'ea'], writes=[btk])
            P.add('dve', lambda e: e.tensor_tensor(out=Kt[:], in0=kd[:], in1=ea[:], op=ALU.mult), reads=['kd', 'ea'], writes=[ktk])
            P.add('act', lambda e: e.activation(out=eb[:], in_=uH[:], func=AF.Exp, scale=-DECAY_C), reads=[kH], writes=['eb'])
            P.add('pool', lambda e: e.tensor_tensor(out=Bh[:], in0=kka[:], in1=eb[:], op=ALU.mult), reads=['kka', 'eb'], writes=['Bh'])
            P.add('dve', lambda e: e.tensor_tensor(out=Kh[:], in0=kd[:], in1=eb[:], op=ALU.mult), reads=['kd', 'eb'], writes=['Kh'])
            yield
            P.add('pool', lambda e: e.tensor_tensor(out=vb[:], in0=zv[:], in1=vmt[:], op=ALU.mult), reads=['zv', 'vmt'], writes=['vb'])
            if own:
                P.add('dve', lambda e, c=c: e.scalar_tensor_tensor(out=rk[c][:], in0=zr[:], scalar=cv[:, 16 + c:17 + c],
                                                                   in1=kd[:], op0=ALU.mult, op1=ALU.mult),
                      reads=['zr', 'kd', 'cv'], writes=['rk%d' % c])
            for j in range(4):
                js = slice(j * 128, (j + 1) * 128)
                for q, (src, sk) in enumerate(((vb, 'vb'), (Bh, 'Bh'), (Kh, 'Kh'))):
                    P.add('pe', lambda e, q=q, src=src, js=js: e.transpose(pT[:, q * 128:(q + 1) * 128], src[:, js], idb[:]),
                          reads=[sk, 'idb'], writes=['pT'])
                P.add('dve', lambda e, c=c, j=j: e.tensor_copy(out=tok[:, c, j, :], in_=pT[:, 0:384]),
                      writes=['pT', 'tok%d' % c])
                yield
            yield
        def head(c, hh):
            Bt, Kt, btk, ktk = Bt2[c % 2], Kt2[c % 2], 'Bt%d' % (c % 2), 'Kt%d' % (c % 2)
            AR = ARt[c]
            arkey = 'ARt%d' % c
            rows = slice(hh * 64, (hh + 1) * 64)
            u0 = (c * 2 + hh) * 4
            for j in range(4):
                js = slice(j * 128, (j + 1) * 128)
                P.add('pe', lambda e, j=j, js=js: e.matmul(pc[4][:, js], Bt[rows, js], AR[rows, j, 0:128], start=True, stop=True),
                      reads=[btk, arkey], writes=['pc4'])
                P.add('pe', lambda e, j=j, js=js: e.matmul(pc[5][:, js], Bt[rows, js], AR[rows, j, 128:256], start=True, stop=True),
                      reads=[btk, arkey], writes=['pc5'])
                P.add('pe', lambda e, j=j, js=js: e.matmul(pc[6 + j // 2][:, (j % 2) * 256:(j % 2) * 256 + 256], Kt[rows, js],
                                                           AR[rows, j, :], start=True, stop=True),
                      reads=[ktk, arkey], writes=['pc%d' % (6 + j // 2)])
                P.add('pe', lambda e, j=j, js=js: e.matmul(pc2[:, js], AR[rows, j, 0:128], Bt[rows, js], start=True, stop=True),
                      reads=[btk, arkey], writes=['pc2'])
            yield
            Msb = Ms[:].unsqueeze(1).broadcast_to([128, 4, 128])
            Mib = Mi[:].unsqueeze(1).broadcast_to([128, 4, 128])
            Mtb = Mt[:].unsqueeze(1).broadcast_to([128, 4, 128])
            P.add('dve', lambda e: e.tensor_tensor(out=QP[0][:, :, 0:128], in0=pc[4][:].rearrange("p (u t) -> p u t", t=128),
                                                   in1=Msb, op=ALU.mult), reads=['masks'], writes=['pc4', 'QP0_0', 'QP0_1'])
            P.add('dve', lambda e, u0=u0: e.tensor_tensor(out=SC_MP[:, u0:u0 + 4, 0:128],
                                                          in0=pc[5][:].rearrange("p (u t) -> p u t", t=128), in1=Mib, op=ALU.mult),
                  reads=['masks'], writes=['pc5', 'SC_MP'])
            for half in range(2):
                M2s = Ms[:].unsqueeze(1).broadcast_to([128, 2, 128])
                M2i = Mi[:].unsqueeze(1).broadcast_to([128, 2, 128])
                pcb = pc[6 + half]
                P.add('dve', lambda e, u0=u0, half=half, pcb=pcb, M2s=M2s: e.tensor_tensor(
                    out=SC_LM[:, u0 + 2 * half:u0 + 2 * half + 2, 0:128],
                    in0=pcb[:].rearrange("p (u t) -> p u t", t=256)[:, :, 0:128], in1=M2s, op=ALU.mult),
                    reads=['masks'], writes=['pc%d' % (6 + half), 'SC_LM'])
                P.add('dve', lambda e, u0=u0, half=half, pcb=pcb, M2i=M2i: e.tensor_tensor(
                    out=SC_LM[:, u0 + 2 * half:u0 + 2 * half + 2, 128:256],
                    in0=pcb[:].rearrange("p (u t) -> p u t", t=256)[:, :, 128:256], in1=M2i, op=ALU.mult),
                    reads=['masks'], writes=['pc%d' % (6 + half), 'SC_LM'])
            P.add('dve', lambda e: e.tensor_tensor(out=QTt[0][:], in0=pc2[:].rearrange("p (u t) -> p u t", t=128), in1=Mtb,
                                                   op=ALU.mult), reads=['masks'], writes=['pc2', 'QT0_0', 'QT0_1'])
            yield
            P.add('pool', lambda e: e.tensor_tensor(out=QP[1][:, :, 128:256], in0=QP[0][:, :, 0:128],
                                                    in1=idb[:].unsqueeze(1).broadcast_to([128, 4, 128]), op=ALU.add),
                  reads=['idb', 'QP0_0', 'QP0_1'], writes=['QP1_0', 'QP1_1'])
            pB = (pc2, pc[5])
            pBk = ('pc2', 'pc5')

            def level(lvl, cur, hf):
                nxt = 1 - cur
                ck, nk = 'QP%d_%d' % (cur, hf), 'QP%d_%d' % (nxt, hf)
                ctk, ntk = 'QT%d_%d' % (cur, hf), 'QT%d_%d' % (nxt, hf)
                pa, pak = pc[6 + hf], 'pc%d' % (6 + hf)
                us = slice(2 * hf, 2 * hf + 2)
                for uu in range(2):
                    u = 2 * hf + uu
                    if lvl == 0:
                        P.add('pe', lambda e, u=u, uu=uu: e.matmul(pa[:, uu * 256:uu * 256 + 128], QTt[0][:, u, :], QP[0][:, u, 0:128],
                                                                   start=True, stop=True), reads=[ck, ctk], writes=[pak])
                        P.add('pe', lambda e, u=u, uu=uu: e.matmul(pB[hf][:, uu * 128:(uu + 1) * 128], QP[0][:, u, 0:128], QTt[0][:, u, :],
                                                                   start=True, stop=True), reads=[ck, ctk], writes=[pBk[hf]])
                    elif lvl < 6:
                        P.add('pe', lambda e, u=u, uu=uu: e.matmul(pa[:, uu * 256:uu * 256 + 256], QTt[cur][:, u, :], QP[cur][:, u, :],
                                                                   start=True, stop=True), reads=[ck, ctk], writes=[pak])
                        P.add('pe', lambda e, u=u, uu=uu: e.matmul(pB[hf][:, uu * 128:(uu + 1) * 128], QP[cur][:, u, 0:128],
                                                                   QTt[cur][:, u, :], start=True, stop=True),
                              reads=[ck, ctk], writes=[pBk[hf]])
                    else:
                        P.add('pe', lambda e, u=u, uu=uu: e.matmul(pa[:, uu * 256 + 128:uu * 256 + 256], QTt[cur][:, u, :],
                                                                   QP[cur][:, u, 128:256], start=True, stop=True),
                              reads=[ck, ctk], writes=[pak])
                yield
                pv = pa[:].rearrange("p (u t) -> p u t", t=256)
                if lvl == 0:
                    P.add('act', lambda e: e.activation(out=QP[1][:, us, 0:128], in_=pv[:, :, 0:128], func=AF.Identity),
                          writes=[pak, nk])
                    P.add('act', lambda e: e.activation(out=QTt[1][:, us, :], in_=pB[hf][:, 0:256].rearrange("p (u t) -> p u t", t=128),
                                                        func=AF.Identity), writes=[pBk[hf], ntk])
                elif lvl < 6:
                    P.add('act', lambda e: e.activation(out=QP[nxt][:, us, 0:128], in_=pv[:, :, 0:128], func=AF.Identity),
                          writes=[pak, nk])
                    P.add('dve', lambda e: e.tensor_tensor(out=QP[nxt][:, us, 128:256], in0=pv[:, :, 128:256],
                                                           in1=QP[cur][:, us, 128:256], op=ALU.add), reads=[ck], writes=[pak, nk])
                    P.add('act', lambda e: e.activation(out=QTt[nxt][:, us, :], in_=pB[hf][:, 0:256].rearrange("p (u t) -> p u t", t=128),
                                                        func=AF.Identity), writes=[pBk[hf], ntk])
                else:
                    P.add('dve', lambda e: e.tensor_tensor(out=SC_MP[:, u0 + 2 * hf:u0 + 2 * hf + 2, 128:256], in0=pv[:, :, 128:256],
                                                           in1=QP[cur][:, us, 128:256], op=ALU.add), reads=[ck], writes=[pak, 'SC_MP'])
                yield

            def half_chain(hf):
                cur = 0
                for lvl in range(7):
                    yield from level(lvl, cur, hf)
                    cur = 1 - cur
            g0, g1 = half_chain(0), half_chain(1)
            live = [g0, g1]
            while live:
                for g_ in list(live):
                    try:
                        next(g_)
                    except StopIteration:
                        live.remove(g_)
                yield

        def run_interleaved(gens):
            gens = list(gens)
            while gens:
                for g_ in list(gens):
                    try:
                        next(g_)
                    except StopIteration:
                        gens.remove(g_)

        def chain_b(c):
            yield from head(c, 0)
            yield from head(c, 1)
        run_interleaved([pair(0)])
        for c in range(1, 4):
            run_interleaved([pair(c), chain_b(c - 1)])
        run_interleaved([chain_b(3)])

    def state(sc):
        t0 = sc * 512
        own = (own0 <= t0 < own1)
        for j in chunk_order:
            js = slice(j * 128, (j + 1) * 128)
            for c in range(4):
                pcc, pk = pc[4 + c], 'pc%d' % (4 + c)
                P.add('pe', lambda e, c=c, j=j, pcc=pcc: e.matmul(pcc[:, 0:128], ARt[c][:, j, 0:128], Sbf[:, c, :], start=True, stop=False),
                      reads=['ARt%d' % c, 'Sbf%d' % c], writes=[pk])
                for hh in range(2):
                    u = (c * 2 + hh) * 4 + j
                    hs = slice(hh * 64, (hh + 1) * 64)
                    P.add('pe', lambda e, c=c, j=j, u=u, hs=hs, hh=hh, pcc=pcc: e.matmul(pcc[:, hs], SC_LM[:, u, 0:128], tok[:, c, j, hs],
                                                                                 start=False, stop=(hh == 1)),
                          reads=['SC_LM', 'tok%d' % c], writes=[pk])
                P.add('act', lambda e, c=c, pcc=pcc: e.activation(out=RHSb[:, c, :], in_=pcc[:, 0:128], func=AF.Identity),
                      writes=[pk, 'RHSb%d' % c])
                yield
            for c in range(4):
                pcc, pk = pc[4 + c], 'pc%d' % (4 + c)
                for hh in range(2):
                    u = (c * 2 + hh) * 4 + j
                    hs = slice(hh * 64, (hh + 1) * 64)
                    P.add('pe', lambda e, c=c, u=u, hs=hs, hh=hh, pcc=pcc: e.matmul(pcc[:, 128 + hh * 64:192 + hh * 64], SC_MP[:, u, 128:256],
                                                                            RHSb[:, c, hs], start=True, stop=True),
                          reads=['SC_MP', 'RHSb%d' % c], writes=[pk])
                P.add('dve', lambda e, c=c, pcc=pcc: e.tensor_copy(out=Ub[:, c, :], in_=pcc[:, 128:256]), writes=[pk, 'Ub%d' % c])
                yield
            for c in range(4):
                pcc, pk = pc[4 + c], 'pc%d' % (4 + c)
                if own:
                    P.add('pe', lambda e, c=c, j=j, pcc=pcc: e.matmul(pcc[:, 256:384], ARt[c][:, j, 128:256], Sbf[:, c, :], start=True, stop=False),
                          reads=['ARt%d' % c, 'Sbf%d' % c], writes=[pk])
                    for hh in range(2):
                        u = (c * 2 + hh) * 4 + j
                        hs = slice(hh * 64, (hh + 1) * 64)
                        os_ = slice(256 + hh * 64, 320 + hh * 64)
                        P.add('pe', lambda e, c=c, u=u, hs=hs, os_=os_, pcc=pcc: e.matmul(pcc[:, os_], SC_MP[:, u, 0:128], Ub[:, c, hs],
                                                                                  start=False, stop=False),
                              reads=['SC_MP', 'Ub%d' % c], writes=[pk])
                        P.add('pe', lambda e, c=c, j=j, u=u, hs=hs, os_=os_, hh=hh, pcc=pcc: e.matmul(pcc[:, os_], SC_LM[:, u, 128:256], tok[:, c, j, hs],
                                                                                              start=False, stop=(hh == 1)),
                              reads=['SC_LM', 'tok%d' % c], writes=[pk])
                P.add('pe', lambda e, c=c, j=j, pcc=pcc: e.matmul(pcc[:, 384:512], tok[:, c, j, 128:256], Ub[:, c, :], start=True, stop=False),
                      reads=['tok%d' % c, 'Ub%d' % c], writes=[pk])
                P.add('pe', lambda e, c=c, j=j, pcc=pcc: e.matmul(pcc[:, 384:512], tok[:, c, j, 256:384], tok[:, c, j, 0:128], start=False, stop=True),
                      reads=['tok%d' % c], writes=[pk])
                if own:
                    P.add('act', lambda e, c=c, pcc=pcc: e.activation(out=yt[:, c * 128:(c + 1) * 128], in_=pcc[:, 256:384], func=AF.Identity),
                          writes=[pk, 'yt'])
                for hh in range(2):
                    hs = slice(hh * 64, (hh + 1) * 64)
                    P.add('dve', lambda e, c=c, j=j, hs=hs, hh=hh, pcc=pcc: e.scalar_tensor_tensor(
                        out=S32[hs, c, hs], in0=S32[hs, c, hs], scalar=wc[hs, c, j:j + 1], in1=pcc[hs, 384 + hh * 64:448 + hh * 64],
                        op0=ALU.mult, op1=ALU.add), reads=['wc'], writes=[pk, 'S32_%d' % c])
                P.add('act', lambda e, c=c: e.activation(out=Sbf[:, c, :], in_=S32[:, c, :], func=AF.Identity),
                      reads=['S32_%d' % c], writes=['Sbf%d' % c])
                yield
            if own:
                tr = t0 + j * 128 - own0
                for c in range(4):
                    P.add('pe', lambda e, c=c, js=js: e.matmul(pc2[:, 32 + 2 * c:34 + 2 * c], rk[c][:, js], E2[:], start=True, stop=True),
                          reads=['rk%d' % c, 'E2'], writes=['pc2'])
                P.add('dve', lambda e: e.tensor_copy(out=st8[:], in_=pc2[:, 32:40]), writes=['pc2', 'st8'])
                P.add('pool', lambda e, tr=tr: e.dma_start(out=y_out[tr:tr + 128, :], in_=yt[:]), reads=['yt'], dma=True)
                P.add('pool', lambda e, tr=tr: e.dma_start(out=s_out[tr:tr + 128, :], in_=st8[:]), reads=['st8'], dma=True)
                if fwd:
                    P.add('pe', lambda e, js=js: e.matmul(pm[0][:, :], sg[:, js], Gl[:], start=True, stop=True),
                          reads=['sg', 'Gl'], writes=['pm0'])
                    P.add('act', lambda e: e.activation(out=gt[:], in_=pm[0][:], func=AF.Identity), writes=['pm0', 'gt'])
                    P.add('pool', lambda e, tr=tr: e.dma_start(out=g_out[tr:tr + 128, :], in_=gt[:]), reads=['gt'], dma=True)
                    P.add('pool', lambda e, tr=tr, j=j: e.dma_start(
                        out=v_out[tr:tr + 128, :].rearrange("t (c v) -> t c v", c=4), in_=tok[:, :, j, 0:128]),
                        reads=['tok0', 'tok1', 'tok2', 'tok3'], dma=True)
            yield

    def run_il(gens):
        gens = list(gens)
        while gens:
            for g_ in list(gens):
                try:
                    next(g_)
                except StopIteration:
                    gens.remove(g_)
    prev = None
    for sc in order:
        run_il([early(sc)] if prev is None else [state(prev), early(sc)])
        mid(sc)
        prev = sc
    run_il([state(prev)])
    P.close()


def phase_assembly(nc, Town, yf, yb, sf, sbk, gd_, vd, yatt, xown, woutd, lnxw, lnxb, x1out):
    P = Phase(nc, "asm")
    NT = Town // 128
    Wo = P.sb("Wo", [128, 8, D], BF16)
    stage = [P.sb("stage%d" % i, [128, D]) for i in range(2)]
    lw = P.sb("lw", [128, 512]); lb = P.sb("lb", [128, 512])
    idf = P.sb("idf", [128, 128]); idb = P.sb("idb", [128, 128], BF16)
    yft = P.sb("yft", [128, 512]); ybt = P.sb("ybt", [128, 512]); gt = P.sb("gt", [128, 512])
    sq = P.sb("sq", [128, 512]); bon = P.sb("bon", [128, 512])
    vt = P.sb("vt", [128, 512], BF16)
    s8 = P.sb("s8", [128, 48])
    xo = P.sb("xo", [128, D])
    yr = P.sb("yr", [128, 512], BF16)
    ycT = P.sb("ycT", [128, 8, 128], BF16)
    pT = P.ps("pT", [128, 1024], BF16)
    pO = [P.ps("pO%d" % i, [128, 512]) for i in range(2)]
    P.add('sp', lambda e: e.dma_start(out=lw[:], in_=lnxw.partition_broadcast(128)), writes=['lw'], dma=True)
    P.add('sp', lambda e: e.dma_start(out=lb[:], in_=lnxb.partition_broadcast(128)), writes=['lb'], dma=True)
    emit_identity(P, idb, idf)
    load_weight_bf16(P, woutd, Wo, 8, D, stage, 'Wo', scale_tile=None, col_piece=D)

    def tile(n):
        r = slice(n * 128, (n + 1) * 128)
        P.add('sp', lambda e: e.dma_start(out=yft[:], in_=yf[r, :]), writes=['yft'], dma=True)
        P.add('sp', lambda e: e.dma_start(out=ybt[:], in_=yb[r, :]), writes=['ybt'], dma=True)
        P.add('sp', lambda e: e.dma_start(out=gt[:], in_=gd_[r, :]), writes=['gt'], dma=True)
        P.add('sp', lambda e: e.dma_start(out=vt[:], in_=vd[r, :]), writes=['vt'], dma=True)
        P.add('sp', lambda e: e.dma_start(out=s8[:, 0:8], in_=sf[r, :]), writes=['s8a'], dma=True)
        P.add('sp', lambda e: e.dma_start(out=s8[:, 8:16], in_=sbk[r, :]), writes=['s8b'], dma=True)
        P.add('sp', lambda e: e.dma_start(out=xo[:], in_=xown[r, :]), writes=['xo'], dma=True)
        P.add('sp', lambda e: e.dma_start(out=ycT[:, 4:8, :], in_=yatt[:, :, r].rearrange("g p t -> p g t")),
              writes=['ycTa'], dma=True)
        y3 = yft[:].rearrange("p (h c) -> p h c", c=64)

        def b8(col):
            return s8[:, col:col + 8].unsqueeze(2).broadcast_to([128, 8, 64])
        P.add('dve', lambda e: e.tensor_tensor(out=yft[:], in0=yft[:], in1=ybt[:], op=ALU.add), reads=['ybt'], writes=['yft'])
        P.add('dve', lambda e: e.tensor_reduce(out=s8[:, 16:24], in_=y3, axis=AX.X, op=ALU.add), reads=['yft'], writes=['s8c'])
        P.add('dve', lambda e: e.tensor_scalar(out=s8[:, 16:24], in0=s8[:, 16:24], scalar1=1.0 / 64, scalar2=None, op0=ALU.mult),
              writes=['s8c'])
        P.add('dve', lambda e: e.tensor_tensor(out=y3, in0=y3, in1=b8(16), op=ALU.subtract), reads=['s8c'], writes=['yft'])
        P.add('act', lambda e: e.activation(out=sq[:], in_=yft[:], func=AF.Square), reads=['yft'], writes=['sq'])
        P.add('dve', lambda e: e.tensor_reduce(out=s8[:, 24:32], in_=sq[:].rearrange("p (h c) -> p h c", c=64), axis=AX.X,
                                               op=ALU.add), reads=['sq'], writes=['s8d'])
        emit_rstd(P, s8[:, 24:32], s8[:, 32:40], s8[:, 40:48], 1.0 / 64, LNX_EPS, ['s8d'], ['s8e'])
        P.add('dve', lambda e: e.tensor_tensor(out=y3, in0=y3, in1=b8(40), op=ALU.mult), reads=['s8e'], writes=['yft'])
        P.add('dve', lambda e: e.tensor_tensor(out=yft[:], in0=yft[:], in1=lw[:], op=ALU.mult), reads=['lw'], writes=['yft'])
        P.add('dve', lambda e: e.tensor_tensor(out=yft[:], in0=yft[:], in1=lb[:], op=ALU.add), reads=['lb'], writes=['yft'])
        P.add('dve', lambda e: e.tensor_tensor(out=s8[:, 0:8], in0=s8[:, 0:8], in1=s8[:, 8:16], op=ALU.add), reads=['s8b'],
              writes=['s8a'])
        P.add('dve', lambda e: e.scalar_tensor_tensor(out=bon[:].rearrange("p (h c) -> p h c", c=64),
                                                      in0=vt[:].rearrange("p (h c) -> p h c", c=64), scalar=0.5, in1=b8(0),
                                                      op0=ALU.mult, op1=ALU.mult), reads=['vt', 's8a'], writes=['bon'])
        P.add('dve', lambda e: e.tensor_tensor(out=yft[:], in0=yft[:], in1=bon[:], op=ALU.add), reads=['bon'], writes=['yft'])
        P.add('dve', lambda e: e.tensor_tensor(out=yr[:], in0=yft[:], in1=gt[:], op=ALU.mult), reads=['yft', 'gt'], writes=['yr'])
        for c in range(4):
            P.add('pe', lambda e, c=c: e.transpose(pT[:, c * 128:(c + 1) * 128], yr[:, c * 128:(c + 1) * 128], idb[:]),
                  reads=['yr', 'idb'], writes=['pT'])
        P.add('act', lambda e: e.activation(out=ycT[:, 0:4, :], in_=pT[:, 0:512].rearrange("p (c t) -> p c t", c=4),
                                            func=AF.Identity), writes=['pT', 'ycTr'])
        for half in range(2):
            for ch in range(8):
                P.add('pe', lambda e, half=half, ch=ch: e.matmul(pO[half][:, :], ycT[:, ch, :], Wo[:, ch, half * 512:(half + 1) * 512],
                                                                 start=(ch == 0), stop=(ch == 7)),
                      reads=['ycTr', 'ycTa', 'Wo'], writes=['pO%d' % half])
            P.add('dve', lambda e, half=half: e.tensor_tensor(out=xo[:, half * 512:(half + 1) * 512], in0=pO[half][:],
                                                              in1=xo[:, half * 512:(half + 1) * 512], op=ALU.add),
                  writes=['pO%d' % half, 'xo'])
        P.add('pool', lambda e: e.dma_start(out=x1out[r, :], in_=xo[:]), reads=['xo'], dma=True)
    for n in range(NT):
        tile(n)
    P.close()


def _cols(vec, nchunk):
    return np.ascontiguousarray(np.asarray(vec, np.float32).reshape(nchunk, 128).T)


def host_prep(p):
    f = lambda a: np.asarray(a, np.float32)
    w_in = f(p['w_in'])[0]
    mu_p = f(p['mu_prev'])[0]
    mu_n = f(p['mu_next'])[0]
    r_, k_, v_ = slice(0, 512), slice(512, 1024), slice(1024, 1536)
    wdf, wdb, adf, adb, gdc = slice(1536, 1600), slice(1600, 1664), slice(1664, 1728), slice(1728, 1792), slice(1792, 1920)

    def cat(a, sl):
        return np.concatenate([a[..., s] for s in sl], axis=-1)
    slf = [r_, k_, v_, wdf, adf, gdc]
    slb = [r_, k_, v_, wdb, adb]
    qcols = np.concatenate([np.arange(1920 + h * 64, 1920 + (h + 1) * 64) for h in QPERM])
    out = {}
    out['WRf'] = np.ascontiguousarray(cat(w_in, slf)); out['WRb'] = np.ascontiguousarray(cat(w_in, slb))
    out['mupf'] = _cols(cat(mu_p, slf), 14); out['munf'] = _cols(cat(mu_n, slf), 14)
    out['mupb'] = _cols(cat(mu_p, slb), 13); out['munb'] = _cols(cat(mu_n, slb), 13)
    out['winA'] = np.ascontiguousarray(np.concatenate([w_in[:, qcols], w_in[:, 2432:2688]], axis=1))
    out['g1c'] = _cols(f(p['norm1_g'])[0], 8)
    out['Wlf'] = np.ascontiguousarray(np.concatenate([f(p['w_lora_f'])[0], f(p['a_lora_f'])[0]], axis=0))
    out['Wlb'] = np.ascontiguousarray(np.concatenate([f(p['w_lora_b'])[0], f(p['a_lora_b'])[0]], axis=0))
    out['Gl'] = np.ascontiguousarray(f(p['g_lora'])[0])
    for nm in ('w0_f', 'w0_b', 'a0_f', 'a0_b', 'k_k', 'k_a'):
        out[nm] = _cols(f(p[nm])[0], 4)
    out['r_k'] = _cols(f(p['r_k'])[0].reshape(512), 4)
    out['lnxw'] = np.ascontiguousarray(f(p['lnx_w'])[0].reshape(1, 512)); out['lnxb'] = np.ascontiguousarray(f(p['lnx_b'])[0].reshape(1, 512))
    out['qg'] = np.ascontiguousarray(f(p['q_gain'])[0].reshape(1, 64)); out['kg'] = np.ascontiguousarray(f(p['k_gain'])[0].reshape(1, 64))
    w_out = f(p['w_out'])[0]
    arows = np.concatenate([np.arange(512 + h * 64, 512 + (h + 1) * 64) for h in QPERM])
    out['wout'] = np.ascontiguousarray(np.concatenate([w_out[0:512], w_out[arows]], axis=0))
    out['g2c'] = _cols(f(p['norm2_g'])[0], 8)
    out['gf'] = np.ascontiguousarray(f(p['norm_f_g']).reshape(1, D))
    out['wg'] = np.ascontiguousarray(f(p['ffn_gate'])[0]); out['wu'] = np.ascontiguousarray(f(p['ffn_up'])[0])
    out['wd'] = np.ascontiguousarray(f(p['ffn_down'])[0])
    return out


WEIGHT_SPECS = [('WRf', [D, 1792]), ('WRb', [D, 1664]), ('mupf', [128, 14]), ('munf', [128, 14]), ('mupb', [128, 13]),
                ('munb', [128, 13]), ('winA', [D, 768]), ('g1c', [128, 8]), ('Wlf', [128, 512]), ('Wlb', [128, 512]),
                ('Gl', [128, 512]), ('w0_f', [128, 4]), ('w0_b', [128, 4]), ('a0_f', [128, 4]), ('a0_b', [128, 4]),
                ('k_k', [128, 4]), ('k_a', [128, 4]), ('r_k', [128, 4]), ('lnxw', [1, 512]), ('lnxb', [1, 512]),
                ('qg', [1, 64]), ('kg', [1, 64]), ('wout', [D, D]), ('g2c', [128, 8]), ('gf', [1, D]),
                ('wg', [D, DFF]), ('wu', [D, DFF]), ('wd', [DFF, D])]


def declare_weights(nc):
    return {n: nc.dram_tensor(n, s, F32, kind="ExternalInput").ap() for n, s in WEIGHT_SPECS}


def mixer_job(nc, W, tag, xw, vmask, Tw_f, own_f, Tw_b, own_b, xw_f_off, xw_b_off, xk, Tk, xown, Town, qrow0, x1rows, scr):
    phase_attention(nc, xk, xown, W['winA'], W['g1c'], W['qg'], W['kg'], qrow0, scr['yatt'], Tk, Town)
    phase_rwkv(nc, False, xw[xw_b_off:xw_b_off + Tw_b + 256, :], vmask[:, xw_b_off:xw_b_off + Tw_b], Tw_b, own_b[0], own_b[1],
               W['WRb'], 13, W['mupb'], W['munb'], W['g1c'], W['Wlb'], W['w0_b'], W['a0_b'], W['k_k'], W['k_a'], W['r_k'],
               scr['yb'], scr['sb'])
    phase_rwkv(nc, True, xw[xw_f_off:xw_f_off + Tw_f + 256, :], vmask[:, xw_f_off:xw_f_off + Tw_f], Tw_f, own_f[0], own_f[1],
               W['WRf'], 14, W['mupf'], W['munf'], W['g1c'], W['Wlf'], W['w0_f'], W['a0_f'], W['k_k'], W['k_a'], W['r_k'],
               scr['yf'], scr['sf'], Gld=W['Gl'], g_out=scr['g'], v_out=scr['v'])
    phase_assembly(nc, Town, scr['yf'], scr['yb'], scr['sf'], scr['sb'], scr['g'], scr['v'], scr['yatt'], xown, W['wout'],
                   W['lnxw'], W['lnxb'], x1rows)


def phase_ffn(nc, x, y, g2c, gf, wg, wu, wd, ntok):
    Phase._n[0] += 1
    tagp = "ffn%d_" % Phase._n[0]
    ngroups = ntok // GT
    xg = x.rearrange("(n s p) d -> n p s d", s=NSUB, p=128)
    yg = y.rearrange("(n s p) d -> n p s d", s=NSUB, p=128)

    es = contextlib.ExitStack()
    with es:
        def sb(name, shape, dt=F32):
            return es.enter_context(nc.sbuf_tensor(tagp + name, shape, dt))

        def pt(name, shape, dt=F32):
            return es.enter_context(nc.psum_tensor(tagp + name, shape, dt))

        Wg = sb("Wg", [128, NK, DFF], BF16)
        Wu = sb("Wu", [128, NK, DFF], BF16)
        Wd = sb("Wd", [128, NFF, D], BF16)
        stage = [sb("stage%d" % i, [128, 1024]) for i in range(2)]
        g2t = sb("g2t", [128, NK])
        gft = sb("gft", [128, D])
        idf = sb("idf", [128, 128])
        idb = sb("idb", [128, 128], BF16)
        xt = [sb("xt%d" % i, [128, NSUB, D]) for i in range(2)]
        hb = sb("hb", [128, NSUB, D], BF16)
        hT = sb("hT", [128, NK, GT], BF16)
        actT = sb("actT", [128, NFF, GT], BF16)
        tmp = [sb("tmp%d" % i, [128, GT]) for i in range(2)]
        ot = sb("ot", [128, NSUB, D])
        ss = sb("ss", [128, 4 * NSUB])
        pst = [pt("pst%d" % i, [128, 1024], BF16)[:, 0:GT] for i in range(2)]
        psg = [pt("psg%d" % i, [128, 512])[:, 0:GT] for i in range(2)]
        psu = [pt("psu%d" % i, [128, 512])[:, 0:GT] for i in range(2)]
        psd = [pt("psd%d" % i, [128, 512]) for i in range(2)]

        S = Sched(nc)

        S.add('sp', lambda e: e.dma_start(out=g2t[:], in_=g2c[:, :]), writes=['g2t'], dma=True)
        S.add('sp', lambda e: e.dma_start(out=gft[:], in_=gf.partition_broadcast(128)), writes=['gft'], dma=True)
        S.add('pool', lambda e: e.memset(idf[:], 1.0), writes=['idf'])
        S.add('pool', lambda e: e.affine_select(out=idf[:], in_=idf[:], pattern=[[-1, 128]], compare_op=ALU.is_equal,
                                                 fill=0.0, base=0, channel_multiplier=1), writes=['idf'])
        S.add('dve', lambda e: e.tensor_copy(out=idb[:], in_=idf[:]), reads=['idf'], writes=['idb'])

        nst = [0]

        def load_cast(src_ap, dst_ap, ncols, dkey, scale_ap=None):
            i = nst[0] % 2
            nst[0] += 1
            skey = 'stage%d' % i
            S.add('sp', lambda e: e.dma_start(out=stage[i][:, 0:ncols], in_=src_ap), writes=[skey], dma=True)
            if scale_ap is not None:
                S.add('dve', lambda e: e.tensor_scalar(out=dst_ap, in0=stage[i][:, 0:ncols], scalar1=scale_ap, scalar2=None,
                                                        op0=ALU.mult), reads=[skey, 'g2t'], writes=[dkey])
            else:
                S.add('act', lambda e: e.activation(out=dst_ap, in_=stage[i][:, 0:ncols], func=AF.Identity),
                      reads=[skey], writes=[dkey])

        pieces = [(0, 1024), (1024, 1024), (2048, DFF - 2048)]
        for k in range(NK):
            for (c0, cn) in pieces:
                load_cast(wg[k * 128:(k + 1) * 128, c0:c0 + cn], Wg[:, k, c0:c0 + cn], cn, 'Wg', g2t[:, k:k + 1])
                load_cast(wu[k * 128:(k + 1) * 128, c0:c0 + cn], Wu[:, k, c0:c0 + cn], cn, 'Wu', g2t[:, k:k + 1])
        for f in range(NFF):
            load_cast(wd[f * 128:(f + 1) * 128, :], Wd[:, f, :], D, 'Wd')

        nps = [0, 0, 0]
        for g in range(ngroups):
            X = xt[g % 2]
            xk = 'xt%d' % (g % 2)
            S.add('sp', lambda e, X=X, g=g: e.dma_start(out=X[:], in_=xg[g]), writes=[xk], dma=True)
            for s in range(NSUB):
                S.add('act', lambda e, X=X, s=s: e.activation(out=hb[:, s, :], in_=X[:, s, :], func=AF.Square,
                                                              accum_out=ss[:, s:s + 1]),
                      reads=[xk], writes=['hb', 'ss'])
            S.add('act', lambda e: e.activation(out=ss[:, NSUB:2 * NSUB], in_=ss[:, 0:NSUB], func=AF.Ln,
                                                scale=1.0 / D, bias=NORM_EPS), reads=[], writes=['ss'])
            S.add('act', lambda e: e.activation(out=ss[:, 0:NSUB], in_=ss[:, NSUB:2 * NSUB], func=AF.Exp, scale=-0.5),
                  reads=[], writes=['ss'])
            for s in range(NSUB):
                S.add('dve', lambda e, X=X, s=s: e.tensor_scalar(out=hb[:, s, :], in0=X[:, s, :], scalar1=ss[:, s:s + 1],
                                                                 scalar2=None, op0=ALU.mult),
                      reads=[xk, 'ss'], writes=['hb'])
            for k in range(NK):
                P = pst[nps[0] % 2]
                pk = 'pst%d' % (nps[0] % 2)
                nps[0] += 1
                for s in range(NSUB):
                    S.add('pe', lambda e, P=P, s=s, k=k: e.transpose(P[:, s * 128:(s + 1) * 128],
                                                                      hb[:, s, k * 128:(k + 1) * 128], idb[:]),
                          reads=['hb', 'idb'], writes=[pk])
                if k % 2 == 0:
                    S.add('dve', lambda e, P=P, k=k: e.tensor_copy(out=hT[:, k, :], in_=P[:]), writes=[pk, 'hT'])
                else:
                    S.add('act', lambda e, P=P, k=k: e.activation(out=hT[:, k, :], in_=P[:], func=AF.Identity),
                          writes=[pk, 'hT'])
            for f in range(NFF):
                i = nps[1] % 2
                nps[1] += 1
                G, U, T = psg[i], psu[i], tmp[i]
                for k in range(NK):
                    S.add('pe', lambda e, G=G, k=k, f=f: e.matmul(G[:, :], Wg[:, k, f * 128:(f + 1) * 128], hT[:, k, :],
                                                                  start=(k == 0), stop=(k == NK - 1)),
                          reads=['Wg', 'hT'], writes=['psg%d' % i])
                for k in range(NK):
                    S.add('pe', lambda e, U=U, k=k, f=f: e.matmul(U[:, :], Wu[:, k, f * 128:(f + 1) * 128], hT[:, k, :],
                                                                  start=(k == 0), stop=(k == NK - 1)),
                          reads=['Wu', 'hT'], writes=['psu%d' % i])
                S.add('act', lambda e, G=G, T=T: e.activation(out=T[:], in_=G[:], func=AF.Silu),
                      writes=['psg%d' % i, 'tmp%d' % i])
                S.add('dve', lambda e, U=U, T=T, f=f: e.tensor_tensor(out=actT[:, f, :], in0=U[:], in1=T[:], op=ALU.mult),
                      reads=['tmp%d' % i], writes=['psu%d' % i, 'actT'])
            for s in range(NSUB):
                for c in range(2):
                    i = nps[2] % 2
                    nps[2] += 1
                    Pd = psd[i]
                    for f in range(NFF):
                        S.add('pe', lambda e, Pd=Pd, f=f, s=s, c=c: e.matmul(Pd[:, :], actT[:, f, s * 128:(s + 1) * 128],
                                                                            Wd[:, f, c * 512:(c + 1) * 512],
                                                                            start=(f == 0), stop=(f == NFF - 1)),
                              reads=['actT', 'Wd'], writes=['psd%d' % i])
                    S.add('dve', lambda e, Pd=Pd, X=X, s=s, c=c: e.tensor_tensor(out=X[:, s, c * 512:(c + 1) * 512],
                                                                               in0=Pd[:], in1=X[:, s, c * 512:(c + 1) * 512],
                                                                               op=ALU.add),
                          writes=['psd%d' % i, xk])
            for s in range(NSUB):
                S.add('act', lambda e, X=X, s=s: e.activation(out=hb[:, s, :], in_=X[:, s, :], func=AF.Square,
                                                              accum_out=ss[:, 2 * NSUB + s:2 * NSUB + s + 1]),
                      reads=[xk], writes=['hb', 'ss'])
            S.add('act', lambda e: e.activation(out=ss[:, 3 * NSUB:4 * NSUB], in_=ss[:, 2 * NSUB:3 * NSUB], func=AF.Ln,
                                                scale=1.0 / D, bias=NORM_EPS), writes=['ss'])
            S.add('act', lambda e: e.activation(out=ss[:, 2 * NSUB:3 * NSUB], in_=ss[:, 3 * NSUB:4 * NSUB], func=AF.Exp,
                                                scale=-0.5), writes=['ss'])
            for s in range(NSUB):
                S.add('dve', lambda e, X=X, s=s: e.scalar_tensor_tensor(out=ot[:, s, :], in0=X[:, s, :],
                                                                        scalar=ss[:, 2 * NSUB + s:2 * NSUB + s + 1],
                                                                        in1=gft[:], op0=ALU.mult, op1=ALU.mult),
                      reads=[xk, 'ss', 'gft'], writes=['ot'])
            S.add('pool', lambda e, g=g: e.dma_start(out=yg[g], in_=ot[:]), reads=['ot'], dma=True)
        S.emit()


def build_program(T, NPQ, ST):
    Q = ST // 4
    ntok = NPQ * T + Q
    nc = bass.Bass("TRN2", target_bir_lowering=False)
    W = declare_weights(nc)
    xp = nc.dram_tensor("xp", [NPQ, T + 256, D], F32, kind="ExternalInput").ap()
    vones = nc.dram_tensor("vones", [1, T], F32, kind="ExternalInput").ap()
    xsw = nc.dram_tensor("xsw", [7 * Q + 256, D], F32, kind="ExternalInput").ap()
    xsk = nc.dram_tensor("xsk", [ST, D], F32, kind="ExternalInput").ap()
    vms = nc.dram_tensor("vms", [1, 7 * Q], F32, kind="ExternalInput").ap()
    qr0s = nc.dram_tensor("qr0s", [1, 1], F32, kind="ExternalInput").ap()
    qr0p = nc.dram_tensor("qr0p", [1, 1], F32, kind="ExternalInput").ap()
    y = nc.dram_tensor("y", [ntok, D], F32, kind="ExternalOutput").ap()
    x1 = nc.dram_tensor("x1", [ntok, D], F32, kind="Internal").ap()

    def scratch(tag, n):
        return dict(yatt=nc.dram_tensor(tag + "yatt", [4, 128, n], BF16, kind="Internal").ap(),
                    yf=nc.dram_tensor(tag + "yf", [n, 512], F32, kind="Internal").ap(),
                    yb=nc.dram_tensor(tag + "yb", [n, 512], F32, kind="Internal").ap(),
                    sf=nc.dram_tensor(tag + "sf", [n, 8], F32, kind="Internal").ap(),
                    sb=nc.dram_tensor(tag + "sb", [n, 8], F32, kind="Internal").ap(),
                    g=nc.dram_tensor(tag + "g", [n, 512], F32, kind="Internal").ap(),
                    v=nc.dram_tensor(tag + "v", [n, 512], BF16, kind="Internal").ap())
    _skip = ''
    _es = contextlib.ExitStack()
    SemPool.current = SemPool(nc, _es)
    scr = scratch("s_", Q)
    if 's' not in _skip:
      mixer_job(nc, W, "s", xsw, vms, 4 * Q, (3 * Q, 4 * Q), 4 * Q, (0, Q), 0, 3 * Q, xsk, ST,
                xsw[128 + 3 * Q:128 + 4 * Q, :], Q, qr0s, x1[NPQ * T:NPQ * T + Q, :], scr)
    for i in range(0 if 'p' not in _skip else NPQ, NPQ):
        scr = scratch("p%d_" % i, T)
        xown = xp[i, 128:128 + T, :]
        mixer_job(nc, W, "p%d" % i, xp[i], vones, T, (0, T), T, (0, T), 0, 0, xown, T, xown, T, qr0p,
                  x1[i * T:(i + 1) * T, :], scr)
    if 'f' not in _skip:
      phase_ffn(nc, x1, y, W['g2c'], W['gf'], W['wg'], W['wu'], W['wd'], ntok)
    SemPool.current = None
    _es.close()
    return nc


_NC_CACHE = {}


def kernel(x_prompt, x_sample, norm1_g, w_in, mu_prev, mu_next, k_k, k_a, r_k, w0_f, w_lora_f, w0_b, w_lora_b,
           a0_f, a_lora_f, a0_b, a_lora_b, g_lora, lnx_w, lnx_b, q_gain, k_gain, w_out, norm2_g, ffn_gate,
           ffn_up, ffn_down, norm_f_g):
    params = dict(norm1_g=norm1_g, w_in=w_in, mu_prev=mu_prev, mu_next=mu_next, k_k=k_k, k_a=k_a, r_k=r_k, w0_f=w0_f,
                  w_lora_f=w_lora_f, w0_b=w0_b, w_lora_b=w_lora_b, a0_f=a0_f, a_lora_f=a_lora_f, a0_b=a0_b,
                  a_lora_b=a_lora_b, g_lora=g_lora, lnx_w=lnx_w, lnx_b=lnx_b, q_gain=q_gain, k_gain=k_gain, w_out=w_out,
                  norm2_g=norm2_g, ffn_gate=ffn_gate, ffn_up=ffn_up, ffn_down=ffn_down, norm_f_g=norm_f_g)
    x_prompt = np.asarray(x_prompt, np.float32)
    x_sample = np.asarray(x_sample, np.float32)
    B, T, _ = x_prompt.shape
    SB, ST, _ = x_sample.shape
    NPQ = B // NCORES
    Q = ST // 4
    key = (T, NPQ, ST)
    if key not in _NC_CACHE:
        _NC_CACHE[key] = build_program(T, NPQ, ST)
    nc = _NC_CACHE[key]
    Wh = host_prep(params)
    in_maps = []
    for c in range(NCORES):
        s, j = c // 4, c % 4
        m = {n: Wh[n] for n, _ in WEIGHT_SPECS}
        xp = np.zeros((NPQ, T + 256, D), np.float32)
        xp[:, 128:128 + T] = x_prompt[c * NPQ:(c + 1) * NPQ]
        m["xp"] = xp
        m["vones"] = np.ones((1, T), np.float32)
        xsw = np.zeros((7 * Q + 256, D), np.float32)
        tlo = j * Q - 3 * Q - 128
        lo, hi = max(0, tlo), min(ST, tlo + 7 * Q + 256)
        xsw[lo - tlo:hi - tlo] = x_sample[s, lo:hi]
        m["xsw"] = xsw
        vm = np.zeros((1, 7 * Q), np.float32)
        t_first = j * Q - 3 * Q
        lo, hi = max(0, t_first), min(ST, t_first + 7 * Q)
        vm[0, lo - t_first:hi - t_first] = 1.0
        m["vms"] = vm
        m["xsk"] = np.ascontiguousarray(x_sample[s])
        m["qr0s"] = np.full((1, 1), float(j * Q // 64), np.float32)
        m["qr0p"] = np.zeros((1, 1), np.float32)
        in_maps.append(m)
    res = run_bass_kernel_spmd(nc, in_maps, core_ids=list(range(NCORES)))
    y_prompt = np.empty((B, T, D), np.float32)
    y_sample = np.empty((SB, ST, D), np.float32)
    for c in range(NCORES):
        yc = np.asarray(res.results[c]["y"])
        y_prompt[c * NPQ:(c + 1) * NPQ] = yc[:NPQ * T].reshape(NPQ, T, D)
        y_sample[c // 4, (c % 4) * Q:(c % 4 + 1) * Q] = yc[NPQ * T:]
    return (y_prompt, y_sample)
```
